# Optimizing a Trainium2 kernel written in Bass

```python
import jax, jax.numpy as jnp
from jax import lax
import numpy as np

D_MODEL = 1024
BATCH = 16
SEQ = 2048
DEPTH = 2

MEM_LEN = 256
CHUNK = 128
SG_GROUPS = 8
SG_WIDTH = D_MODEL
SG_GDIM = SG_WIDTH // SG_GROUPS
SSM_INNER = 2 * D_MODEL
SSM_HEADDIM = 64
SSM_HEADS = SSM_INNER // SSM_HEADDIM
SSM_STATE = 128
SSM_GROUPS = 4
SSM_RPG = SSM_HEADS // SSM_GROUPS
SSM_CONV = 4
SSM_CONV_DIM = SSM_INNER + 2 * SSM_GROUPS * SSM_STATE
X_HEADS = 4
X_HEADDIM = D_MODEL // X_HEADS
FFN_HIDDEN = -(-(8 * D_MODEL) // (3 * 256)) * 256
ALPHA = float((2 * DEPTH) ** 0.25)
BETA = float((8 * DEPTH) ** -0.25)
LN_EPS = 1e-5
RMS_EPS = 1e-5

U_END = SG_WIDTH
V_END = U_END + SG_WIDTH
Z_END = V_END + SSM_INNER
XBC_END = Z_END + SSM_CONV_DIM
DT_END = XBC_END + SSM_HEADS
GA_END = DT_END + D_MODEL
IN_COLS = GA_END + D_MODEL
IN_SPLITS = (U_END, V_END, Z_END, XBC_END, DT_END, GA_END)

kernel_name = "hybrid_gmlp_ssd_gated_deepnorm"


def layer_norm(x, g, b):
    xf = x.astype(jnp.float32)
    mu = jnp.mean(xf, axis=-1, keepdims=True)
    var = jnp.mean(jnp.square(xf - mu), axis=-1, keepdims=True)
    return ((xf - mu) * lax.rsqrt(var + LN_EPS) * g.astype(jnp.float32) + b.astype(jnp.float32)).astype(x.dtype)


def spatial_gating(u, v, ln_g, ln_b, w_s, b_s):
    bn, s, _ = v.shape
    u = jax.nn.gelu(u, approximate=False)
    v = layer_norm(jax.nn.gelu(v, approximate=False), ln_g, ln_b)
    vc = v.reshape(bn, s // CHUNK, CHUNK, SG_GROUPS, SG_GDIM)
    causal = jnp.tril(jnp.ones((CHUNK, CHUNK), dtype=bool))
    w = jnp.where(causal[None], w_s, jnp.zeros((), w_s.dtype))
    mixed = jnp.einsum('gts,bcsgd->bctgd', w, vc) + jnp.transpose(b_s)[None, None, :, :, None]
    return u * mixed.reshape(bn, s, SG_WIDTH)


def ssd_branch(z, xbc, dt, conv_w, conv_b, dt_bias, a_log, d_skip, norm_g):
    bn, s, _ = xbc.shape
    nc = s // CHUNK
    xbc = lax.conv_general_dilated(xbc, conv_w[:, None, :], window_strides=(1,),
                                   padding=[(SSM_CONV - 1, 0)],
                                   dimension_numbers=('NWC', 'WIO', 'NWC'),
                                   feature_group_count=SSM_CONV_DIM) + conv_b
    xbc = jax.nn.silu(xbc).astype(jnp.float32)
    xs = xbc[..., :SSM_INNER]
    bm = xbc[..., SSM_INNER:SSM_INNER + SSM_GROUPS * SSM_STATE]
    cm = xbc[..., SSM_INNER + SSM_GROUPS * SSM_STATE:]
    dt = jax.nn.softplus(dt.astype(jnp.float32) + dt_bias.astype(jnp.float32))
    a = -jnp.exp(a_log.astype(jnp.float32)).reshape(SSM_GROUPS, SSM_RPG)

    x = xs.reshape(bn, nc, CHUNK, SSM_GROUPS, SSM_RPG, SSM_HEADDIM)
    dtc = dt.reshape(bn, nc, CHUNK, SSM_GROUPS, SSM_RPG)
    bm = bm.reshape(bn, nc, CHUNK, SSM_GROUPS, SSM_STATE)
    cm = cm.reshape(bn, nc, CHUNK, SSM_GROUPS, SSM_STATE)
    xdt = x * dtc[..., None]
    da = jnp.moveaxis(dtc * a, 2, -1)
    da_cs = jnp.cumsum(da, axis=-1)

    causal = jnp.tril(jnp.ones((CHUNK, CHUNK), dtype=bool))
    seg = da_cs[..., :, None] - da_cs[..., None, :]
    decay = jnp.exp(jnp.where(causal, seg, -jnp.inf))
    cb = jnp.einsum('bclgn,bcsgn->bcgls', cm, bm)
    y_diag = jnp.einsum('bcgls,bcgrls,bcsgrp->bclgrp', cb, decay, xdt)

    decay_states = jnp.exp(da_cs[..., -1:] - da_cs)
    states = jnp.einsum('bclgn,bcgrl,bclgrp->bcgrpn', bm, decay_states, xdt)
    chunk_decay = jnp.exp(da_cs[..., -1])

    def step(carry, inp):
        st, dec = inp
        return carry * dec[..., None, None] + st, carry

    init = jnp.zeros((bn, SSM_GROUPS, SSM_RPG, SSM_HEADDIM, SSM_STATE), jnp.float32)
    _, prev = lax.scan(step, init, (jnp.moveaxis(states, 1, 0), jnp.moveaxis(chunk_decay, 1, 0)))
    prev = jnp.moveaxis(prev, 0, 1)

    y_off = jnp.einsum('bclgn,bcgrpn,bcgrl->bclgrp', cm, prev, jnp.exp(da_cs))
    y = y_diag + y_off + x * d_skip.astype(jnp.float32).reshape(SSM_GROUPS, SSM_RPG)[..., None]
    y = y.reshape(bn, s, SSM_INNER) * jax.nn.silu(z.astype(jnp.float32))
    yg = y.reshape(bn, s, SSM_GROUPS, SSM_INNER // SSM_GROUPS)
    yg = yg * lax.rsqrt(jnp.mean(jnp.square(yg), axis=-1, keepdims=True) + RMS_EPS)
    y = yg.reshape(bn, s, SSM_INNER) * norm_g.astype(jnp.float32)
    return y.astype(z.dtype)


def mixer(h, w_in, sg_ln_g, sg_ln_b, sg_w, sg_b, conv_w, conv_b, dt_bias, a_log,
          d_skip, ssm_norm_g, p_a, p_b, w_mix_o):
    proj = h @ w_in
    u, v, z, xbc, dt, g_a, g_b = jnp.split(proj, IN_SPLITS, axis=-1)
    br_a = spatial_gating(u, v, sg_ln_g, sg_ln_b, sg_w, sg_b) @ p_a
    br_b = ssd_branch(z, xbc, dt, conv_w, conv_b, dt_bias, a_log, d_skip, ssm_norm_g) @ p_b
    merged = jax.nn.sigmoid(g_a) * br_a + jax.nn.sigmoid(g_b) * br_b
    return merged @ w_mix_o


def cross_attention(h, mem_n, w_xq, w_xkv, w_xo):
    bn, s, _ = h.shape
    q = (h @ w_xq).reshape(bn, s, X_HEADS, X_HEADDIM)
    k, v = jnp.split(mem_n @ w_xkv, 2, axis=-1)
    k = k.reshape(bn, -1, X_HEADS, X_HEADDIM)
    v = v.reshape(bn, -1, X_HEADS, X_HEADDIM)
    scores = jnp.einsum('bshd,bmhd->bhsm', q, k).astype(jnp.float32) * (X_HEADDIM ** -0.5)
    p = jax.nn.softmax(scores, axis=-1).astype(v.dtype)
    o = jnp.einsum('bhsm,bmhd->bshd', p, v).reshape(bn, s, D_MODEL)
    return o @ w_xo


def swiglu(h, w_ffn_in, w_ffn_out):
    g, u = jnp.split(h @ w_ffn_in, 2, axis=-1)
    return (jax.nn.silu(g) * u) @ w_ffn_out


def setup_inputs(seed: int = 0) -> dict:
    key = jax.random.key(seed)
    ks = iter(jax.random.split(key, 40))
    f32 = jnp.float32

    def nrm(shape, scale):
        return jax.random.normal(next(ks), shape, f32) * scale

    x = jax.random.normal(next(ks), (BATCH, SEQ, D_MODEL), f32)
    mem = jax.random.normal(next(ks), (BATCH, MEM_LEN, D_MODEL), f32)
    mem_ln_g = 1.0 + nrm((D_MODEL,), 0.02)
    mem_ln_b = nrm((D_MODEL,), 0.02)
    w_in = nrm((DEPTH, D_MODEL, IN_COLS), D_MODEL ** -0.5)
    sg_ln_g = 1.0 + nrm((DEPTH, SG_WIDTH), 0.02)
    sg_ln_b = nrm((DEPTH, SG_WIDTH), 0.02)
    sg_w = nrm((DEPTH, SG_GROUPS, CHUNK, CHUNK), CHUNK ** -0.5)
    sg_b = 1.0 + nrm((DEPTH, SG_GROUPS, CHUNK), 0.01)
    conv_w = nrm((DEPTH, SSM_CONV, SSM_CONV_DIM), SSM_CONV ** -0.5)
    conv_b = nrm((DEPTH, SSM_CONV_DIM), 0.02)
    dt0 = jnp.exp(jax.random.uniform(next(ks), (DEPTH, SSM_HEADS), f32,
                                     float(np.log(1e-3)), float(np.log(1e-1))))
    dt_bias = dt0 + jnp.log(-jnp.expm1(-dt0))
    a_log = jnp.log(jax.random.uniform(next(ks), (DEPTH, SSM_HEADS), f32, 1.0, 16.0))
    d_skip = 1.0 + nrm((DEPTH, SSM_HEADS), 0.02)
    ssm_norm_g = 1.0 + nrm((DEPTH, SSM_INNER), 0.02)
    p_a = nrm((DEPTH, SG_WIDTH, D_MODEL), BETA * SG_WIDTH ** -0.5)
    p_b = nrm((DEPTH, SSM_INNER, D_MODEL), BETA * SSM_INNER ** -0.5)
    w_mix_o = nrm((DEPTH, D_MODEL, D_MODEL), BETA * D_MODEL ** -0.5)
    w_xq = nrm((DEPTH, D_MODEL, D_MODEL), D_MODEL ** -0.5)
    w_xkv = jnp.concatenate([nrm((DEPTH, D_MODEL, D_MODEL), D_MODEL ** -0.5),
                             nrm((DEPTH, D_MODEL, D_MODEL), BETA * D_MODEL ** -0.5)], axis=-1)
    w_xo = nrm((DEPTH, D_MODEL, D_MODEL), BETA * D_MODEL ** -0.5)
    w_ffn_in = nrm((DEPTH, D_MODEL, 2 * FFN_HIDDEN), BETA * D_MODEL ** -0.5)
    w_ffn_out = nrm((DEPTH, FFN_HIDDEN, D_MODEL), BETA * FFN_HIDDEN ** -0.5)
    ln_g = 1.0 + nrm((DEPTH, 3, D_MODEL), 0.02)
    ln_b = nrm((DEPTH, 3, D_MODEL), 0.02)
    return {"x": x, "mem": mem, "mem_ln_g": mem_ln_g, "mem_ln_b": mem_ln_b,
            "w_in": w_in, "sg_ln_g": sg_ln_g, "sg_ln_b": sg_ln_b, "sg_w": sg_w, "sg_b": sg_b,
            "conv_w": conv_w, "conv_b": conv_b, "dt_bias": dt_bias, "a_log": a_log,
            "d_skip": d_skip, "ssm_norm_g": ssm_norm_g, "p_a": p_a, "p_b": p_b,
            "w_mix_o": w_mix_o, "w_xq": w_xq, "w_xkv": w_xkv, "w_xo": w_xo,
            "w_ffn_in": w_ffn_in, "w_ffn_out": w_ffn_out, "ln_g": ln_g, "ln_b": ln_b}


def reference(x, mem, mem_ln_g, mem_ln_b, w_in, sg_ln_g, sg_ln_b, sg_w, sg_b, conv_w, conv_b,
              dt_bias, a_log, d_skip, ssm_norm_g, p_a, p_b, w_mix_o, w_xq, w_xkv, w_xo,
              w_ffn_in, w_ffn_out, ln_g, ln_b):
    mem_n = layer_norm(mem, mem_ln_g, mem_ln_b)
    for i in range(DEPTH):
        y = mixer(x, w_in[i], sg_ln_g[i], sg_ln_b[i], sg_w[i], sg_b[i], conv_w[i], conv_b[i],
                  dt_bias[i], a_log[i], d_skip[i], ssm_norm_g[i], p_a[i], p_b[i], w_mix_o[i])
        x = layer_norm(ALPHA * x + y, ln_g[i, 0], ln_b[i, 0])
        y = cross_attention(x, mem_n, w_xq[i], w_xkv[i], w_xo[i])
        x = layer_norm(ALPHA * x + y, ln_g[i, 1], ln_b[i, 1])
        y = swiglu(x, w_ffn_in[i], w_ffn_out[i])
        x = layer_norm(ALPHA * x + y, ln_g[i, 2], ln_b[i, 2])
    return x
```

```python
import contextlib
import numpy as np
import concourse.bass as bass
import concourse.mybir as mybir
from concourse.bass_utils import run_bass_kernel_spmd

F32 = mybir.dt.float32
BF16 = mybir.dt.bfloat16
AF = mybir.ActivationFunctionType
ALU = mybir.AluOpType

D = 1024
SEQ = 2048
BATCH = 16
DEPTH = 2
MEM = 256
NCORES = 8
ALPHA = float((2 * DEPTH) ** 0.25)
LN_EPS = 1e-5
RMS_EPS = 1e-5
FFH = 2816
NKF = FFH // 128
GSZ = 4096

GROUPS = (["kv%d" % i for i in range(4)] + ["v0", "v1", "u0", "u1", "ga0", "pa0", "ga1", "pa1"]
          + ["xbc%d" % i for i in range(6)] + ["z%d" % i for i in range(4)]
          + ["gb0", "pb0", "pb1", "gb1", "pb2", "pb3", "mo0", "mo1", "xq0", "xq1", "xo0", "xo1"]
          + ["fi%d" % i for i in range(11)] + ["fo%d" % i for i in range(8)])
GIDX = {n: i for i, n in enumerate(GROUPS)}
NG = len(GROUPS)

ENGS = ("pe", "act", "dve", "pool", "sp")


class Op:
    __slots__ = ("eng", "fn", "deps", "marked", "count", "is_dma", "chan", "chan_val", "cost", "idx", "fin", "ndep", "succ", "rdy")

    def __init__(self, eng, fn, is_dma):
        self.cost = 300.0
        self.idx = 0
        self.eng = eng
        self.fn = fn
        self.deps = {}
        self.marked = False
        self.count = 0
        self.is_dma = is_dma
        self.chan = None
        self.chan_val = 0


class Prog:
    def __init__(self, nc):
        self.nc = nc
        self.ops = {e: [] for e in ENGS}
        self.last_writer = {}
        self.readers = {}
        self.chan_count = {}
        self.chan_last = {}
        self.bulk = set()
        self.nops = 0

    def add(self, eng, fn, reads=(), writes=(), dma=None, cost=300.0):
        op = Op(eng, fn, dma is not None)
        op.cost = cost
        op.idx = self.nops
        self.nops += 1
        for k in reads:
            w = self.last_writer.get(k)
            if w is not None:
                op.deps[w] = True
        for k in writes:
            w = self.last_writer.get(k)
            if w is not None and w not in op.deps:
                op.deps[w] = False
            for r in self.readers.get(k, ()):
                if r is not op and r not in op.deps:
                    op.deps[r] = False
        for k in reads:
            self.readers.setdefault(k, []).append(op)
        for k in writes:
            self.last_writer[k] = op
            self.readers[k] = []
        if dma is not None:
            op.chan = dma
            self.chan_count[dma] = self.chan_count.get(dma, 0) + 16
            op.chan_val = self.chan_count[dma]
            if dma not in self.bulk:
                pl = self.chan_last.get(dma)
                if pl is not None:
                    op.deps[pl] = True
                self.chan_last[dma] = op
        for p, raw in op.deps.items():
            if (not p.is_dma) and self._needs_wait(op, p, raw):
                p.marked = True
        self.ops[eng].append(op)
        return op

    @staticmethod
    def _needs_wait(c, p, raw):
        if p.is_dma:
            return True
        if p.eng != c.eng:
            return True
        if c.is_dma:
            return True
        if c.eng == "pe":
            return False
        return True

    def emit(self, e, eng, sems, chan_sems):
        c = 0
        for op in self.ops[e]:
            if op.marked and not op.is_dma:
                c += 1
            op.count = c
        waited = {}
        for op in self.ops[e]:
            need = {}
            for p, raw in op.deps.items():
                if not self._needs_wait(op, p, raw):
                    continue
                if p.is_dma:
                    s, v = ("c", p.chan), (self.chan_count[p.chan] if p.chan in self.bulk else p.chan_val)
                else:
                    s, v = ("e", p.eng), p.count
                if v > need.get(s, 0):
                    need[s] = v
            for s, v in need.items():
                if waited.get(s, 0) >= v:
                    continue
                waited[s] = v
                sem = chan_sems[s[1]] if s[0] == "c" else sems[s[1]]
                eng.wait_ge(sem, v)
            ins = op.fn(eng)
            if op.is_dma:
                ins.then_inc(chan_sems[op.chan], 16)
            elif op.marked:
                ins.then_inc(sems[e], 1)

    def schedule(self, window=1500):
        import heapq
        allops = [op for e in ENGS for op in self.ops[e]]
        for op in allops:
            op.succ = []
            op.rdy = 0.0
        for op in allops:
            op.ndep = len(op.deps)
            for p in op.deps:
                p.succ.append(op)
        tfree = {e: 0.0 for e in ENGS}
        h_rdy = {e: [] for e in ENGS}
        h_idx = {e: [] for e in ENGS}
        order = {e: [] for e in ENGS}
        for op in allops:
            if op.ndep == 0:
                heapq.heappush(h_rdy[op.eng], (0.0, op.idx, op))
        done = 0
        nall = len(allops)
        base = 0
        sched = bytearray(nall)
        while done < nall:
            best = None
            for e in ENGS:
                hr, hi = h_rdy[e], h_idx[e]
                while hr and hr[0][0] <= tfree[e]:
                    _, i_, o_ = heapq.heappop(hr)
                    heapq.heappush(hi, (i_, o_))
                if hi:
                    cand = (tfree[e], 0, e)
                elif hr:
                    cand = (hr[0][0], 1, e)
                else:
                    continue
                if best is None or cand < best:
                    best = cand
            t0, kind, e = best
            if kind == 0:
                _, op = heapq.heappop(h_idx[e])
            else:
                _, _, op = heapq.heappop(h_rdy[e])
            start = max(tfree[e], op.rdy)
            if op.is_dma:
                tfree[e] = start + 60.0
                op.fin = start + op.cost
            elif e == "pe":
                tfree[e] = start + op.cost
                op.fin = start + op.cost + 150.0
            else:
                tfree[e] = start + op.cost
                op.fin = start + op.cost + 60.0
            order[e].append(op)
            done += 1
            for sx in op.succ:
                lat = 0.0 if (sx.eng == e and e == "pe") else 80.0
                if op.fin + lat > sx.rdy:
                    sx.rdy = op.fin + lat
                sx.ndep -= 1
                if sx.ndep == 0:
                    heapq.heappush(h_rdy[sx.eng], (sx.rdy, sx.idx, sx))
        self.ops = order
        self.est_ns = max(tfree.values())

    def finalize(self):
        for e in ENGS:
            c = 0
            for op in self.ops[e]:
                if op.marked and not op.is_dma:
                    c += 1
                op.count = c


def bc(ap, shape):
    return ap.to_broadcast(shape) if hasattr(ap, "to_broadcast") else ap.broadcast_to(shape)


class Cfg:
    def __init__(self, nseq=2, nt=8, nch=2, layers=(0, 1), dbg=None, stages=("mixer", "xattn", "ffn"), sched=True):
        self.nseq, self.nt, self.nch, self.layers = nseq, nt, nch, tuple(layers)
        self.T = nch * 128
        self.dbg = dbg or {}
        self.stages = stages
        self.sched = sched


def build(cfg):
    nc = bass.Bass("TRN2", target_bir_lowering=False)
    P = Prog(nc)
    P.bulk = {"cst"}
    NSEQ, NT, NCH, T = cfg.nseq, cfg.nt, cfg.nch, cfg.T
    LAY = cfg.layers
    NL = len(LAY)

    xin = nc.dram_tensor("xT", [NSEQ * NT * 128, 8 * T], F32, kind="ExternalInput").ap()
    yout = nc.dram_tensor("yT", [NSEQ * NT * 128, 8 * T], F32, kind="ExternalOutput").ap()
    memd = nc.dram_tensor("mem", [NSEQ * 2 * 128, D], F32, kind="ExternalInput").ap()
    wall = nc.dram_tensor("wall", [NL * NG * 128, GSZ], F32, kind="ExternalInput").ap()
    wbf = nc.dram_tensor("wbf", [NL * NG * 128, GSZ], BF16, kind="Internal").ap()
    cst = nc.dram_tensor("cst", [128, 5 * 128], F32, kind="ExternalInput").ap()
    NPC = 3 * 8 * 2 + 8 * 2 + 24 * 5 + 16 * 2
    pcol = nc.dram_tensor("pcol", [128, NL * NPC], F32, kind="ExternalInput").ap()
    NPR = 32 * 2 + 1024
    prow = nc.dram_tensor("prow", [1, NL * NPR + 2 * D], F32, kind="ExternalInput").ap()
    wdt_d = nc.dram_tensor("wdt", [128, NL * 8 * 32], F32, kind="ExternalInput").ap()
    sgw_d = nc.dram_tensor("sgw", [128, NL * 8 * 128], F32, kind="ExternalInput").ap()
    dbg_d = {}
    for name, shp in cfg.dbg.items():
        dbg_d[name] = nc.dram_tensor("dbg_" + name, list(shp), F32, kind="ExternalOutput").ap()

    st = contextlib.ExitStack()
    with st:
        def sb(name, shape, dt):
            return st.enter_context(nc.sbuf_tensor(name, list(shape), dt))

        c32 = sb("c32", [128, 5, 128], F32)
        cbf = sb("cbf", [128, 5, 128], BF16)
        IDENT, TRI, UM, ONES, ODIV = 0, 1, 2, 3, 4
        pc = sb("pc", [128, NL, NPC], F32)
        O_LNG, O_LNB, O_SGG, O_SGB, O_CW, O_CB, O_DC, O_NG = 0, 24, 48, 56, 64, 160, 184, 200
        pr = sb("pr", [128, NL, 64], F32)
        memgb = sb("memgb", [128, 2, D], F32)
        wdt = sb("wdt_s", [128, NL, 8, 32], F32)
        wTm = sb("wTm", [128, NL, 8, 128], BF16)
        Esg = sb("Esg", [128, NL, 8, 128], F32)
        Ddg = sb("Ddg", [128, NL, 16, 128], BF16)
        KT = sb("KT", [128, NL, 8, MEM], BF16)
        Vt = sb("Vt", [128, NL, 2, D], BF16)
        NSLOT = 4
        wr = [sb("wr%d" % i, [128, GSZ], BF16) for i in range(NSLOT)]
        xts = [sb("xt0", [128, 8, T], F32), sb("xt1", [128, 8, T], F32)]
        xt = xts[0]
        KX = ["xt0"]
        xb = sb("xb", [128, 8, T], BF16)
        Sst = sb("Sst", [128, NL, 2048], F32)
        Sbf = sb("Sbf", [128, 2048], BF16)
        halo = sb("halo", [128, NL, 24, 3], F32)
        A1 = sb("A1", [128, 16, T], BF16)
        A2 = sb("A2", [128, 32, T], BF16)
        A4 = sb("A4", [128, 16, T], BF16)
        A3 = sb("A3", [128, 8, T], BF16)
        mg = sb("mg", [128, 8, T], BF16)
        W1 = sb("W1", [128, 2, 2048], F32)
        W2 = sb("W2", [128, 2, 2048], BF16)
        W3 = sb("W3", [128, 10, 512], BF16)
        sgt = sb("sgt", [128, 4, T], BF16)
        sm = sb("sm", [128, 512], F32)
        dahl = sb("dahl", [128, 4, 2, 32], BF16)
        lnst = sb("lnst", [128, 3, T], F32)

        ps = [st.enter_context(nc.psum_tensor("ps%d" % i, [128, 512], F32)) for i in range(8)]

        sems = {e: st.enter_context(nc.semaphore("s_" + e)) for e in ENGS}
        chan_names = (["cv%d" % i for i in range(8)] + ["cst", "sgw", "x0", "x1", "y0", "y1", "mem", "dbg"] + ["w%d" % i for i in range(NSLOT)])
        chans = {c: st.enter_context(nc.semaphore("c_" + c)) for c in chan_names}

        def kW1(i, qs):
            return [("W1", i, q) for q in qs]

        def flat(ks):
            out = []
            for k in ks:
                if isinstance(k, list):
                    out.extend(flat(k))
                else:
                    out.append(k)
            return out

        def A(eng, fn, r=(), w=(), dma=None, cost=300.0):
            return P.add(eng, fn, reads=flat(r), writes=flat(w), dma=dma, cost=cost)

        KXB = [("xb", k_) for k_ in range(8)]

        def kxt():
            return [(KX[0], k_) for k_ in range(8)]

        def fsz(ap):
            n = 1
            for d in ap.shape[1:]:
                n *= d
            return n

        ECOST = {"act": 0.75, "dve": 1.0, "pool": 2.0}

        def ecost(eng, out):
            return 220.0 + ECOST[eng] * fsz(out)

        def act(out, in_, func, r, w, **kw):
            A("act", lambda e: e.activation(out=out, in_=in_, func=func, **kw), r, w, cost=ecost("act", out) + (90.0 if kw else 0.0))

        def tt(eng, out, in0, in1, op, r, w):
            A(eng, lambda e: e.tensor_tensor(out=out, in0=in0, in1=in1, op=op), r, w, cost=ecost(eng, out))

        def ts(eng, out, in0, s1, s2, op0, op1, r, w):
            c_ = ecost(eng, out) if eng != "pool" else 3500.0
            if op1 is None:
                A(eng, lambda e: e.tensor_scalar(out=out, in0=in0, scalar1=s1, scalar2=None, op0=op0), r, w, cost=c_)
            else:
                A(eng, lambda e: e.tensor_scalar(out=out, in0=in0, scalar1=s1, scalar2=s2, op0=op0, op1=op1), r, w, cost=c_)

        def stt(out, in0, scalar, in1, op0, op1, r, w):
            A("dve", lambda e: e.scalar_tensor_tensor(out=out, in0=in0, scalar=scalar, in1=in1, op0=op0, op1=op1), r, w,
              cost=ecost("dve", out))

        def cp(eng, out, in_, r, w):
            if eng == "act":
                A("act", lambda e: e.copy(out=out, in_=in_), r, w, cost=ecost("act", out))
            else:
                A(eng, lambda e: e.tensor_copy(out=out, in_=in_), r, w, cost=ecost(eng, out))

        def mm(out, lhsT, rhs, start, stop, r, w):
            n_ = fsz(rhs)
            c_ = max(n_, 64) / 1.9 + 8.0
            if rhs.dtype == F32:
                c_ *= 4.0
            A("pe", lambda e: e.matmul(out, lhsT=lhsT, rhs=rhs, start=start, stop=stop), r, w, cost=c_)

        def tp(out, in_, ident, r, w):
            A("pe", lambda e: e.transpose(out, in_, ident), r, w, cost=90.0)

        def dbg(name, src_ap, key, row0=0):
            if name in dbg_d:
                d = dbg_d[name]
                n = src_ap.shape[0]
                A("pool", lambda e: e.dma_start(out=d[row0:row0 + n], in_=src_ap), r=key, w=[("dbg", name, row0)], dma="dbg")

        wstate = {"n": 0}

        def wload(li, gname):
            slot = wstate["n"] % NSLOT
            wstate["n"] += 1
            g = li * NG + GIDX[gname]
            A("sp", lambda e: e.dma_start(out=wr[slot][:], in_=wbf[g * 128:(g + 1) * 128, :]),
              r=[("wbf", g)], w=[("wr", slot)], dma="w%d" % slot, cost=7000.0)
            if li + 1 < NL:
                cast((li + 1) * NG + GIDX[gname], extra_reads=[("wr", slot)])
            return wr[slot], ("wr", slot)

        def w3(wt_, ncol):
            return wt_[:].rearrange("p (k c) -> p k c", c=ncol)

        ncast = {"n": 0}
        cast_done = set()

        def cast(g, extra_reads=()):
            if g in cast_done:
                return
            cast_done.add(g)
            i_ = ncast["n"]
            ncast["n"] += 1
            A("pool", lambda e: e.dma_start(out=wbf[g * 128:(g + 1) * 128, :], in_=wall[g * 128:(g + 1) * 128, :],
                                            max_dma_last_dim=8192), r=list(extra_reads), w=[("wbf", g)], dma="cv%d" % (i_ % 8), cost=12000.0)

        for i in range(NG):
            cast(i)
        A("sp", lambda e: e.dma_start(out=c32[:].rearrange("p a b -> p (a b)"), in_=cst), w=["c32"], dma="cst")
        A("sp", lambda e: e.dma_start(out=pc[:].rearrange("p a b -> p (a b)"), in_=pcol), w=["pc"], dma="cst")
        for li_ in range(NL):
            A("sp", lambda e, li_=li_: e.dma_start(out=pr[:, li_, :], in_=prow[:, li_ * NPR:li_ * NPR + 64].partition_broadcast(128)),
              w=(["pr"] if li_ == NL - 1 else [("prx", li_)]), dma="cst")
        A("sp", lambda e: e.dma_start(out=memgb[:].rearrange("p a b -> p (a b)"),
                                      in_=prow[:, NL * NPR:NL * NPR + 2 * D].partition_broadcast(128)), w=["memgb"], dma="cst")
        A("sp", lambda e: e.dma_start(out=wdt[:].rearrange("p a b c -> p (a b c)"), in_=wdt_d), w=["wdt"], dma="cst")
        cp("dve", cbf[:], c32[:], ["c32"], ["cbf"])
        for li in range(NL):
            act(pr[:, li, 32:64], pr[:, li, 32:64], AF.Exp, ["pr"], ["pr"])
            ts("dve", pr[:, li, 32:64], pr[:, li, 32:64], -1.0, None, ALU.mult, None, ["pr"], ["pr"])
        sgbt = A2[:].rearrange("p a t -> p (a t)").bitcast(F32)[:, 0:NL * 1024].rearrange("p (l f) -> p l f", f=1024)
        for li_ in range(NL):
            A("sp", lambda e, li_=li_: e.dma_start(out=sgbt[:, li_, :],
                                                   in_=prow[:, li_ * NPR + 64:(li_ + 1) * NPR].partition_broadcast(128)),
              w=["sgbt"] + [("A2", i_) for i_ in range(32)], dma="sgw")
        for li in range(NL):
            w1v = W1[:, 0, 0:1024].rearrange("p (g t) -> p g t", t=128)
            A("sp", lambda e, li=li: e.dma_start(out=W1[:, 0, 0:1024], in_=sgw_d[:, li * 1024:(li + 1) * 1024]),
              r=[], w=[*kW1(0, (0, 1))], dma="sgw")
            tt("dve", w1v, w1v, bc(c32[:, TRI:TRI + 1, :], [128, 8, 128]), ALU.mult, [*kW1(0, (0, 1)), "c32"], [*kW1(0, (0, 1))])
            cp("pool", wTm[:, li], w1v, [*kW1(0, (0, 1))], [("wTm", li)])
            for hh in range(2):
                mm(ps[hh][:], c32[:, ONES, :], W1[:, 0, hh * 512:(hh + 1) * 512], True, True, [*kW1(0, (0, 1)), "c32"], [("ps", hh)])
                e_v = Esg[:, li, hh * 4:(hh + 1) * 4, :]
                tt("dve", e_v, ps[hh][:].rearrange("p (g t) -> p g t", t=128),
                   bc(pc[:, li, O_SGB + hh * 4:O_SGB + hh * 4 + 4].unsqueeze(2), [128, 4, 128]), ALU.mult,
                   [("ps", hh), "pc"], [("Esg", li)])
                tt("pool", e_v, e_v, sgbt[:, li, hh * 512:(hh + 1) * 512].rearrange("p (g t) -> p g t", t=128),
                   ALU.add, [("Esg", li), "sgbt"], [("Esg", li)])
            for kc in range(16):
                ts("pool", Ddg[:, li, kc, :], c32[:, IDENT, :], pc[:, li, O_DC + kc:O_DC + kc + 1], None, ALU.mult, None,
                   ["c32", "pc"], [("Ddg", li)])
        dbg("Esg", Esg[:, 0].rearrange("p g t -> p (g t)"), [("Esg", 0)])

        eps_ln = LN_EPS
        psrot = {"n": 0}

        def nextps():
            b = psrot["n"] % 4
            psrot["n"] += 1
            return b

        def layer_norm(li, which):
            rbf = A1[:, 0:8, :]
            rsq = A1[:, 8:16, :]
            kA2 = [("A1", i) for i in range(16)]
            cp("dve", rbf, xt[:], kxt(), kA2[0:8])
            act(rsq, xt[:], AF.Square, kxt(), kA2[8:16])
            for kc in range(8):
                mm(ps[4][:, 0:T], cbf[:, ODIV, :], rbf[:, kc, :], kc == 0, kc == 7, kA2[0:8] + ["cbf"], [("ps", 4)])
            for kc in range(8):
                mm(ps[5][:, 0:T], cbf[:, ODIV, :], rsq[:, kc, :], kc == 0, kc == 7, kA2[8:16] + ["cbf"], [("ps", 5)])
            mean, var, rstd = lnst[:, 0, :], lnst[:, 1, :], lnst[:, 2, :]
            cp("act", mean, ps[4][:, 0:T], [("ps", 4)], ["ln0"])
            act(var, ps[4][:, 0:T], AF.Square, [("ps", 4)], ["ln1"])
            tt("dve", var, ps[5][:, 0:T], var, ALU.subtract, [("ps", 5), "ln1"], ["ln1"])
            ts("dve", var, var, eps_ln, None, ALU.add, None, ["ln1"], ["ln1"])
            act(rstd, var, AF.Ln, ["ln1"], ["ln2"])
            act(rstd, rstd, AF.Exp, ["ln2"], ["ln2"], scale=-0.5)
            for kc in range(8):
                o = O_LNG + which * 8 + kc
                ob = O_LNB + which * 8 + kc
                kk = (KX[0], kc)
                tt("dve", xt[:, kc, :], xt[:, kc, :], mean, ALU.subtract, [kk, "ln0"], [kk])
                stt(xt[:, kc, :], xt[:, kc, :], pc[:, li, o:o + 1], rstd, ALU.mult, ALU.mult, [kk, "ln2", "pc"], [kk])
                act(xb[:, kc, :], xt[:, kc, :], AF.Identity, [kk, "pc"], [("xb", kc)], bias=pc[:, li, ob:ob + 1])
                ts("dve", xt[:, kc, :], xt[:, kc, :], pc[:, li, ob:ob + 1], None, ALU.add, None, [kk, "pc"], [kk])

        def mem_phase(s):
            mraw = W1[:].rearrange("p a b -> p (a b)")[:, 0:2048].rearrange("p (c d) -> p c d", d=D)
            kW1m = kW1(0, (0, 1, 2, 3))
            A("pool", lambda e: e.dma_start(out=mraw, in_=memd[s * 256:(s + 1) * 256, :].rearrange("(c p) d -> p c d", p=128)),
              r=[], w=kW1m, dma="mem")
            st6 = sm[:, 0:24].rearrange("p (c h k) -> p c h k", c=2, h=2)
            mv = sm[:, 24:28].rearrange("p (c k) -> p c k", k=2)
            for c in range(2):
                for h in range(2):
                    A("dve", lambda e, c=c, h=h: e.bn_stats(out=st6[:, c, h, :], in_=mraw[:, c, h * 512:(h + 1) * 512]), kW1m, ["sm"])
                A("dve", lambda e, c=c: e.bn_aggr(out=mv[:, c, :], in_=st6[:, c].rearrange("p h k -> p (h k)")), ["sm"], ["sm"])
            rs = sm[:, 28:30]
            ts("dve", rs, mv[:, :, 1], LN_EPS, None, ALU.add, None, ["sm"], ["sm"])
            act(rs, rs, AF.Ln, ["sm"], ["sm"])
            act(rs, rs, AF.Exp, ["sm"], ["sm"], scale=-0.5)
            mnb = W2[:, 0, :].rearrange("p (c d) -> p c d", d=D)
            for c in range(2):
                ts("dve", mraw[:, c, :], mraw[:, c, :], mv[:, c, 0:1], rs[:, c:c + 1], ALU.subtract, ALU.mult, kW1m + ["sm"], kW1m)
                tt("pool", mraw[:, c, :], mraw[:, c, :], memgb[:, 0, :], ALU.mult, kW1m + ["memgb"], kW1m)
                tt("dve", mnb[:, c, :], mraw[:, c, :], memgb[:, 1, :], ALU.add, kW1m + ["memgb"], ["W2a"])
            memT = W2[:, 1, :].rearrange("p (k m) -> p k m", m=MEM)
            for c in range(2):
                pb16 = ps[4 + c][:].bitcast(BF16)
                for kc in range(8):
                    tp(pb16[:, kc * 128:(kc + 1) * 128], mnb[:, c, kc * 128:(kc + 1) * 128], cbf[:, IDENT, :],
                       ["W2a", "cbf"], [("ps", 4 + c)])
                cp("act", memT[:, :, c * 128:(c + 1) * 128], pb16.rearrange("p (k m) -> p k m", m=128),
                   [("ps", 4 + c)], ["W2b"])
            for li in range(NL):
                for gi in range(2):
                    wt_, wk = wload(li, "kv%d" % gi)
                    wv = w3(wt_, 512)
                    for j in range(4):
                        b = nextps()
                        for kc in range(8):
                            mm(ps[b][:, 0:MEM], wv[:, kc, j * 128:(j + 1) * 128], memT[:, kc, :], kc == 0, kc == 7,
                               [wk, "W2b"], [("ps", b)])
                        cp("act", KT[:, li, gi * 4 + j, :], ps[b][:, 0:MEM], [("ps", b)], [("KT", li)])
                for gi in range(2):
                    wt_, wk = wload(li, "kv%d" % (2 + gi))
                    wv = w3(wt_, 512)
                    for c in range(2):
                        b = nextps()
                        for kc in range(8):
                            mm(ps[b][:], memT[:, kc, c * 128:(c + 1) * 128], wv[:, kc, :], kc == 0, kc == 7,
                               [wk, "W2b"], [("ps", b)])
                        cp("dve", Vt[:, li, c, gi * 512:(gi + 1) * 512], ps[b][:], [("ps", b)], [("Vt", li)])

        def proj_fm(li, gname, rhs3, rkeys, nk, ncol, nblk, evac):
            wt_, wk = wload(li, gname)
            wv = w3(wt_, ncol)
            for j in range(nblk):
                b = nextps()
                for kc in range(nk):
                    mm(ps[b][:, 0:T], wv[:, kc, j * 128:(j + 1) * 128], rhs3[:, kc, :], kc == 0, kc == nk - 1,
                       [wk] + rkeys, [("ps", b)])
                evac(j, ps[b][:, 0:T], ("ps", b))

        kA1 = [("A1", i) for i in range(16)]
        kA2 = [("A2", i) for i in range(32)]
        kA3 = [("A3", i) for i in range(8)]
        kA4 = [("A4", i) for i in range(16)]
        kmg = [("mg", i) for i in range(8)]

        def mixer(li, dbgrow):
            def sg_branch():
                uT = A4[:, 0:8, :]
                vn = A4[:, 8:16, :].rearrange("p a t -> p (a t)").rearrange("p (c f) -> p c f", f=1024)
                st6 = sm[:, 0:NCH * 12].rearrange("p (c h k) -> p c h k", c=NCH, h=2)
                mv = sm[:, 48:48 + NCH * 2].rearrange("p (c k) -> p c k", k=2)
                rs = sm[:, 64:64 + NCH]
                for gi in range(2):
                    wt_, wk = wload(li, "v%d" % gi)
                    wv = w3(wt_, 512)
                    for c in range(NCH):
                        b = nextps()
                        for kc in range(8):
                            mm(ps[b][:], xb[:, kc, c * 128:(c + 1) * 128], wv[:, kc, :], kc == 0, kc == 7, [wk, KXB], [("ps", b)])
                        act(vn[:, c, gi * 512:(gi + 1) * 512], ps[b][:], AF.Gelu, [("ps", b)], kA4[8:16])
                        A("dve", lambda e, c=c, gi=gi: e.bn_stats(out=st6[:, c, gi, :], in_=vn[:, c, gi * 512:(gi + 1) * 512]),
                          kA4[8:16], ["sm"])
                for c in range(NCH):
                    A("dve", lambda e, c=c: e.bn_aggr(out=mv[:, c, :], in_=st6[:, c].rearrange("p h k -> p (h k)")), ["sm"], ["sm"])
                ts("dve", rs, mv[:, :, 1], LN_EPS, None, ALU.add, None, ["sm"], ["sm"])
                act(rs, rs, AF.Ln, ["sm"], ["sm"])
                act(rs, rs, AF.Exp, ["sm"], ["sm"], scale=-0.5)
                nmr = sm[:, 72:72 + NCH]
                tt("dve", nmr, mv[:, :, 0], rs, ALU.mult, ["sm"], ["sm"])
                ts("dve", nmr, nmr, -1.0, None, ALU.mult, None, ["sm"], ["sm"])
                for c in range(NCH):
                    act(vn[:, c, :], vn[:, c, :], AF.Identity, kA4[8:16] + ["sm"], kA4[8:16], scale=rs[:, c:c + 1], bias=nmr[:, c:c + 1])
                for gi in range(2):
                    proj_fm(li, "u%d" % gi, xb, [KXB], 8, 512, 4,
                            lambda j, p_, pk, gi=gi: act(uT[:, gi * 4 + j, :], p_, AF.Gelu, [pk], [kA4[gi * 4 + j]]))
                for c in range(NCH):
                    for g in range(8):
                        b = 6 + g // 4
                        mm(ps[b][:, (g % 4) * 128:(g % 4 + 1) * 128], vn[:, c, g * 128:(g + 1) * 128], wTm[:, li, g, :],
                           True, True, kA4[8:16] + [("wTm", li)], [("ps", b)])
                    t1 = W1[:, 0, 0:1024].rearrange("p (g t) -> p g t", t=128)
                    for hh in range(2):
                        tt("dve", t1[:, hh * 4:(hh + 1) * 4, :], ps[6 + hh][:].rearrange("p (g t) -> p g t", t=128),
                           bc(pc[:, li, O_SGG + hh * 4:O_SGG + hh * 4 + 4].unsqueeze(2), [128, 4, 128]), ALU.mult,
                           [("ps", 6 + hh), "pc"], [*kW1(0, (0, 1))])
                    tt("pool", t1, t1, Esg[:, li], ALU.add, [*kW1(0, (0, 1)), ("Esg", li)], [*kW1(0, (0, 1))])
                    tt("dve", uT[:, :, c * 128:(c + 1) * 128], t1, uT[:, :, c * 128:(c + 1) * 128], ALU.mult,
                       [*kW1(0, (0, 1))] + kA4[0:8], kA4[0:8])
                dbg("saT", uT.rearrange("p a t -> p (a t)"), kA4[0:8], row0=dbgrow)
                for gi in range(2):
                    proj_fm(li, "ga%d" % gi, xb, [KXB], 8, 512, 4,
                            lambda j, p_, pk: act(sgt[:, j, :], p_, AF.Sigmoid, [pk], [("sgt", j)]))
                    proj_fm(li, "pa%d" % gi, uT, kA4[0:8], 8, 512, 4,
                            lambda j, p_, pk, gi=gi: tt("dve", mg[:, gi * 4 + j, :], p_, sgt[:, j, :], ALU.mult,
                                                        [pk, ("sgt", j)], [kmg[gi * 4 + j]]))
                dbg("m1", mg[:].rearrange("p a t -> p (a t)"), kmg, row0=dbgrow)


            xsT = A2[:, 16:32, :]
            BCT = A3
            ztm = A2[:, 0:16, :].rearrange("p a t -> p (a t)").rearrange("p (c f) -> p c f", f=2048)
            kxs = kA2[16:32]
            kz = kA2[0:16]
            for gi in range(6):
                def ev(j, p_, pk, gi=gi):
                    blk = gi * 4 + j
                    i = blk % 2
                    raw = W1[:, i, 0:T + 3]
                    acc = W1[:, i, 1024:1024 + T]
                    kr, ka = kW1(i, (0, 1)), kW1(i, (2,))
                    cp("pool", raw[:, 0:3], halo[:, li, blk, :], [("halo", li, blk)], kr)
                    act(raw[:, 3:3 + T], p_, AF.Identity, [pk], kr)
                    cp("pool", halo[:, li, blk, :], raw[:, T:T + 3], kr, [("halo", li, blk)])
                    ow = O_CW + blk * 4
                    act(acc, raw[:, 0:T], AF.Identity, kr + ["pc"], ka, scale=pc[:, li, ow:ow + 1],
                        bias=pc[:, li, O_CB + blk:O_CB + blk + 1])
                    for k in range(1, 4):
                        stt(acc, raw[:, k:k + T], pc[:, li, ow + k:ow + k + 1], acc, ALU.mult, ALU.add, kr + ka + ["pc"], ka)
                    if blk < 16:
                        act(xsT[:, blk, :], acc, AF.Silu, ka, [kxs[blk]])
                    else:
                        act(BCT[:, blk - 16, :], acc, AF.Silu, ka, [kA3[blk - 16]])
                proj_fm(li, "xbc%d" % gi, xb, [KXB], 8, 512, 4, ev)
            for gi in range(4):
                wt_, wk = wload(li, "z%d" % gi)
                wv = w3(wt_, 512)
                for c in range(NCH):
                    b = nextps()
                    for kc in range(8):
                        mm(ps[b][:], xb[:, kc, c * 128:(c + 1) * 128], wv[:, kc, :], kc == 0, kc == 7, [wk, KXB], [("ps", b)])
                    act(ztm[:, c, gi * 512:(gi + 1) * 512], ps[b][:], AF.Silu, [("ps", b)], kz)
            for c in range(NCH):
                dtc = sm[:, 96 + c * 32:128 + c * 32]
                dac = sm[:, 224 + c * 32:256 + c * 32]
                for kc in range(8):
                    mm(ps[5][:, 0:32], xt[:, kc, c * 128:(c + 1) * 128], wdt[:, li, kc, :], kc == 0, kc == 7,
                       [kxt(), "wdt"], [("ps", 5)])
                tt("dve", dtc, ps[5][:, 0:32], pr[:, li, 0:32], ALU.add, [("ps", 5), "pr"], [("dt", c)])
                act(dtc, dtc, AF.Exp, [("dt", c)], [("dt", c)])
                act(dtc, dtc, AF.Ln, [("dt", c)], [("dt", c)], bias=1.0)
                tt("pool", dac, dtc, pr[:, li, 32:64], ALU.mult, [("dt", c), "pr"], [("da", c)])
                dhl = dahl[:, c]
                cp("dve", dhl[:, 0, :], dac, [("da", c)], [("dahl", c)])
                tt("dve", dhl[:, 1, :], dac, dhl[:, 0, :], ALU.subtract, [("da", c), ("dahl", c)], [("dahl", c)])
            cp("act", Sbf[:], Sst[:, li, :], [("Sst", li)], ["Sbf"])
            yT = A1[:, 0:16, :]
            xdt = W2[:, 0, :]
            xdt2 = W2[:, 1, :]
            dec = [W3[:, 0, :], W3[:, 1, :], W3[:, 2, :], W3[:, 3, :]]
            CBm = W3[:, 4, :]
            Btm = W3[:, 5, :]
            yn = W3[:, 6:10, :].rearrange("p a b -> p (a b)")
            ex = sm[:, 352:448]
            ecs, cdb, dst = ex[:, 0:32], ex[:, 32:64], ex[:, 64:96]
            dtds = sm[:, 448:480]
            ssq = sm[:, 480:484]
            rsg = sm[:, 484:488]
            for c in range(NCH):
                cs_ = slice(c * 128, (c + 1) * 128)
                dtc = sm[:, 96 + c * 32:128 + c * 32]
                dac = sm[:, 224 + c * 32:256 + c * 32]
                mm(ps[5][:, 0:32], c32[:, TRI, :], dac, True, True, ["c32", ("da", c)], [("ps", 5)])
                mm(ps[5][:, 32:64], c32[:, ONES, :], dac, True, True, ["c32", ("da", c)], [("ps", 5)])
                mm(ps[5][:, 64:96], c32[:, UM, :], dac, True, True, ["c32", ("da", c)], [("ps", 5)])
                act(ex, ps[5][:, 0:96], AF.Exp, [("ps", 5)], ["ex"])
                tt("dve", dtds, dtc, dst, ALU.mult, [("dt", c), "ex"], ["dtds"])
                pb16 = ps[4][:].bitcast(BF16)
                for rnd in range(2):
                    for j in range(8):
                        blk = rnd * 8 + j
                        tp(pb16[:, j * 128:(j + 1) * 128], xsT[:, blk, cs_], cbf[:, IDENT, :], [kxs[blk], "cbf"], [("ps", 4)])
                    hs = slice(rnd * 16, (rnd + 1) * 16)
                    o1 = xdt[:, rnd * 1024:(rnd + 1) * 1024].rearrange("p (h d) -> p h d", d=64)
                    o2 = xdt2[:, rnd * 1024:(rnd + 1) * 1024].rearrange("p (h d) -> p h d", d=64)
                    tt("dve", o1, pb16.rearrange("p (h d) -> p h d", d=64), bc(dtc[:, hs].unsqueeze(2), [128, 16, 64]),
                       ALU.mult, [("ps", 4), ("dt", c)], [("xdt", rnd)])
                    tt("pool", o2, o1, bc(dst[:, hs].unsqueeze(2), [128, 16, 64]), ALU.mult, [("xdt", rnd), "ex"], [("xdt2", rnd)])
                pB = ps[5][:, 256:512].bitcast(BF16)
                for g in range(4):
                    tp(pB[:, g * 128:(g + 1) * 128], BCT[:, g, cs_], cbf[:, IDENT, :], [kA3[g], "cbf"], [("ps", 5)])
                cp("act", Btm, pB, [("ps", 5)], ["Btm"])
                for g in range(4):
                    mm(ps[2][:, g * 128:(g + 1) * 128], BCT[:, g, cs_], BCT[:, 4 + g, cs_], True, True,
                       [kA3[g], kA3[4 + g]], [("ps", 2)])
                tt("dve", CBm.rearrange("p (g l) -> p g l", l=128), ps[2][:].rearrange("p (g l) -> p g l", l=128),
                   bc(c32[:, TRI:TRI + 1, :], [128, 4, 128]), ALU.mult, [("ps", 2), "c32"], ["CBm"])
                def stageA(g):
                    for hh in range(2):
                        h0 = g * 8 + hh * 4
                        di = (g % 2) * 2 + hh
                        rr = W1[:, hh, 1024:1536].bitcast(BF16).rearrange("p (a r l) -> p a r l", a=2, l=128)
                        for a_ in range(2):
                            tt("pool", rr[:, a_], bc(cbf[:, TRI:TRI + 1, :], [128, 4, 128]),
                               bc(dahl[:, c, a_, h0:h0 + 4].unsqueeze(2), [128, 4, 128]), ALU.mult, ["cbf", ("dahl", c)], kW1(hh, (2,)))
                        for a_ in range(2):
                            mm(ps[hh][:], cbf[:, UM, :], rr[:, a_].rearrange("p r l -> p (r l)"), a_ == 0, a_ == 1,
                               kW1(hh, (2,)) + ["cbf"], [("ps", hh)])
                        d3 = dec[di].rearrange("p (r l) -> p r l", l=128)
                        act(dec[di], ps[hh][:], AF.Exp, [("ps", hh)], [("dec", di)])
                        tt("dve", d3, d3, bc(CBm[:, g * 128:(g + 1) * 128].unsqueeze(1), [128, 4, 128]), ALU.mult,
                           [("dec", di), "CBm"], [("dec", di)])

                stageA(0)
                for g in range(4):
                    if g + 1 < 4:
                        stageA(g + 1)
                    dg = (g % 2) * 2
                    bpy, bpo, bst = (3, 6, 7) if g % 2 == 0 else (2, 5, 4)
                    for i in range(4):
                        kc = g * 4 + i
                        mm(ps[bpy][:, i * 128:(i + 1) * 128], xsT[:, kc, cs_], Ddg[:, li, kc, :], i == 0, False,
                           [kxs[kc], ("Ddg", li)], [("ps", bpy)])
                    for r in range(8):
                        h = g * 8 + r
                        mm(ps[bpy][:, r * 64:(r + 1) * 64], dec[dg + r // 4][:, (r % 4) * 128:(r % 4 + 1) * 128], xdt[:, h * 64:(h + 1) * 64],
                           False, r == 7, [("dec", dg + r // 4), ("xdt", h // 16)], [("ps", bpy)])
                    mm(ps[bpo][:], BCT[:, 4 + g, cs_], Sbf[:, g * 512:(g + 1) * 512], True, True, [kA3[4 + g], "Sbf"], [("ps", bpo)])
                    i2 = g % 2
                    yw = W1[:, i2, 1536:2048]
                    yg = W1[:, i2, 0:512]
                    junk = W1[:, i2, 512:1024]
                    tt("dve", yw.rearrange("p (h d) -> p h d", d=64), ps[bpo][:].rearrange("p (h d) -> p h d", d=64),
                       bc(ecs[:, g * 8:(g + 1) * 8].unsqueeze(2), [128, 8, 64]), ALU.mult, [("ps", bpo), "ex"], kW1(i2, (3,)))
                    tt("dve", yw, yw, ps[bpy][:], ALU.add, kW1(i2, (3,)) + [("ps", bpy)], kW1(i2, (3,)))
                    tt("pool", yg, yw, ztm[:, c, g * 512:(g + 1) * 512], ALU.mult, kW1(i2, (3,)) + kz, kW1(i2, (0,)))
                    A("act", lambda e, yg=yg, junk=junk, g=g: e.activation(out=junk, in_=yg, func=AF.Square, accum_out=ssq[:, g:g + 1]),
                      kW1(i2, (0,)), kW1(i2, (1,)) + [("ssq", g)])
                    ts("dve", rsg[:, g:g + 1], ssq[:, g:g + 1], 1.0 / 512.0, RMS_EPS, ALU.mult, ALU.add, [("ssq", g)], [("rsg", g)])
                    act(rsg[:, g:g + 1], rsg[:, g:g + 1], AF.Ln, [("rsg", g)], [("rsg", g)])
                    act(rsg[:, g:g + 1], rsg[:, g:g + 1], AF.Exp, [("rsg", g)], [("rsg", g)], scale=-0.5)
                    act(yn[:, g * 512:(g + 1) * 512], yg, AF.Identity, kW1(i2, (0,)) + [("rsg", g)], [("yn", g)], scale=rsg[:, g:g + 1])
                    mm(ps[bst][:], Btm[:, g * 128:(g + 1) * 128], xdt2[:, g * 512:(g + 1) * 512], True, True,
                       ["Btm", ("xdt2", g // 2)], [("ps", bst)])
                    Sg = Sst[:, li, g * 512:(g + 1) * 512]
                    tt("pool", Sg.rearrange("p (h d) -> p h d", d=64), Sg.rearrange("p (h d) -> p h d", d=64),
                       bc(cdb[:, g * 8:(g + 1) * 8].unsqueeze(2), [128, 8, 64]), ALU.mult, [("Sst", li), "ex"], [("Sst", li)])
                    tt("dve", Sg, Sg, ps[bst][:], ALU.add, [("Sst", li), ("ps", bst)], [("Sst", li)])
                    cp("act", Sbf[:, g * 512:(g + 1) * 512], Sg, [("Sst", li)], ["Sbf"])
                for rnd in range(2):
                    for j in range(8):
                        blk = rnd * 8 + j
                        tp(pb16[:, j * 128:(j + 1) * 128], yn[:, blk * 128:(blk + 1) * 128], cbf[:, IDENT, :],
                           [("yn", blk // 4), "cbf"], [("ps", 4)])
                    og = O_NG + rnd * 8
                    tt("dve", yT[:, rnd * 8:(rnd + 1) * 8, cs_], pb16.rearrange("p (k l) -> p k l", l=128),
                       bc(pc[:, li, og:og + 8].unsqueeze(2), [128, 8, 128]), ALU.mult, [("ps", 4), "pc"], kA1[rnd * 8:(rnd + 1) * 8])
            dbg("yT", yT.rearrange("p a t -> p (a t)"), kA1, row0=dbgrow)
            sg_branch()
            for gi in range(2):
                proj_fm(li, "gb%d" % gi, xb, [KXB], 8, 512, 4,
                        lambda j, p_, pk: act(sgt[:, j, :], p_, AF.Sigmoid, [pk], [("sgt", j)]))
                for pi in range(2):
                    def ev2(jj, p_, pk, gi=gi, pi=pi):
                        blk = (gi * 2 + pi) * 2 + jj
                        j = pi * 2 + jj
                        tmp = W1[:, jj, 1536:1536 + T]
                        tt("dve", tmp, p_, sgt[:, j, :], ALU.mult, [pk, ("sgt", j)], kW1(jj, (3,)))
                        tt("pool", mg[:, blk, :], tmp, mg[:, blk, :], ALU.add, kW1(jj, (3,)) + [kmg[blk]], [kmg[blk]])
                    proj_fm(li, "pb%d" % (gi * 2 + pi), yT, kA1, 16, 256, 2, ev2)
            for gi in range(2):
                proj_fm(li, "mo%d" % gi, mg, kmg, 8, 512, 4,
                        lambda j, p_, pk, gi=gi: stt(xt[:, gi * 4 + j, :], xt[:, gi * 4 + j, :], ALPHA, p_, ALU.mult, ALU.add,
                                                     [(KX[0], gi * 4 + j), pk], [(KX[0], gi * 4 + j)]))
            layer_norm(li, 0)
            dbg("x1", xt[:].rearrange("p a t -> p (a t)"), kxt(), row0=dbgrow)

        def xattn(li):
            qT = A1[:, 0:8, :]
            ET = A1[:, 8:16, :]
            oT = A3
            for gi in range(2):
                proj_fm(li, "xq%d" % gi, xb, [KXB], 8, 512, 4,
                        lambda j, p_, pk, gi=gi: act(qT[:, gi * 4 + j, :], p_, AF.Identity, [pk], [kA1[gi * 4 + j]], scale=0.0625))
            rden = lnst[:, 0, :]
            for hd in range(4):
                for mc in range(2):
                    b = nextps()
                    for dd in range(2):
                        mm(ps[b][:, 0:T], KT[:, li, 2 * hd + dd, mc * 128:(mc + 1) * 128], qT[:, 2 * hd + dd, :], dd == 0, dd == 1,
                           [("KT", li), kA1[2 * hd + dd]], [("ps", b)])
                    act(ET[:, hd * 2 + mc, :], ps[b][:, 0:T], AF.Exp, [("ps", b)], [kA1[8 + hd * 2 + mc]])
                for mc in range(2):
                    mm(ps[6][:, 0:T], cbf[:, ONES, :], ET[:, hd * 2 + mc, :], mc == 0, mc == 1, ["cbf", kA1[8 + hd * 2 + mc]], [("ps", 6)])
                act(rden, ps[6][:, 0:T], AF.Ln, [("ps", 6)], ["ln0"])
                act(rden, rden, AF.Exp, ["ln0"], ["ln0"], scale=-1.0)
                for dd in range(2):
                    b = nextps()
                    for mc in range(2):
                        mm(ps[b][:, 0:T], Vt[:, li, mc, (2 * hd + dd) * 128:(2 * hd + dd + 1) * 128], ET[:, hd * 2 + mc, :],
                           mc == 0, mc == 1, [("Vt", li), kA1[8 + hd * 2 + mc]], [("ps", b)])
                    tt("dve", oT[:, 2 * hd + dd, :], ps[b][:, 0:T], rden, ALU.mult, [("ps", b), "ln0"], [kA3[2 * hd + dd]])
            for gi in range(2):
                proj_fm(li, "xo%d" % gi, oT, kA3, 8, 512, 4,
                        lambda j, p_, pk, gi=gi: stt(xt[:, gi * 4 + j, :], xt[:, gi * 4 + j, :], ALPHA, p_, ALU.mult, ALU.add,
                                                     [(KX[0], gi * 4 + j), pk], [(KX[0], gi * 4 + j)]))
            layer_norm(li, 1)

        def ffn(li):
            hT = A2[:, 0:NKF, :]
            for gi in range(11):
                wt_, wk = wload(li, "fi%d" % gi)
                wv = w3(wt_, 512)
                for jj in range(2):
                    bg = nextps()
                    for kc in range(8):
                        mm(ps[bg][:, 0:T], wv[:, kc, jj * 128:(jj + 1) * 128], xb[:, kc, :], kc == 0, kc == 7, [wk, KXB], [("ps", bg)])
                    bu = nextps()
                    for kc in range(8):
                        mm(ps[bu][:, 0:T], wv[:, kc, (2 + jj) * 128:(3 + jj) * 128], xb[:, kc, :], kc == 0, kc == 7, [wk, KXB], [("ps", bu)])
                    tmp = lnst[:, jj, :]
                    act(tmp, ps[bg][:, 0:T], AF.Silu, [("ps", bg)], ["ln%d" % jj])
                    tt("dve", hT[:, gi * 2 + jj, :], ps[bu][:, 0:T], tmp, ALU.mult, [("ps", bu), "ln%d" % jj], [kA2[gi * 2 + jj]])
            for blk in range(8):
                wt_, wk = wload(li, "fo%d" % blk)
                wv = wt_[:, 0:NKF * 128].rearrange("p (k c) -> p k c", c=128)
                b = nextps()
                for kc in range(NKF):
                    mm(ps[b][:, 0:T], wv[:, kc, :], hT[:, kc, :], kc == 0, kc == NKF - 1, [wk, kA2[kc]], [("ps", b)])
                stt(xt[:, blk, :], xt[:, blk, :], ALPHA, ps[b][:, 0:T], ALU.mult, ALU.add, [(KX[0], blk), ("ps", b)], [(KX[0], blk)])
            layer_norm(li, 2)

        for s in range(NSEQ):
            if "xattn" in cfg.stages:
                mem_phase(s)
            for li in range(NL):
                A("pool", lambda e, li=li: e.memset(Sst[:, li, :], 0.0), [], [("Sst", li)])
                A("pool", lambda e, li=li: e.memset(halo[:, li], 0.0), [("halo", li, b_) for b_ in range(24)], [("halo", li, b_) for b_ in range(24)])
            def xload(ti_):
                row_ = (s * NT + ti_) * 128
                buf = xts[ti_ % 2]
                A("pool", lambda e: e.dma_start(out=buf[:].rearrange("p a t -> p (a t)"), in_=xin[row_:row_ + 128, :]),
                  r=[], w=[("xt%d" % (ti_ % 2), k_) for k_ in range(8)], dma="x%d" % (ti_ % 2), cost=6000.0)
            xload(0)
            for ti in range(NT):
                row = (s * NT + ti) * 128
                xt = xts[ti % 2]
                KX[0] = "xt%d" % (ti % 2)
                if ti + 1 < NT:
                    xload(ti + 1)
                cp("act", xb[:], xt[:], kxt(), [KXB])
                for li in range(NL):
                    if "mixer" in cfg.stages:
                        mixer(li, (s * NT + ti) * 128)
                    if "xattn" in cfg.stages:
                        xattn(li)
                    if "ffn" in cfg.stages:
                        ffn(li)
                A("pool", lambda e, row=row, xt=xt: e.dma_start(out=yout[row:row + 128, :], in_=xt[:].rearrange("p a t -> p (a t)")),
                  r=kxt(), w=[("y", row)], dma="y%d" % (ti % 2), cost=6000.0)
        allout = [("y", (s * NT + ti) * 128) for s in range(NSEQ) for ti in range(NT)]
        allout += [k for k in P.last_writer if isinstance(k, tuple) and k[0] == "dbg"]
        A("sp", lambda e: e.nop(), r=allout, w=[])

        if cfg.sched:
            P.schedule()
            if cfg.dbg is not None and "verbose" in cfg.stages:
                print("sched est_ns", P.est_ns, {e: (len(P.ops[e]), round(sum(o.cost for o in P.ops[e] if not o.is_dma))) for e in ENGS})
        P.finalize()
        with nc.Block() as block:
            @block.sync
            def _(e):
                P.emit("sp", e, sems, chans)

            @block.scalar
            def _(e):
                P.emit("act", e, sems, chans)

            @block.vector
            def _(e):
                P.emit("dve", e, sems, chans)

            @block.gpsimd
            def _(e):
                P.emit("pool", e, sems, chans)

            @block.tensor
            def _(e):
                P.emit("pe", e, sems, chans)
    return nc


def _kgroup(Wm, c0, ncol):
    K = Wm.shape[0]
    blk = Wm[:, c0:c0 + ncol].reshape(K // 128, 128, ncol).transpose(1, 0, 2).reshape(128, -1)
    out = np.zeros((128, GSZ), np.float32)
    out[:, :blk.shape[1]] = blk
    return out


def layout_weights(inp, layers):
    allg = []
    for l in layers:
        w_in = inp["w_in"][l]
        g = {}
        for i in range(4):
            g["kv%d" % i] = _kgroup(inp["w_xkv"][l], i * 512, 512)
        for i in range(2):
            g["u%d" % i] = _kgroup(w_in, i * 512, 512)
            g["v%d" % i] = _kgroup(w_in, 1024 + i * 512, 512)
            g["ga%d" % i] = _kgroup(w_in, 7200 + i * 512, 512)
            g["gb%d" % i] = _kgroup(w_in, 8224 + i * 512, 512)
            g["pa%d" % i] = _kgroup(inp["p_a"][l], i * 512, 512)
            g["mo%d" % i] = _kgroup(inp["w_mix_o"][l], i * 512, 512)
            g["xq%d" % i] = _kgroup(inp["w_xq"][l], i * 512, 512)
            g["xo%d" % i] = _kgroup(inp["w_xo"][l], i * 512, 512)
        for i in range(4):
            g["z%d" % i] = _kgroup(w_in, 2048 + i * 512, 512)
            g["pb%d" % i] = _kgroup(inp["p_b"][l], i * 256, 256)
        for i in range(6):
            g["xbc%d" % i] = _kgroup(w_in, 4096 + i * 512, 512)
        wfi = inp["w_ffn_in"][l]
        for i in range(11):
            cols = np.concatenate([np.arange(i * 256, i * 256 + 256), FFH + np.arange(i * 256, i * 256 + 256)])
            g["fi%d" % i] = _kgroup(wfi[:, cols], 0, 512)
        for i in range(8):
            g["fo%d" % i] = _kgroup(inp["w_ffn_out"][l], i * 128, 128)
        allg += [g[n] for n in GROUPS]
    return np.ascontiguousarray(np.stack(allg).reshape(-1, GSZ))


def layout_small(inp, layers):
    NL = len(layers)
    ident = np.eye(128, dtype=np.float32)
    k = np.arange(128)
    tri = (k[:, None] <= k[None, :]).astype(np.float32)
    um = (k[:, None] > k[None, :]).astype(np.float32)
    ones = np.ones((128, 128), np.float32)
    cst = np.concatenate([ident, tri, um, ones, ones / 1024.0], axis=1)
    pcs, prs, wdts, sgws = [], [], [], []
    for l in layers:
        cols = []
        cols.append(inp["ln_g"][l].reshape(3, 8, 128).transpose(2, 0, 1).reshape(128, 24))
        cols.append(inp["ln_b"][l].reshape(3, 8, 128).transpose(2, 0, 1).reshape(128, 24))
        cols.append(inp["sg_ln_g"][l].reshape(8, 128).T)
        cols.append(inp["sg_ln_b"][l].reshape(8, 128).T)
        cols.append(inp["conv_w"][l].reshape(4, 24, 128).transpose(2, 1, 0).reshape(128, 96))
        cols.append(inp["conv_b"][l].reshape(24, 128).T)
        cols.append(np.repeat(inp["d_skip"][l], 64).reshape(16, 128).T)
        cols.append(inp["ssm_norm_g"][l].reshape(16, 128).T)
        pcs.append(np.concatenate(cols, axis=1))
        prs.append(np.concatenate([inp["dt_bias"][l], inp["a_log"][l], inp["sg_b"][l].reshape(-1)]))
        wdts.append(inp["w_in"][l][:, 7168:7200].reshape(8, 128, 32).transpose(1, 0, 2).reshape(128, 256))
        sgws.append(inp["sg_w"][l].transpose(2, 0, 1).reshape(128, 1024))
    pcol = np.concatenate(pcs, axis=1).astype(np.float32)
    prow = np.concatenate(prs + [inp["mem_ln_g"], inp["mem_ln_b"]])[None, :].astype(np.float32)
    return {"cst": np.ascontiguousarray(cst), "pcol": np.ascontiguousarray(pcol), "prow": np.ascontiguousarray(prow),
            "wdt": np.ascontiguousarray(np.concatenate(wdts, axis=1)), "sgw": np.ascontiguousarray(np.concatenate(sgws, axis=1))}


def layout_x(xs, nt, T):
    nseq = xs.shape[0]
    a = xs.reshape(nseq, nt, T, 8, 128).transpose(0, 1, 4, 3, 2)
    return np.ascontiguousarray(a.reshape(nseq * nt * 128, 8 * T))


def unlayout_x(y, nseq, nt, T):
    a = y.reshape(nseq, nt, 128, 8, T).transpose(0, 1, 4, 3, 2)
    return np.ascontiguousarray(a.reshape(nseq, nt * T, D))


def kernel(**inputs):
    inp = {k: np.asarray(v) for k, v in inputs.items()}
    cfg = Cfg(nseq=2, nt=SEQ // 256, nch=2, layers=(0, 1))
    nc = build(cfg)
    wall = layout_weights(inp, cfg.layers)
    small = layout_small(inp, cfg.layers)
    in_maps = []
    for c in range(NCORES):
        m = {"xT": layout_x(inp["x"][2 * c:2 * c + 2], cfg.nt, cfg.T),
             "mem": np.ascontiguousarray(inp["mem"][2 * c:2 * c + 2].reshape(-1, D)), "wall": wall}
        m.update(small)
        in_maps.append(m)
    res = run_bass_kernel_spmd(nc, in_maps, core_ids=list(range(NCORES)))
    out = np.concatenate([unlayout_x(r["yT"], 2, cfg.nt, cfg.T) for r in res.results], axis=0)
    return out.astype(np.float32)
```

```python
import contextlib
import numpy as np
import concourse.bass as bass
import concourse.mybir as mybir
from concourse.bass_utils import run_bass_kernel_spmd

F32 = mybir.dt.float32
BF16 = mybir.dt.bfloat16
AF = mybir.ActivationFunctionType
ALU = mybir.AluOpType

D = 1024
SEQ = 2048
BATCH = 16
DEPTH = 2
MEM = 256
NCORES = 8
ALPHA = float((2 * DEPTH) ** 0.25)
LN_EPS = 1e-5
RMS_EPS = 1e-5
FFH = 2816
NKF = FFH // 128
GSZ = 4096

GROUPS = (["kv%d" % i for i in range(4)] + ["v0", "v1", "u0", "u1", "ga0", "pa0", "ga1", "pa1"]
          + ["xbc%d" % i for i in range(6)] + ["z%d" % i for i in range(4)]
          + ["gb0", "pb0", "pb1", "gb1", "pb2", "pb3", "mo0", "mo1", "xq0", "xq1", "xo0", "xo1"]
          + ["fi%d" % i for i in range(11)] + ["fo%d" % i for i in range(8)])
GIDX = {n: i for i, n in enumerate(GROUPS)}
NG = len(GROUPS)

ENGS = ("pe", "act", "dve", "pool", "sp")


class Op:
    __slots__ = ("eng", "fn", "deps", "marked", "count", "is_dma", "chan", "chan_val", "cost", "idx", "fin", "ndep", "succ", "rdy")

    def __init__(self, eng, fn, is_dma):
        self.cost = 300.0
        self.idx = 0
        self.eng = eng
        self.fn = fn
        self.deps = {}
        self.marked = False
        self.count = 0
        self.is_dma = is_dma
        self.chan = None
        self.chan_val = 0


class Prog:
    def __init__(self, nc):
        self.nc = nc
        self.ops = {e: [] for e in ENGS}
        self.last_writer = {}
        self.readers = {}
        self.chan_count = {}
        self.chan_last = {}
        self.bulk = set()
        self.nops = 0

    def add(self, eng, fn, reads=(), writes=(), dma=None, cost=300.0):
        op = Op(eng, fn, dma is not None)
        op.cost = cost
        op.idx = self.nops
        self.nops += 1
        for k in reads:
            w = self.last_writer.get(k)
            if w is not None:
                op.deps[w] = True
        for k in writes:
            w = self.last_writer.get(k)
            if w is not None and w not in op.deps:
                op.deps[w] = False
            for r in self.readers.get(k, ()):
                if r is not op and r not in op.deps:
                    op.deps[r] = False
        for k in reads:
            self.readers.setdefault(k, []).append(op)
        for k in writes:
            self.last_writer[k] = op
            self.readers[k] = []
        if dma is not None:
            op.chan = dma
            self.chan_count[dma] = self.chan_count.get(dma, 0) + 16
            op.chan_val = self.chan_count[dma]
            if dma not in self.bulk:
                pl = self.chan_last.get(dma)
                if pl is not None:
                    op.deps[pl] = True
                self.chan_last[dma] = op
        for p, raw in op.deps.items():
            if (not p.is_dma) and self._needs_wait(op, p, raw):
                p.marked = True
        self.ops[eng].append(op)
        return op

    @staticmethod
    def _needs_wait(c, p, raw):
        if p.is_dma:
            return True
        if p.eng != c.eng:
            return True
        if c.is_dma:
            return True
        if c.eng == "pe":
            return False
        return True

    def emit(self, e, eng, sems, chan_sems):
        c = 0
        for op in self.ops[e]:
            if op.marked and not op.is_dma:
                c += 1
            op.count = c
        waited = {}
        for op in self.ops[e]:
            need = {}
            for p, raw in op.deps.items():
                if not self._needs_wait(op, p, raw):
                    continue
                if p.is_dma:
                    s, v = ("c", p.chan), (self.chan_count[p.chan] if p.chan in self.bulk else p.chan_val)
                else:
                    s, v = ("e", p.eng), p.count
                if v > need.get(s, 0):
                    need[s] = v
            for s, v in need.items():
                if waited.get(s, 0) >= v:
                    continue
                waited[s] = v
                sem = chan_sems[s[1]] if s[0] == "c" else sems[s[1]]
                eng.wait_ge(sem, v)
            ins = op.fn(eng)
            if op.is_dma:
                ins.then_inc(chan_sems[op.chan], 16)
            elif op.marked:
                ins.then_inc(sems[e], 1)

    def schedule(self, window=1500):
        import heapq
        allops = [op for e in ENGS for op in self.ops[e]]
        for op in allops:
            op.succ = []
            op.rdy = 0.0
        for op in allops:
            op.ndep = len(op.deps)
            for p in op.deps:
                p.succ.append(op)
        tfree = {e: 0.0 for e in ENGS}
        h_rdy = {e: [] for e in ENGS}
        h_idx = {e: [] for e in ENGS}
        order = {e: [] for e in ENGS}
        for op in allops:
            if op.ndep == 0:
                heapq.heappush(h_rdy[op.eng], (0.0, op.idx, op))
        done = 0
        nall = len(allops)
        base = 0
        sched = bytearray(nall)
        while done < nall:
            best = None
            for e in ENGS:
                hr, hi = h_rdy[e], h_idx[e]
                while hr and hr[0][0] <= tfree[e]:
                    _, i_, o_ = heapq.heappop(hr)
                    heapq.heappush(hi, (i_, o_))
                if hi:
                    cand = (tfree[e], 0, e)
                elif hr:
                    cand = (hr[0][0], 1, e)
                else:
                    continue
                if best is None or cand < best:
                    best = cand
            t0, kind, e = best
            if kind == 0:
                _, op = heapq.heappop(h_idx[e])
            else:
                _, _, op = heapq.heappop(h_rdy[e])
            start = max(tfree[e], op.rdy)
            if op.is_dma:
                tfree[e] = start + 60.0
                op.fin = start + op.cost
            elif e == "pe":
                tfree[e] = start + op.cost
                op.fin = start + op.cost + 150.0
            else:
                tfree[e] = start + op.cost
                op.fin = start + op.cost + 60.0
            order[e].append(op)
            done += 1
            for sx in op.succ:
                lat = 0.0 if (sx.eng == e and e == "pe") else 80.0
                if op.fin + lat > sx.rdy:
                    sx.rdy = op.fin + lat
                sx.ndep -= 1
                if sx.ndep == 0:
                    heapq.heappush(h_rdy[sx.eng], (sx.rdy, sx.idx, sx))
        self.ops = order
        self.est_ns = max(tfree.values())

    def finalize(self):
        for e in ENGS:
            c = 0
            for op in self.ops[e]:
                if op.marked and not op.is_dma:
                    c += 1
                op.count = c


def bc(ap, shape):
    return ap.to_broadcast(shape) if hasattr(ap, "to_broadcast") else ap.broadcast_to(shape)


class Cfg:
    def __init__(self, nseq=2, nt=8, nch=2, layers=(0, 1), dbg=None, stages=("mixer", "xattn", "ffn"), sched=True):
        self.nseq, self.nt, self.nch, self.layers = nseq, nt, nch, tuple(layers)
        self.T = nch * 128
        self.dbg = dbg or {}
        self.stages = stages
        self.sched = sched


def build(cfg):
    nc = bass.Bass("TRN2", target_bir_lowering=False)
    P = Prog(nc)
    P.bulk = {"cst"}
    NSEQ, NT, NCH, T = cfg.nseq, cfg.nt, cfg.nch, cfg.T
    LAY = cfg.layers
    NL = len(LAY)
    BIG = NCH >= 4

    xin = nc.dram_tensor("xT", [NSEQ * NT * 128, 8 * T], F32, kind="ExternalInput").ap()
    yout = nc.dram_tensor("yT", [NSEQ * NT * 128, 8 * T], F32, kind="ExternalOutput").ap()
    memd = nc.dram_tensor("mem", [NSEQ * 2 * 128, D], F32, kind="ExternalInput").ap()
    wall = nc.dram_tensor("wall", [NL * NG * 128, GSZ], F32, kind="ExternalInput").ap()
    wbf = nc.dram_tensor("wbf", [NL * NG * 128, GSZ], BF16, kind="Internal").ap()
    cst = nc.dram_tensor("cst", [128, 5 * 128], F32, kind="ExternalInput").ap()
    NPC = 3 * 8 * 2 + 8 * 2 + 24 * 5 + 16 * 2
    pcol = nc.dram_tensor("pcol", [128, NL * NPC], F32, kind="ExternalInput").ap()
    NPR = 32 * 2 + 1024
    prow = nc.dram_tensor("prow", [1, NL * NPR + 2 * D], F32, kind="ExternalInput").ap()
    wdt_d = nc.dram_tensor("wdt", [128, NL * 8 * 32], F32, kind="ExternalInput").ap()
    sgw_d = nc.dram_tensor("sgw", [128, NL * 8 * 128], F32, kind="ExternalInput").ap()
    dbg_d = {}
    for name, shp in cfg.dbg.items():
        dbg_d[name] = nc.dram_tensor("dbg_" + name, list(shp), F32, kind="ExternalOutput").ap()

    st = contextlib.ExitStack()
    with st:
        def sb(name, shape, dt):
            return st.enter_context(nc.sbuf_tensor(name, list(shape), dt))

        c32 = sb("c32", [128, 5, 128], F32)
        cbf = sb("cbf", [128, 5, 128], BF16)
        IDENT, TRI, UM, ONES, ODIV = 0, 1, 2, 3, 4
        pc = sb("pc", [128, NL, NPC], F32)
        O_LNG, O_LNB, O_SGG, O_SGB, O_CW, O_CB, O_DC, O_NG = 0, 24, 48, 56, 64, 160, 184, 200
        pr = sb("pr", [128, NL, 64], F32)
        wdt = sb("wdt_s", [128, NL, 8, 32], F32)
        wTm = sb("wTm", [128, NL, 8, 128], BF16)
        Esg = sb("Esg", [128, NL, 8, 128], F32)
        Ddg = sb("Ddg", [128, 16, 128], BF16)
        KT = sb("KT", [128, NL, 8, MEM], BF16)
        Vt = sb("Vt", [128, NL, 2, D], BF16)
        NSLOT = 3 if BIG else 4
        wr = [sb("wr%d" % i, [128, GSZ], BF16) for i in range(NSLOT)]
        xts = [sb("xt0", [128, 8, T], F32)] if BIG else [sb("xt0", [128, 8, T], F32), sb("xt1", [128, 8, T], F32)]
        nxb = len(xts)
        xt = xts[0]
        KX = ["xt0"]
        xb = sb("xb", [128, 8, T], BF16)
        Sst = sb("Sst", [128, NL, 2048], F32)
        halo = sb("halo", [128, NL, 24, 3], F32)
        A1 = sb("A1", [128, 16, T], BF16)
        A2 = sb("A2", [128, 32, T], BF16)
        A4 = A1 if BIG else sb("A4", [128, 16, T], BF16)
        A3 = sb("A3", [128, 8, T], BF16)
        mg = sb("mg", [128, 8, T], BF16)
        W1 = sb("W1", [128, 2, 1540], F32)
        W2 = sb("W2", [128, 2, 2048], BF16)
        W3 = sb("W3", [128, 10, 512], BF16)
        sgt = W2[:, 0, :].rearrange("p (j t) -> p j t", t=T) if BIG else sb("sgt", [128, 4, T], BF16)
        sm = sb("sm", [128, 512], F32)
        dahl = sb("dahl", [128, 4, 2, 32], BF16)
        lnst = sb("lnst", [128, 2, T], F32)
        if BIG:
            Sbf = lnst[:].rearrange("p a t -> p (a t)").bitcast(BF16)
            KSBF = ["ln0", "ln1"]
        else:
            Sbf = sb("Sbf", [128, 2048], BF16)
            KSBF = ["Sbf"]
        memgb = A2[:].rearrange("p a t -> p (a t)").bitcast(F32)[:, 0:2 * D].rearrange("p (a d) -> p a d", d=D)

        def ksgt(j):
            return ("xdt", j // 2) if BIG else ("sgt", j)
        kW2a = [("xdt", 0), ("xdt", 1)]
        kW2b = [("xdt2", 0), ("xdt2", 1)]

        ps = [st.enter_context(nc.psum_tensor("ps%d" % i, [128, 512], F32)) for i in range(8)]

        sems = {e: st.enter_context(nc.semaphore("s_" + e)) for e in ENGS}
        chan_names = (["cv%d" % i for i in range(8)] + ["cst", "sgw", "x0", "x1", "y0", "y1", "mem", "dbg"] + ["w%d" % i for i in range(NSLOT)])
        chans = {c: st.enter_context(nc.semaphore("c_" + c)) for c in chan_names}

        def kW1(i, qs):
            return sorted(set(("W1", i, (0 if q == 1 else q)) for q in qs))

        def flat(ks):
            out = []
            for k in ks:
                if isinstance(k, list):
                    out.extend(flat(k))
                else:
                    out.append(k)
            return out

        def A(eng, fn, r=(), w=(), dma=None, cost=300.0):
            return P.add(eng, fn, reads=flat(r), writes=flat(w), dma=dma, cost=cost)

        KXB = [("xb", k_) for k_ in range(8)]

        def kxt():
            return [(KX[0], k_) for k_ in range(8)]

        def fsz(ap):
            n = 1
            for d in ap.shape[1:]:
                n *= d
            return n

        ECOST = {"act": 0.75, "dve": 1.0, "pool": 2.0}

        def ecost(eng, out):
            return 220.0 + ECOST[eng] * fsz(out)

        def act(out, in_, func, r, w, **kw):
            A("act", lambda e: e.activation(out=out, in_=in_, func=func, **kw), r, w, cost=ecost("act", out) + (90.0 if kw else 0.0))

        def tt(eng, out, in0, in1, op, r, w):
            A(eng, lambda e: e.tensor_tensor(out=out, in0=in0, in1=in1, op=op), r, w, cost=ecost(eng, out))

        def ts(eng, out, in0, s1, s2, op0, op1, r, w):
            c_ = ecost(eng, out) if eng != "pool" else 3500.0
            if op1 is None:
                A(eng, lambda e: e.tensor_scalar(out=out, in0=in0, scalar1=s1, scalar2=None, op0=op0), r, w, cost=c_)
            else:
                A(eng, lambda e: e.tensor_scalar(out=out, in0=in0, scalar1=s1, scalar2=s2, op0=op0, op1=op1), r, w, cost=c_)

        def stt(out, in0, scalar, in1, op0, op1, r, w):
            A("dve", lambda e: e.scalar_tensor_tensor(out=out, in0=in0, scalar=scalar, in1=in1, op0=op0, op1=op1), r, w,
              cost=ecost("dve", out))

        def cp(eng, out, in_, r, w):
            if eng == "act":
                A("act", lambda e: e.copy(out=out, in_=in_), r, w, cost=ecost("act", out))
            else:
                A(eng, lambda e: e.tensor_copy(out=out, in_=in_), r, w, cost=ecost(eng, out))

        def mm(out, lhsT, rhs, start, stop, r, w):
            n_ = fsz(rhs)
            c_ = max(n_, 64) / 1.9 + 8.0
            if rhs.dtype == F32:
                c_ *= 4.0
            A("pe", lambda e: e.matmul(out, lhsT=lhsT, rhs=rhs, start=start, stop=stop), r, w, cost=c_)

        def tp(out, in_, ident, r, w):
            A("pe", lambda e: e.transpose(out, in_, ident), r, w, cost=90.0)

        def dbg(name, src_ap, key, row0=0):
            if name in dbg_d:
                d = dbg_d[name]
                n = src_ap.shape[0]
                A("pool", lambda e: e.dma_start(out=d[row0:row0 + n], in_=src_ap), r=key, w=[("dbg", name, row0)], dma="dbg")

        wstate = {"n": 0}

        def wload(li, gname):
            slot = wstate["n"] % NSLOT
            wstate["n"] += 1
            g = li * NG + GIDX[gname]
            A("sp", lambda e: e.dma_start(out=wr[slot][:], in_=wbf[g * 128:(g + 1) * 128, :]),
              r=[("wbf", g)], w=[("wr", slot)], dma="w%d" % slot, cost=7000.0)
            if li + 1 < NL:
                cast((li + 1) * NG + GIDX[gname], extra_reads=[("wr", slot)])
            return wr[slot], ("wr", slot)

        def w3(wt_, ncol):
            return wt_[:].rearrange("p (k c) -> p k c", c=ncol)

        ncast = {"n": 0}
        cast_done = set()

        def cast(g, extra_reads=()):
            if g in cast_done:
                return
            cast_done.add(g)
            i_ = ncast["n"]
            ncast["n"] += 1
            A("pool", lambda e: e.dma_start(out=wbf[g * 128:(g + 1) * 128, :], in_=wall[g * 128:(g + 1) * 128, :],
                                            max_dma_last_dim=8192), r=list(extra_reads), w=[("wbf", g)], dma="cv%d" % (i_ % 8), cost=12000.0)

        for i in range(NG):
            cast(i)
        A("sp", lambda e: e.dma_start(out=c32[:].rearrange("p a b -> p (a b)"), in_=cst), w=["c32"], dma="cst")
        A("sp", lambda e: e.dma_start(out=pc[:].rearrange("p a b -> p (a b)"), in_=pcol), w=["pc"], dma="cst")
        for li_ in range(NL):
            A("sp", lambda e, li_=li_: e.dma_start(out=pr[:, li_, :], in_=prow[:, li_ * NPR:li_ * NPR + 64].partition_broadcast(128)),
              w=(["pr"] if li_ == NL - 1 else [("prx", li_)]), dma="cst")
        A("sp", lambda e: e.dma_start(out=wdt[:].rearrange("p a b c -> p (a b c)"), in_=wdt_d), w=["wdt"], dma="cst")
        cp("dve", cbf[:], c32[:], ["c32"], ["cbf"])
        for li in range(NL):
            act(pr[:, li, 32:64], pr[:, li, 32:64], AF.Exp, ["pr"], ["pr"])
            ts("dve", pr[:, li, 32:64], pr[:, li, 32:64], -1.0, None, ALU.mult, None, ["pr"], ["pr"])
        sgbt = A2[:].rearrange("p a t -> p (a t)").bitcast(F32)[:, 0:NL * 1024].rearrange("p (l f) -> p l f", f=1024)
        for li_ in range(NL):
            A("sp", lambda e, li_=li_: e.dma_start(out=sgbt[:, li_, :],
                                                   in_=prow[:, li_ * NPR + 64:(li_ + 1) * NPR].partition_broadcast(128)),
              w=["sgbt"] + [("A2", i_) for i_ in range(32)], dma="sgw")
        for li in range(NL):
            w1v = W1[:, 0, 0:1024].rearrange("p (g t) -> p g t", t=128)
            A("sp", lambda e, li=li: e.dma_start(out=W1[:, 0, 0:1024], in_=sgw_d[:, li * 1024:(li + 1) * 1024]),
              r=[], w=[*kW1(0, (0, 2))], dma="sgw")
            tt("dve", w1v, w1v, bc(c32[:, TRI:TRI + 1, :], [128, 8, 128]), ALU.mult, [*kW1(0, (0, 2)), "c32"], [*kW1(0, (0, 2))])
            cp("pool", wTm[:, li], w1v, [*kW1(0, (0, 2))], [("wTm", li)])
            for hh in range(2):
                mm(ps[hh][:], c32[:, ONES, :], W1[:, 0, hh * 512:(hh + 1) * 512], True, True, [*kW1(0, (0, 2)), "c32"], [("ps", hh)])
                e_v = Esg[:, li, hh * 4:(hh + 1) * 4, :]
                tt("dve", e_v, ps[hh][:].rearrange("p (g t) -> p g t", t=128),
                   bc(pc[:, li, O_SGB + hh * 4:O_SGB + hh * 4 + 4].unsqueeze(2), [128, 4, 128]), ALU.mult,
                   [("ps", hh), "pc"], [("Esg", li)])
                tt("pool", e_v, e_v, sgbt[:, li, hh * 512:(hh + 1) * 512].rearrange("p (g t) -> p g t", t=128),
                   ALU.add, [("Esg", li), "sgbt"] + [("A2", i_) for i_ in range(32)], [("Esg", li)])
        dbg("Esg", Esg[:, 0].rearrange("p g t -> p (g t)"), [("Esg", 0)])

        eps_ln = LN_EPS
        psrot = {"n": 0}

        def nextps():
            b = psrot["n"] % 4
            psrot["n"] += 1
            return b

        def layer_norm(li, which):
            rbf = A1[:, 0:8, :]
            rsq = A1[:, 8:16, :]
            kA2 = [("A1", i) for i in range(16)]
            cp("dve", rbf, xt[:], kxt(), kA2[0:8])
            act(rsq, xt[:], AF.Square, kxt(), kA2[8:16])
            for kc in range(8):
                mm(ps[4][:, 0:T], cbf[:, ODIV, :], rbf[:, kc, :], kc == 0, kc == 7, kA2[0:8] + ["cbf"], [("ps", 4)])
            for kc in range(8):
                mm(ps[5][:, 0:T], cbf[:, ODIV, :], rsq[:, kc, :], kc == 0, kc == 7, kA2[8:16] + ["cbf"], [("ps", 5)])
            mean, var, rstd = lnst[:, 0, :], lnst[:, 1, :], lnst[:, 1, :]
            cp("act", mean, ps[4][:, 0:T], [("ps", 4)], ["ln0"])
            act(var, ps[4][:, 0:T], AF.Square, [("ps", 4)], ["ln1"])
            tt("dve", var, ps[5][:, 0:T], var, ALU.subtract, [("ps", 5), "ln1"], ["ln1"])
            ts("dve", var, var, eps_ln, None, ALU.add, None, ["ln1"], ["ln1"])
            act(rstd, var, AF.Ln, ["ln1"], ["ln1"])
            act(rstd, rstd, AF.Exp, ["ln1"], ["ln1"], scale=-0.5)
            for kc in range(8):
                o = O_LNG + which * 8 + kc
                ob = O_LNB + which * 8 + kc
                kk = (KX[0], kc)
                tt("dve", xt[:, kc, :], xt[:, kc, :], mean, ALU.subtract, [kk, "ln0"], [kk])
                stt(xt[:, kc, :], xt[:, kc, :], pc[:, li, o:o + 1], rstd, ALU.mult, ALU.mult, [kk, "ln1", "pc"], [kk])
                act(xb[:, kc, :], xt[:, kc, :], AF.Identity, [kk, "pc"], [("xb", kc)], bias=pc[:, li, ob:ob + 1])
                ts("dve", xt[:, kc, :], xt[:, kc, :], pc[:, li, ob:ob + 1], None, ALU.add, None, [kk, "pc"], [kk])

        def mem_phase(s):
            mraw = W1[:].rearrange("p a b -> p (a b)")[:, 0:2048].rearrange("p (c d) -> p c d", d=D)
            kmgb = [("A2", i_) for i_ in range(32)]
            A("pool", lambda e: e.dma_start(out=memgb.rearrange("p a b -> p (a b)"),
                                            in_=prow[:, NL * NPR:NL * NPR + 2 * D].partition_broadcast(128)), r=[], w=kmgb, dma="mem")
            kW1m = kW1(0, (0, 2, 3)) + kW1(1, (0,))
            A("pool", lambda e: e.dma_start(out=mraw, in_=memd[s * 256:(s + 1) * 256, :].rearrange("(c p) d -> p c d", p=128)),
              r=[], w=kW1m, dma="mem")
            st6 = sm[:, 0:24].rearrange("p (c h k) -> p c h k", c=2, h=2)
            mv = sm[:, 24:28].rearrange("p (c k) -> p c k", k=2)
            for c in range(2):
                for h in range(2):
                    A("dve", lambda e, c=c, h=h: e.bn_stats(out=st6[:, c, h, :], in_=mraw[:, c, h * 512:(h + 1) * 512]), kW1m, ["sm"])
                A("dve", lambda e, c=c: e.bn_aggr(out=mv[:, c, :], in_=st6[:, c].rearrange("p h k -> p (h k)")), ["sm"], ["sm"])
            rs = sm[:, 28:30]
            ts("dve", rs, mv[:, :, 1], LN_EPS, None, ALU.add, None, ["sm"], ["sm"])
            act(rs, rs, AF.Ln, ["sm"], ["sm"])
            act(rs, rs, AF.Exp, ["sm"], ["sm"], scale=-0.5)
            mnb = W2[:, 0, :].rearrange("p (c d) -> p c d", d=D)
            for c in range(2):
                ts("dve", mraw[:, c, :], mraw[:, c, :], mv[:, c, 0:1], rs[:, c:c + 1], ALU.subtract, ALU.mult, kW1m + ["sm"], kW1m)
                tt("pool", mraw[:, c, :], mraw[:, c, :], memgb[:, 0, :], ALU.mult, kW1m + kmgb, kW1m)
                tt("dve", mnb[:, c, :], mraw[:, c, :], memgb[:, 1, :], ALU.add, kW1m + kmgb, kW2a)
            memT = W2[:, 1, :].rearrange("p (k m) -> p k m", m=MEM)
            for c in range(2):
                pb16 = ps[4 + c][:].bitcast(BF16)
                for kc in range(8):
                    tp(pb16[:, kc * 128:(kc + 1) * 128], mnb[:, c, kc * 128:(kc + 1) * 128], cbf[:, IDENT, :],
                       kW2a + ["cbf"], [("ps", 4 + c)])
                cp("act", memT[:, :, c * 128:(c + 1) * 128], pb16.rearrange("p (k m) -> p k m", m=128),
                   [("ps", 4 + c)], kW2b)
            for li in range(NL):
                for gi in range(2):
                    wt_, wk = wload(li, "kv%d" % gi)
                    wv = w3(wt_, 512)
                    for j in range(4):
                        b = nextps()
                        for kc in range(8):
                            mm(ps[b][:, 0:MEM], wv[:, kc, j * 128:(j + 1) * 128], memT[:, kc, :], kc == 0, kc == 7,
                               [wk] + kW2b, [("ps", b)])
                        cp("act", KT[:, li, gi * 4 + j, :], ps[b][:, 0:MEM], [("ps", b)], [("KT", li)])
                for gi in range(2):
                    wt_, wk = wload(li, "kv%d" % (2 + gi))
                    wv = w3(wt_, 512)
                    for c in range(2):
                        b = nextps()
                        for kc in range(8):
                            mm(ps[b][:], memT[:, kc, c * 128:(c + 1) * 128], wv[:, kc, :], kc == 0, kc == 7,
                               [wk] + kW2b, [("ps", b)])
                        cp("dve", Vt[:, li, c, gi * 512:(gi + 1) * 512], ps[b][:], [("ps", b)], [("Vt", li)])

        def proj_fm(li, gname, rhs3, rkeys, nk, ncol, nblk, evac):
            wt_, wk = wload(li, gname)
            wv = w3(wt_, ncol)
            for j in range(nblk):
                b = nextps()
                for kc in range(nk):
                    mm(ps[b][:, 0:T], wv[:, kc, j * 128:(j + 1) * 128], rhs3[:, kc, :], kc == 0, kc == nk - 1,
                       [wk] + rkeys, [("ps", b)])
                evac(j, ps[b][:, 0:T], ("ps", b))

        kA1 = [("A1", i) for i in range(16)]
        kA2 = [("A2", i) for i in range(32)]
        kA3 = [("A3", i) for i in range(8)]
        kA4 = kA1 if BIG else [("A4", i) for i in range(16)]
        kmg = [("mg", i) for i in range(8)]

        def mixer(li, dbgrow):
            def sg_branch():
                uT = A4[:, 0:8, :]
                vn = A4[:, 8:16, :].rearrange("p a t -> p (a t)").rearrange("p (c f) -> p c f", f=1024)
                st6 = sm[:, 0:NCH * 12].rearrange("p (c h k) -> p c h k", c=NCH, h=2)
                mv = sm[:, 48:48 + NCH * 2].rearrange("p (c k) -> p c k", k=2)
                rs = sm[:, 64:64 + NCH]
                for gi in range(2):
                    wt_, wk = wload(li, "v%d" % gi)
                    wv = w3(wt_, 512)
                    for c in range(NCH):
                        b = nextps()
                        for kc in range(8):
                            mm(ps[b][:], xb[:, kc, c * 128:(c + 1) * 128], wv[:, kc, :], kc == 0, kc == 7, [wk, KXB], [("ps", b)])
                        act(vn[:, c, gi * 512:(gi + 1) * 512], ps[b][:], AF.Gelu, [("ps", b)], kA4[8:16])
                        A("dve", lambda e, c=c, gi=gi: e.bn_stats(out=st6[:, c, gi, :], in_=vn[:, c, gi * 512:(gi + 1) * 512]),
                          kA4[8:16], ["sm"])
                for c in range(NCH):
                    A("dve", lambda e, c=c: e.bn_aggr(out=mv[:, c, :], in_=st6[:, c].rearrange("p h k -> p (h k)")), ["sm"], ["sm"])
                ts("dve", rs, mv[:, :, 1], LN_EPS, None, ALU.add, None, ["sm"], ["sm"])
                act(rs, rs, AF.Ln, ["sm"], ["sm"])
                act(rs, rs, AF.Exp, ["sm"], ["sm"], scale=-0.5)
                nmr = sm[:, 72:72 + NCH]
                tt("dve", nmr, mv[:, :, 0], rs, ALU.mult, ["sm"], ["sm"])
                ts("dve", nmr, nmr, -1.0, None, ALU.mult, None, ["sm"], ["sm"])
                for c in range(NCH):
                    act(vn[:, c, :], vn[:, c, :], AF.Identity, kA4[8:16] + ["sm"], kA4[8:16], scale=rs[:, c:c + 1], bias=nmr[:, c:c + 1])
                for gi in range(2):
                    proj_fm(li, "u%d" % gi, xb, [KXB], 8, 512, 4,
                            lambda j, p_, pk, gi=gi: act(uT[:, gi * 4 + j, :], p_, AF.Gelu, [pk], [kA4[gi * 4 + j]]))
                for c in range(NCH):
                    for g in range(8):
                        b = 6 + g // 4
                        mm(ps[b][:, (g % 4) * 128:(g % 4 + 1) * 128], vn[:, c, g * 128:(g + 1) * 128], wTm[:, li, g, :],
                           True, True, kA4[8:16] + [("wTm", li)], [("ps", b)])
                    t1 = W1[:, 0, 0:1024].rearrange("p (g t) -> p g t", t=128)
                    for hh in range(2):
                        tt("dve", t1[:, hh * 4:(hh + 1) * 4, :], ps[6 + hh][:].rearrange("p (g t) -> p g t", t=128),
                           bc(pc[:, li, O_SGG + hh * 4:O_SGG + hh * 4 + 4].unsqueeze(2), [128, 4, 128]), ALU.mult,
                           [("ps", 6 + hh), "pc"], [*kW1(0, (0, 2))])
                    tt("pool", t1, t1, Esg[:, li], ALU.add, [*kW1(0, (0, 2)), ("Esg", li)], [*kW1(0, (0, 2))])
                    tt("dve", uT[:, :, c * 128:(c + 1) * 128], t1, uT[:, :, c * 128:(c + 1) * 128], ALU.mult,
                       [*kW1(0, (0, 2))] + kA4[0:8], kA4[0:8])
                dbg("saT", uT.rearrange("p a t -> p (a t)"), kA4[0:8], row0=dbgrow)
                for gi in range(2):
                    proj_fm(li, "ga%d" % gi, xb, [KXB], 8, 512, 4,
                            lambda j, p_, pk: act(sgt[:, j, :], p_, AF.Sigmoid, [pk], [ksgt(j)]))
                    proj_fm(li, "pa%d" % gi, uT, kA4[0:8], 8, 512, 4,
                            lambda j, p_, pk, gi=gi: tt("dve", mg[:, gi * 4 + j, :], p_, sgt[:, j, :], ALU.mult,
                                                        [pk, ksgt(j)], [kmg[gi * 4 + j]]))
                dbg("m1", mg[:].rearrange("p a t -> p (a t)"), kmg, row0=dbgrow)


            if BIG:
                sg_branch()
            for kc in range(16):
                act(Ddg[:, kc, :], cbf[:, IDENT, :], AF.Identity, ["cbf", "pc"], [("Ddg",)], scale=pc[:, li, O_DC + kc:O_DC + kc + 1])
            xsT = A2[:, 16:32, :]
            BCT = A3
            ztm = A2[:, 0:16, :].rearrange("p a t -> p (a t)").rearrange("p (c f) -> p c f", f=2048)
            kxs = kA2[16:32]
            kz = kA2[0:16]
            for gi in range(6):
                def ev(j, p_, pk, gi=gi):
                    blk = gi * 4 + j
                    i = blk % 2
                    raw = W1[:, i, 0:T + 3]
                    acc = W1[:, i, 516:516 + T]
                    kr, ka = kW1(i, (0, 1)), kW1(i, (2,))
                    cp("pool", raw[:, 0:3], halo[:, li, blk, :], [("halo", li, blk)], kr)
                    act(raw[:, 3:3 + T], p_, AF.Identity, [pk], kr)
                    cp("pool", halo[:, li, blk, :], raw[:, T:T + 3], kr, [("halo", li, blk)])
                    ow = O_CW + blk * 4
                    act(acc, raw[:, 0:T], AF.Identity, kr + ["pc"], ka, scale=pc[:, li, ow:ow + 1],
                        bias=pc[:, li, O_CB + blk:O_CB + blk + 1])
                    for k in range(1, 4):
                        stt(acc, raw[:, k:k + T], pc[:, li, ow + k:ow + k + 1], acc, ALU.mult, ALU.add, kr + ka + ["pc"], ka)
                    if blk < 16:
                        act(xsT[:, blk, :], acc, AF.Silu, ka, [kxs[blk]])
                    else:
                        act(BCT[:, blk - 16, :], acc, AF.Silu, ka, [kA3[blk - 16]])
                proj_fm(li, "xbc%d" % gi, xb, [KXB], 8, 512, 4, ev)
            for gi in range(4):
                wt_, wk = wload(li, "z%d" % gi)
                wv = w3(wt_, 512)
                for c in range(NCH):
                    b = nextps()
                    for kc in range(8):
                        mm(ps[b][:], xb[:, kc, c * 128:(c + 1) * 128], wv[:, kc, :], kc == 0, kc == 7, [wk, KXB], [("ps", b)])
                    act(ztm[:, c, gi * 512:(gi + 1) * 512], ps[b][:], AF.Silu, [("ps", b)], kz)
            for c in range(NCH):
                dtc = sm[:, 96 + c * 32:128 + c * 32]
                dac = sm[:, 224 + c * 32:256 + c * 32]
                for kc in range(8):
                    mm(ps[5][:, 0:32], xt[:, kc, c * 128:(c + 1) * 128], wdt[:, li, kc, :], kc == 0, kc == 7,
                       [kxt(), "wdt"], [("ps", 5)])
                tt("dve", dtc, ps[5][:, 0:32], pr[:, li, 0:32], ALU.add, [("ps", 5), "pr"], [("dt", c)])
                act(dtc, dtc, AF.Exp, [("dt", c)], [("dt", c)])
                act(dtc, dtc, AF.Ln, [("dt", c)], [("dt", c)], bias=1.0)
                tt("pool", dac, dtc, pr[:, li, 32:64], ALU.mult, [("dt", c), "pr"], [("da", c)])
                dhl = dahl[:, c]
                cp("dve", dhl[:, 0, :], dac, [("da", c)], [("dahl", c)])
                tt("dve", dhl[:, 1, :], dac, dhl[:, 0, :], ALU.subtract, [("da", c), ("dahl", c)], [("dahl", c)])
            cp("act", Sbf[:], Sst[:, li, :], [("Sst", li)], KSBF)
            yT = A1[:, 0:16, :]
            xdt = W2[:, 0, :]
            xdt2 = W2[:, 1, :]
            dec = [W3[:, 0, :], W3[:, 1, :], W3[:, 2, :], W3[:, 3, :]]
            CBm = W3[:, 4, :]
            Btm = W3[:, 5, :]
            yn = W3[:, 6:10, :].rearrange("p a b -> p (a b)")
            ex = sm[:, 352:448]
            ecs, cdb, dst = ex[:, 0:32], ex[:, 32:64], ex[:, 64:96]
            dtds = sm[:, 448:480]
            ssq = sm[:, 480:484]
            rsg = sm[:, 484:488]
            for c in range(NCH):
                cs_ = slice(c * 128, (c + 1) * 128)
                dtc = sm[:, 96 + c * 32:128 + c * 32]
                dac = sm[:, 224 + c * 32:256 + c * 32]
                mm(ps[5][:, 0:32], c32[:, TRI, :], dac, True, True, ["c32", ("da", c)], [("ps", 5)])
                mm(ps[5][:, 32:64], c32[:, ONES, :], dac, True, True, ["c32", ("da", c)], [("ps", 5)])
                mm(ps[5][:, 64:96], c32[:, UM, :], dac, True, True, ["c32", ("da", c)], [("ps", 5)])
                act(ex, ps[5][:, 0:96], AF.Exp, [("ps", 5)], ["ex"])
                tt("dve", dtds, dtc, dst, ALU.mult, [("dt", c), "ex"], ["dtds"])
                pb16 = ps[4][:].bitcast(BF16)
                for rnd in range(2):
                    for j in range(8):
                        blk = rnd * 8 + j
                        tp(pb16[:, j * 128:(j + 1) * 128], xsT[:, blk, cs_], cbf[:, IDENT, :], [kxs[blk], "cbf"], [("ps", 4)])
                    hs = slice(rnd * 16, (rnd + 1) * 16)
                    o1 = xdt[:, rnd * 1024:(rnd + 1) * 1024].rearrange("p (h d) -> p h d", d=64)
                    o2 = xdt2[:, rnd * 1024:(rnd + 1) * 1024].rearrange("p (h d) -> p h d", d=64)
                    tt("dve", o1, pb16.rearrange("p (h d) -> p h d", d=64), bc(dtc[:, hs].unsqueeze(2), [128, 16, 64]),
                       ALU.mult, [("ps", 4), ("dt", c)], [("xdt", rnd)])
                    tt("pool", o2, o1, bc(dst[:, hs].unsqueeze(2), [128, 16, 64]), ALU.mult, [("xdt", rnd), "ex"], [("xdt2", rnd)])
                pB = ps[5][:, 256:512].bitcast(BF16)
                for g in range(4):
                    tp(pB[:, g * 128:(g + 1) * 128], BCT[:, g, cs_], cbf[:, IDENT, :], [kA3[g], "cbf"], [("ps", 5)])
                cp("act", Btm, pB, [("ps", 5)], ["Btm"])
                for g in range(4):
                    mm(ps[2][:, g * 128:(g + 1) * 128], BCT[:, g, cs_], BCT[:, 4 + g, cs_], True, True,
                       [kA3[g], kA3[4 + g]], [("ps", 2)])
                tt("dve", CBm.rearrange("p (g l) -> p g l", l=128), ps[2][:].rearrange("p (g l) -> p g l", l=128),
                   bc(c32[:, TRI:TRI + 1, :], [128, 4, 128]), ALU.mult, [("ps", 2), "c32"], ["CBm"])
                def stageA(g):
                    for hh in range(2):
                        h0 = g * 8 + hh * 4
                        di = (g % 2) * 2 + hh
                        rr = W1[:, hh, 516:1028].bitcast(BF16).rearrange("p (a r l) -> p a r l", a=2, l=128)
                        for a_ in range(2):
                            tt("pool", rr[:, a_], bc(cbf[:, TRI:TRI + 1, :], [128, 4, 128]),
                               bc(dahl[:, c, a_, h0:h0 + 4].unsqueeze(2), [128, 4, 128]), ALU.mult, ["cbf", ("dahl", c)], kW1(hh, (2,)))
                        for a_ in range(2):
                            mm(ps[hh][:], cbf[:, UM, :], rr[:, a_].rearrange("p r l -> p (r l)"), a_ == 0, a_ == 1,
                               kW1(hh, (2,)) + ["cbf"], [("ps", hh)])
                        d3 = dec[di].rearrange("p (r l) -> p r l", l=128)
                        act(dec[di], ps[hh][:], AF.Exp, [("ps", hh)], [("dec", di)])
                        tt("dve", d3, d3, bc(CBm[:, g * 128:(g + 1) * 128].unsqueeze(1), [128, 4, 128]), ALU.mult,
                           [("dec", di), "CBm"], [("dec", di)])

                stageA(0)
                for g in range(4):
                    if g + 1 < 4:
                        stageA(g + 1)
                    dg = (g % 2) * 2
                    bpy, bpo, bst = (3, 6, 7) if g % 2 == 0 else (2, 5, 4)
                    for i in range(4):
                        kc = g * 4 + i
                        mm(ps[bpy][:, i * 128:(i + 1) * 128], xsT[:, kc, cs_], Ddg[:, kc, :], i == 0, False,
                           [kxs[kc], ("Ddg",)], [("ps", bpy)])
                    for r in range(8):
                        h = g * 8 + r
                        mm(ps[bpy][:, r * 64:(r + 1) * 64], dec[dg + r // 4][:, (r % 4) * 128:(r % 4 + 1) * 128], xdt[:, h * 64:(h + 1) * 64],
                           False, r == 7, [("dec", dg + r // 4), ("xdt", h // 16)], [("ps", bpy)])
                    mm(ps[bpo][:], BCT[:, 4 + g, cs_], Sbf[:, g * 512:(g + 1) * 512], True, True, [kA3[4 + g]] + KSBF, [("ps", bpo)])
                    i2 = g % 2
                    yw = W1[:, i2, 1028:1540]
                    yg = W1[:, i2, 0:512]
                    junk = yw
                    tt("dve", yw.rearrange("p (h d) -> p h d", d=64), ps[bpo][:].rearrange("p (h d) -> p h d", d=64),
                       bc(ecs[:, g * 8:(g + 1) * 8].unsqueeze(2), [128, 8, 64]), ALU.mult, [("ps", bpo), "ex"], kW1(i2, (3,)))
                    tt("dve", yw, yw, ps[bpy][:], ALU.add, kW1(i2, (3,)) + [("ps", bpy)], kW1(i2, (3,)))
                    tt("pool", yg, yw, ztm[:, c, g * 512:(g + 1) * 512], ALU.mult, kW1(i2, (3,)) + kz, kW1(i2, (0,)))
                    A("act", lambda e, yg=yg, junk=junk, g=g: e.activation(out=junk, in_=yg, func=AF.Square, accum_out=ssq[:, g:g + 1]),
                      kW1(i2, (0,)), kW1(i2, (3,)) + [("ssq", g)])
                    ts("dve", rsg[:, g:g + 1], ssq[:, g:g + 1], 1.0 / 512.0, RMS_EPS, ALU.mult, ALU.add, [("ssq", g)], [("rsg", g)])
                    act(rsg[:, g:g + 1], rsg[:, g:g + 1], AF.Ln, [("rsg", g)], [("rsg", g)])
                    act(rsg[:, g:g + 1], rsg[:, g:g + 1], AF.Exp, [("rsg", g)], [("rsg", g)], scale=-0.5)
                    act(yn[:, g * 512:(g + 1) * 512], yg, AF.Identity, kW1(i2, (0,)) + [("rsg", g)], [("yn", g)], scale=rsg[:, g:g + 1])
                    mm(ps[bst][:], Btm[:, g * 128:(g + 1) * 128], xdt2[:, g * 512:(g + 1) * 512], True, True,
                       ["Btm", ("xdt2", g // 2)], [("ps", bst)])
                    Sg = Sst[:, li, g * 512:(g + 1) * 512]
                    tt("pool", Sg.rearrange("p (h d) -> p h d", d=64), Sg.rearrange("p (h d) -> p h d", d=64),
                       bc(cdb[:, g * 8:(g + 1) * 8].unsqueeze(2), [128, 8, 64]), ALU.mult, [("Sst", li), "ex"], [("Sst", li)])
                    tt("dve", Sg, Sg, ps[bst][:], ALU.add, [("Sst", li), ("ps", bst)], [("Sst", li)])
                    cp("act", Sbf[:, g * 512:(g + 1) * 512], Sg, [("Sst", li)], KSBF)
                for rnd in range(2):
                    for j in range(8):
                        blk = rnd * 8 + j
                        tp(pb16[:, j * 128:(j + 1) * 128], yn[:, blk * 128:(blk + 1) * 128], cbf[:, IDENT, :],
                           [("yn", blk // 4), "cbf"], [("ps", 4)])
                    og = O_NG + rnd * 8
                    tt("dve", yT[:, rnd * 8:(rnd + 1) * 8, cs_], pb16.rearrange("p (k l) -> p k l", l=128),
                       bc(pc[:, li, og:og + 8].unsqueeze(2), [128, 8, 128]), ALU.mult, [("ps", 4), "pc"], kA1[rnd * 8:(rnd + 1) * 8])
            dbg("yT", yT.rearrange("p a t -> p (a t)"), kA1, row0=dbgrow)
            if not BIG:
                sg_branch()
            for gi in range(2):
                proj_fm(li, "gb%d" % gi, xb, [KXB], 8, 512, 4,
                        lambda j, p_, pk: act(sgt[:, j, :], p_, AF.Sigmoid, [pk], [ksgt(j)]))
                for pi in range(2):
                    def ev2(jj, p_, pk, gi=gi, pi=pi):
                        blk = (gi * 2 + pi) * 2 + jj
                        j = pi * 2 + jj
                        tmp = W1[:, jj, 1028:1028 + T]
                        tt("dve", tmp, p_, sgt[:, j, :], ALU.mult, [pk, ksgt(j)], kW1(jj, (3,)))
                        tt("pool", mg[:, blk, :], tmp, mg[:, blk, :], ALU.add, kW1(jj, (3,)) + [kmg[blk]], [kmg[blk]])
                    proj_fm(li, "pb%d" % (gi * 2 + pi), yT, kA1, 16, 256, 2, ev2)
            for gi in range(2):
                proj_fm(li, "mo%d" % gi, mg, kmg, 8, 512, 4,
                        lambda j, p_, pk, gi=gi: stt(xt[:, gi * 4 + j, :], xt[:, gi * 4 + j, :], ALPHA, p_, ALU.mult, ALU.add,
                                                     [(KX[0], gi * 4 + j), pk], [(KX[0], gi * 4 + j)]))
            layer_norm(li, 0)
            dbg("x1", xt[:].rearrange("p a t -> p (a t)"), kxt(), row0=dbgrow)

        def xattn(li):
            qT = A1[:, 0:8, :]
            ET = A1[:, 8:16, :]
            oT = A3
            for gi in range(2):
                proj_fm(li, "xq%d" % gi, xb, [KXB], 8, 512, 4,
                        lambda j, p_, pk, gi=gi: act(qT[:, gi * 4 + j, :], p_, AF.Identity, [pk], [kA1[gi * 4 + j]], scale=0.0625))
            rden = lnst[:, 0, :]
            for hd in range(4):
                for mc in range(2):
                    b = nextps()
                    for dd in range(2):
                        mm(ps[b][:, 0:T], KT[:, li, 2 * hd + dd, mc * 128:(mc + 1) * 128], qT[:, 2 * hd + dd, :], dd == 0, dd == 1,
                           [("KT", li), kA1[2 * hd + dd]], [("ps", b)])
                    act(ET[:, hd * 2 + mc, :], ps[b][:, 0:T], AF.Exp, [("ps", b)], [kA1[8 + hd * 2 + mc]])
                for mc in range(2):
                    mm(ps[6][:, 0:T], cbf[:, ONES, :], ET[:, hd * 2 + mc, :], mc == 0, mc == 1, ["cbf", kA1[8 + hd * 2 + mc]], [("ps", 6)])
                act(rden, ps[6][:, 0:T], AF.Ln, [("ps", 6)], ["ln0"])
                act(rden, rden, AF.Exp, ["ln0"], ["ln0"], scale=-1.0)
                for dd in range(2):
                    b = nextps()
                    for mc in range(2):
                        mm(ps[b][:, 0:T], Vt[:, li, mc, (2 * hd + dd) * 128:(2 * hd + dd + 1) * 128], ET[:, hd * 2 + mc, :],
                           mc == 0, mc == 1, [("Vt", li), kA1[8 + hd * 2 + mc]], [("ps", b)])
                    tt("dve", oT[:, 2 * hd + dd, :], ps[b][:, 0:T], rden, ALU.mult, [("ps", b), "ln0"], [kA3[2 * hd + dd]])
            for gi in range(2):
                proj_fm(li, "xo%d" % gi, oT, kA3, 8, 512, 4,
                        lambda j, p_, pk, gi=gi: stt(xt[:, gi * 4 + j, :], xt[:, gi * 4 + j, :], ALPHA, p_, ALU.mult, ALU.add,
                                                     [(KX[0], gi * 4 + j), pk], [(KX[0], gi * 4 + j)]))
            layer_norm(li, 1)

        def ffn(li):
            hT = A2[:, 0:NKF, :]
            for gi in range(11):
                wt_, wk = wload(li, "fi%d" % gi)
                wv = w3(wt_, 512)
                for jj in range(2):
                    bg = nextps()
                    for kc in range(8):
                        mm(ps[bg][:, 0:T], wv[:, kc, jj * 128:(jj + 1) * 128], xb[:, kc, :], kc == 0, kc == 7, [wk, KXB], [("ps", bg)])
                    bu = nextps()
                    for kc in range(8):
                        mm(ps[bu][:, 0:T], wv[:, kc, (2 + jj) * 128:(3 + jj) * 128], xb[:, kc, :], kc == 0, kc == 7, [wk, KXB], [("ps", bu)])
                    tmp = lnst[:, jj, :]
                    act(tmp, ps[bg][:, 0:T], AF.Silu, [("ps", bg)], ["ln%d" % jj])
                    tt("dve", hT[:, gi * 2 + jj, :], ps[bu][:, 0:T], tmp, ALU.mult, [("ps", bu), "ln%d" % jj], [kA2[gi * 2 + jj]])
            for blk in range(8):
                wt_, wk = wload(li, "fo%d" % blk)
                wv = wt_[:, 0:NKF * 128].rearrange("p (k c) -> p k c", c=128)
                b = nextps()
                for kc in range(NKF):
                    mm(ps[b][:, 0:T], wv[:, kc, :], hT[:, kc, :], kc == 0, kc == NKF - 1, [wk, kA2[kc]], [("ps", b)])
                stt(xt[:, blk, :], xt[:, blk, :], ALPHA, ps[b][:, 0:T], ALU.mult, ALU.add, [(KX[0], blk), ("ps", b)], [(KX[0], blk)])
            layer_norm(li, 2)

        for s in range(NSEQ):
            if "xattn" in cfg.stages:
                mem_phase(s)
            for li in range(NL):
                A("pool", lambda e, li=li: e.memset(Sst[:, li, :], 0.0), [], [("Sst", li)])
                A("pool", lambda e, li=li: e.memset(halo[:, li], 0.0), [("halo", li, b_) for b_ in range(24)], [("halo", li, b_) for b_ in range(24)])
            def xload(ti_):
                row_ = (s * NT + ti_) * 128
                buf = xts[ti_ % nxb]
                A("pool", lambda e: e.dma_start(out=buf[:].rearrange("p a t -> p (a t)"), in_=xin[row_:row_ + 128, :]),
                  r=[], w=[("xt%d" % (ti_ % nxb), k_) for k_ in range(8)], dma="x%d" % (ti_ % nxb), cost=6000.0)
            if nxb == 2:
                xload(0)
            for ti in range(NT):
                row = (s * NT + ti) * 128
                xt = xts[ti % nxb]
                KX[0] = "xt%d" % (ti % nxb)
                if nxb == 1:
                    xload(ti)
                if nxb == 2 and ti + 1 < NT:
                    xload(ti + 1)
                cp("act", xb[:], xt[:], kxt(), [KXB])
                for li in range(NL):
                    if "mixer" in cfg.stages:
                        mixer(li, (s * NT + ti) * 128)
                    if "xattn" in cfg.stages:
                        xattn(li)
                    if "ffn" in cfg.stages:
                        ffn(li)
                A("pool", lambda e, row=row, xt=xt: e.dma_start(out=yout[row:row + 128, :], in_=xt[:].rearrange("p a t -> p (a t)")),
                  r=kxt(), w=[("y", row)], dma="y%d" % (ti % nxb), cost=6000.0)
        allout = [("y", (s * NT + ti) * 128) for s in range(NSEQ) for ti in range(NT)]
        allout += [k for k in P.last_writer if isinstance(k, tuple) and k[0] == "dbg"]
        A("sp", lambda e: e.nop(), r=allout, w=[])

        if cfg.sched:
            P.schedule()
            if cfg.dbg is not None and "verbose" in cfg.stages:
                print("sched est_ns", P.est_ns, {e: (len(P.ops[e]), round(sum(o.cost for o in P.ops[e] if not o.is_dma))) for e in ENGS})
        P.finalize()
        with nc.Block() as block:
            @block.sync
            def _(e):
                P.emit("sp", e, sems, chans)

            @block.scalar
            def _(e):
                P.emit("act", e, sems, chans)

            @block.vector
            def _(e):
                P.emit("dve", e, sems, chans)

            @block.gpsimd
            def _(e):
                P.emit("pool", e, sems, chans)

            @block.tensor
            def _(e):
                P.emit("pe", e, sems, chans)
    return nc


def _kgroup(Wm, c0, ncol):
    K = Wm.shape[0]
    blk = Wm[:, c0:c0 + ncol].reshape(K // 128, 128, ncol).transpose(1, 0, 2).reshape(128, -1)
    out = np.zeros((128, GSZ), np.float32)
    out[:, :blk.shape[1]] = blk
    return out


def layout_weights(inp, layers):
    allg = []
    for l in layers:
        w_in = inp["w_in"][l]
        g = {}
        for i in range(4):
            g["kv%d" % i] = _kgroup(inp["w_xkv"][l], i * 512, 512)
        for i in range(2):
            g["u%d" % i] = _kgroup(w_in, i * 512, 512)
            g["v%d" % i] = _kgroup(w_in, 1024 + i * 512, 512)
            g["ga%d" % i] = _kgroup(w_in, 7200 + i * 512, 512)
            g["gb%d" % i] = _kgroup(w_in, 8224 + i * 512, 512)
            g["pa%d" % i] = _kgroup(inp["p_a"][l], i * 512, 512)
            g["mo%d" % i] = _kgroup(inp["w_mix_o"][l], i * 512, 512)
            g["xq%d" % i] = _kgroup(inp["w_xq"][l], i * 512, 512)
            g["xo%d" % i] = _kgroup(inp["w_xo"][l], i * 512, 512)
        for i in range(4):
            g["z%d" % i] = _kgroup(w_in, 2048 + i * 512, 512)
            g["pb%d" % i] = _kgroup(inp["p_b"][l], i * 256, 256)
        for i in range(6):
            g["xbc%d" % i] = _kgroup(w_in, 4096 + i * 512, 512)
        wfi = inp["w_ffn_in"][l]
        for i in range(11):
            cols = np.concatenate([np.arange(i * 256, i * 256 + 256), FFH + np.arange(i * 256, i * 256 + 256)])
            g["fi%d" % i] = _kgroup(wfi[:, cols], 0, 512)
        for i in range(8):
            g["fo%d" % i] = _kgroup(inp["w_ffn_out"][l], i * 128, 128)
        allg += [g[n] for n in GROUPS]
    return np.ascontiguousarray(np.stack(allg).reshape(-1, GSZ))


def layout_small(inp, layers):
    NL = len(layers)
    ident = np.eye(128, dtype=np.float32)
    k = np.arange(128)
    tri = (k[:, None] <= k[None, :]).astype(np.float32)
    um = (k[:, None] > k[None, :]).astype(np.float32)
    ones = np.ones((128, 128), np.float32)
    cst = np.concatenate([ident, tri, um, ones, ones / 1024.0], axis=1)
    pcs, prs, wdts, sgws = [], [], [], []
    for l in layers:
        cols = []
        cols.append(inp["ln_g"][l].reshape(3, 8, 128).transpose(2, 0, 1).reshape(128, 24))
        cols.append(inp["ln_b"][l].reshape(3, 8, 128).transpose(2, 0, 1).reshape(128, 24))
        cols.append(inp["sg_ln_g"][l].reshape(8, 128).T)
        cols.append(inp["sg_ln_b"][l].reshape(8, 128).T)
        cols.append(inp["conv_w"][l].reshape(4, 24, 128).transpose(2, 1, 0).reshape(128, 96))
        cols.append(inp["conv_b"][l].reshape(24, 128).T)
        cols.append(np.repeat(inp["d_skip"][l], 64).reshape(16, 128).T)
        cols.append(inp["ssm_norm_g"][l].reshape(16, 128).T)
        pcs.append(np.concatenate(cols, axis=1))
        prs.append(np.concatenate([inp["dt_bias"][l], inp["a_log"][l], inp["sg_b"][l].reshape(-1)]))
        wdts.append(inp["w_in"][l][:, 7168:7200].reshape(8, 128, 32).transpose(1, 0, 2).reshape(128, 256))
        sgws.append(inp["sg_w"][l].transpose(2, 0, 1).reshape(128, 1024))
    pcol = np.concatenate(pcs, axis=1).astype(np.float32)
    prow = np.concatenate(prs + [inp["mem_ln_g"], inp["mem_ln_b"]])[None, :].astype(np.float32)
    return {"cst": np.ascontiguousarray(cst), "pcol": np.ascontiguousarray(pcol), "prow": np.ascontiguousarray(prow),
            "wdt": np.ascontiguousarray(np.concatenate(wdts, axis=1)), "sgw": np.ascontiguousarray(np.concatenate(sgws, axis=1))}


def layout_x(xs, nt, T):
    nseq = xs.shape[0]
    a = xs.reshape(nseq, nt, T, 8, 128).transpose(0, 1, 4, 3, 2)
    return np.ascontiguousarray(a.reshape(nseq * nt * 128, 8 * T))


def unlayout_x(y, nseq, nt, T):
    a = y.reshape(nseq, nt, 128, 8, T).transpose(0, 1, 4, 3, 2)
    return np.ascontiguousarray(a.reshape(nseq, nt * T, D))


def kernel(**inputs):
    inp = {k: np.asarray(v) for k, v in inputs.items()}
    cfg = Cfg(nseq=2, nt=SEQ // 512, nch=4, layers=(0, 1))
    nc = build(cfg)
    wall = layout_weights(inp, cfg.layers)
    small = layout_small(inp, cfg.layers)
    in_maps = []
    for c in range(NCORES):
        m = {"xT": layout_x(inp["x"][2 * c:2 * c + 2], cfg.nt, cfg.T),
             "mem": np.ascontiguousarray(inp["mem"][2 * c:2 * c + 2].reshape(-1, D)), "wall": wall}
        m.update(small)
        in_maps.append(m)
    res = run_bass_kernel_spmd(nc, in_maps, core_ids=list(range(NCORES)))
    out = np.concatenate([unlayout_x(r["yT"], 2, cfg.nt, cfg.T) for r in res.results], axis=0)
    return out.astype(np.float32)
```

```python
import contextlib
import numpy as np
import concourse.bass as bass
import concourse.mybir as mybir
from concourse.bass_utils import run_bass_kernel_spmd

F32 = mybir.dt.float32
BF16 = mybir.dt.bfloat16
AF = mybir.ActivationFunctionType
ALU = mybir.AluOpType

D = 1024
SEQ = 2048
BATCH = 16
DEPTH = 2
MEM = 256
NCORES = 8
ALPHA = float((2 * DEPTH) ** 0.25)
LN_EPS = 1e-5
RMS_EPS = 1e-5
FFH = 2816
NKF = FFH // 128
GSZ = 4096

GROUPS = (["kv%d" % i for i in range(4)] + ["v0", "v1", "u0", "u1", "ga0", "pa0", "ga1", "pa1"]
          + ["xbc%d" % i for i in range(6)] + ["z%d" % i for i in range(4)]
          + ["gb0", "pb0", "pb1", "gb1", "pb2", "pb3", "mo0", "mo1", "xq0", "xq1", "xo0", "xo1"]
          + ["fi%d" % i for i in range(11)] + ["fo%d" % i for i in range(8)])
GIDX = {n: i for i, n in enumerate(GROUPS)}
NG = len(GROUPS)

ENGS = ("pe", "act", "dve", "pool", "sp")


class Op:
    __slots__ = ("eng", "fn", "deps", "marked", "count", "is_dma", "chan", "chan_val", "cost", "idx", "fin", "ndep", "succ", "rdy")

    def __init__(self, eng, fn, is_dma):
        self.cost = 300.0
        self.idx = 0
        self.eng = eng
        self.fn = fn
        self.deps = {}
        self.marked = False
        self.count = 0
        self.is_dma = is_dma
        self.chan = None
        self.chan_val = 0


class Prog:
    def __init__(self, nc):
        self.nc = nc
        self.ops = {e: [] for e in ENGS}
        self.last_writer = {}
        self.readers = {}
        self.chan_count = {}
        self.chan_last = {}
        self.bulk = set()
        self.nops = 0

    def add(self, eng, fn, reads=(), writes=(), dma=None, cost=300.0):
        op = Op(eng, fn, dma is not None)
        op.cost = cost
        op.idx = self.nops
        self.nops += 1
        for k in reads:
            w = self.last_writer.get(k)
            if w is not None:
                op.deps[w] = True
        for k in writes:
            w = self.last_writer.get(k)
            if w is not None and w not in op.deps:
                op.deps[w] = False
            for r in self.readers.get(k, ()):
                if r is not op and r not in op.deps:
                    op.deps[r] = False
        for k in reads:
            self.readers.setdefault(k, []).append(op)
        for k in writes:
            self.last_writer[k] = op
            self.readers[k] = []
        if dma is not None:
            op.chan = dma
            self.chan_count[dma] = self.chan_count.get(dma, 0) + 16
            op.chan_val = self.chan_count[dma]
            if dma not in self.bulk:
                pl = self.chan_last.get(dma)
                if pl is not None:
                    op.deps[pl] = True
                self.chan_last[dma] = op
        for p, raw in op.deps.items():
            if (not p.is_dma) and self._needs_wait(op, p, raw):
                p.marked = True
        self.ops[eng].append(op)
        return op

    @staticmethod
    def _needs_wait(c, p, raw):
        if p.is_dma:
            return True
        if p.eng != c.eng:
            return True
        if c.is_dma:
            return True
        if c.eng == "pe":
            return False
        return True

    def emit(self, e, eng, sems, chan_sems):
        c = 0
        for op in self.ops[e]:
            if op.marked and not op.is_dma:
                c += 1
            op.count = c
        waited = {}
        for op in self.ops[e]:
            need = {}
            for p, raw in op.deps.items():
                if not self._needs_wait(op, p, raw):
                    continue
                if p.is_dma:
                    s, v = ("c", p.chan), (self.chan_count[p.chan] if p.chan in self.bulk else p.chan_val)
                else:
                    s, v = ("e", p.eng), p.count
                if v > need.get(s, 0):
                    need[s] = v
            for s, v in need.items():
                if waited.get(s, 0) >= v:
                    continue
                waited[s] = v
                sem = chan_sems[s[1]] if s[0] == "c" else sems[s[1]]
                eng.wait_ge(sem, v)
            ins = op.fn(eng)
            if op.is_dma:
                ins.then_inc(chan_sems[op.chan], 16)
            elif op.marked:
                ins.then_inc(sems[e], 1)

    def schedule(self, window=1500):
        import heapq
        allops = [op for e in ENGS for op in self.ops[e]]
        for op in allops:
            op.succ = []
            op.rdy = 0.0
        for op in allops:
            op.ndep = len(op.deps)
            for p in op.deps:
                p.succ.append(op)
        tfree = {e: 0.0 for e in ENGS}
        h_rdy = {e: [] for e in ENGS}
        h_idx = {e: [] for e in ENGS}
        order = {e: [] for e in ENGS}
        for op in allops:
            if op.ndep == 0:
                heapq.heappush(h_rdy[op.eng], (0.0, op.idx, op))
        done = 0
        nall = len(allops)
        base = 0
        sched = bytearray(nall)
        while done < nall:
            best = None
            for e in ENGS:
                hr, hi = h_rdy[e], h_idx[e]
                while hr and hr[0][0] <= tfree[e]:
                    _, i_, o_ = heapq.heappop(hr)
                    heapq.heappush(hi, (i_, o_))
                if hi:
                    cand = (tfree[e], 0, e)
                elif hr:
                    cand = (hr[0][0], 1, e)
                else:
                    continue
                if best is None or cand < best:
                    best = cand
            t0, kind, e = best
            if kind == 0:
                _, op = heapq.heappop(h_idx[e])
            else:
                _, _, op = heapq.heappop(h_rdy[e])
            start = max(tfree[e], op.rdy)
            if op.is_dma:
                tfree[e] = start + 60.0
                op.fin = start + op.cost
            elif e == "pe":
                tfree[e] = start + op.cost
                op.fin = start + op.cost + 150.0
            else:
                tfree[e] = start + op.cost
                op.fin = start + op.cost + 60.0
            order[e].append(op)
            done += 1
            for sx in op.succ:
                lat = 0.0 if (sx.eng == e and e == "pe") else 80.0
                if op.fin + lat > sx.rdy:
                    sx.rdy = op.fin + lat
                sx.ndep -= 1
                if sx.ndep == 0:
                    heapq.heappush(h_rdy[sx.eng], (sx.rdy, sx.idx, sx))
        self.ops = order
        self.est_ns = max(tfree.values())

    def finalize(self):
        for e in ENGS:
            c = 0
            for op in self.ops[e]:
                if op.marked and not op.is_dma:
                    c += 1
                op.count = c


def bc(ap, shape):
    return ap.to_broadcast(shape) if hasattr(ap, "to_broadcast") else ap.broadcast_to(shape)


class Cfg:
    def __init__(self, nseq=2, nt=8, nch=2, layers=(0, 1), dbg=None, stages=("mixer", "xattn", "ffn"), sched=True):
        self.nseq, self.nt, self.nch, self.layers = nseq, nt, nch, tuple(layers)
        self.T = nch * 128
        self.dbg = dbg or {}
        self.stages = stages
        self.sched = sched


def build(cfg):
    nc = bass.Bass("TRN2", target_bir_lowering=False)
    P = Prog(nc)
    P.bulk = {"cst"}
    NSEQ, NT, NCH, T = cfg.nseq, cfg.nt, cfg.nch, cfg.T
    LAY = cfg.layers
    NL = len(LAY)
    BIG = NCH >= 4

    xin = nc.dram_tensor("xT", [NSEQ * NT * 128, 8 * T], F32, kind="ExternalInput").ap()
    yout = nc.dram_tensor("yT", [NSEQ * NT * 128, 8 * T], F32, kind="ExternalOutput").ap()
    memd = nc.dram_tensor("mem", [NSEQ * 2 * 128, D], F32, kind="ExternalInput").ap()
    wall = nc.dram_tensor("wall", [NL * NG * 128, GSZ], F32, kind="ExternalInput").ap()
    wbf = nc.dram_tensor("wbf", [NL * NG * 128, GSZ], BF16, kind="Internal").ap()
    cst = nc.dram_tensor("cst", [128, 5 * 128], F32, kind="ExternalInput").ap()
    NPC = 3 * 8 * 2 + 8 * 2 + 24 * 5 + 16 * 2
    pcol = nc.dram_tensor("pcol", [128, NL * NPC], F32, kind="ExternalInput").ap()
    NPR = 32 * 2 + 1024
    prow = nc.dram_tensor("prow", [1, NL * NPR + 2 * D], F32, kind="ExternalInput").ap()
    wdt_d = nc.dram_tensor("wdt", [128, NL * 8 * 32], F32, kind="ExternalInput").ap()
    sgw_d = nc.dram_tensor("sgw", [128, NL * 8 * 128], F32, kind="ExternalInput").ap()
    dbg_d = {}
    for name, shp in cfg.dbg.items():
        dbg_d[name] = nc.dram_tensor("dbg_" + name, list(shp), F32, kind="ExternalOutput").ap()

    st = contextlib.ExitStack()
    with st:
        def sb(name, shape, dt):
            return st.enter_context(nc.sbuf_tensor(name, list(shape), dt))

        c32 = sb("c32", [128, 5, 128], F32)
        cbf = sb("cbf", [128, 5, 128], BF16)
        IDENT, TRI, UM, ONES, ODIV = 0, 1, 2, 3, 4
        pc = sb("pc", [128, NL, NPC], F32)
        O_LNG, O_LNB, O_SGG, O_SGB, O_CW, O_CB, O_DC, O_NG = 0, 24, 48, 56, 64, 160, 184, 200
        pr = sb("pr", [128, NL, 64], F32)
        wdt = sb("wdt_s", [128, NL, 8, 32], F32)
        wTm = sb("wTm", [128, NL, 8, 128], BF16)
        Esg = sb("Esg", [128, NL, 8, 128], F32)
        Ddg = sb("Ddg", [128, 16, 128], BF16)
        KT = sb("KT", [128, NL, 8, MEM], BF16)
        Vt = sb("Vt", [128, NL, 2, D], BF16)
        NSLOT = 3 if BIG else 4
        wr = [sb("wr%d" % i, [128, GSZ], BF16) for i in range(NSLOT)]
        xts = [sb("xt0", [128, 8, T], F32)] if BIG else [sb("xt0", [128, 8, T], F32), sb("xt1", [128, 8, T], F32)]
        nxb = len(xts)
        xt = xts[0]
        KX = ["xt0"]
        xb = sb("xb", [128, 8, T], BF16)
        Sst = sb("Sst", [128, NL, 2048], F32)
        halo = sb("halo", [128, NL, 24, 3], F32)
        A1 = sb("A1", [128, 16, T], BF16)
        A2 = sb("A2", [128, 32, T], BF16)
        A4 = A1 if BIG else sb("A4", [128, 16, T], BF16)
        A3 = sb("A3", [128, 8, T], BF16)
        mg = sb("mg", [128, 8, T], BF16)
        W1 = sb("W1", [128, 2, 1540], F32)
        W2 = sb("W2", [128, 2, 2048], BF16)
        W3 = sb("W3", [128, 10, 512], BF16)
        sgt = W2[:, 0, :].rearrange("p (j t) -> p j t", t=T) if BIG else sb("sgt", [128, 4, T], BF16)
        sm = sb("sm", [128, 512], F32)
        dahl = sb("dahl", [128, 4, 2, 32], BF16)
        lnst = sb("lnst", [128, 2, T], F32)
        if BIG:
            Sbf = lnst[:].rearrange("p a t -> p (a t)").bitcast(BF16)
            KSBF = ["ln0", "ln1"]
        else:
            Sbf = sb("Sbf", [128, 2048], BF16)
            KSBF = ["Sbf"]
        memgb = A2[:].rearrange("p a t -> p (a t)").bitcast(F32)[:, 0:2 * D].rearrange("p (a d) -> p a d", d=D)

        def ksgt(j):
            return ("xdt", j // 2) if BIG else ("sgt", j)
        kW2a = [("xdt", 0), ("xdt", 1)]
        kW2b = [("xdt2", 0), ("xdt2", 1)]

        ps = [st.enter_context(nc.psum_tensor("ps%d" % i, [128, 512], F32)) for i in range(8)]

        sems = {e: st.enter_context(nc.semaphore("s_" + e)) for e in ENGS}
        chan_names = (["cv%d" % i for i in range(8)] + ["cst", "sgw", "x0", "x1", "y0", "y1", "mem", "dbg"] + ["w%d" % i for i in range(NSLOT)])
        chans = {c: st.enter_context(nc.semaphore("c_" + c)) for c in chan_names}

        def kW1(i, qs):
            return sorted(set(("W1", i, (0 if q == 1 else q)) for q in qs))

        def flat(ks):
            out = []
            for k in ks:
                if isinstance(k, list):
                    out.extend(flat(k))
                else:
                    out.append(k)
            return out

        def A(eng, fn, r=(), w=(), dma=None, cost=300.0):
            return P.add(eng, fn, reads=flat(r), writes=flat(w), dma=dma, cost=cost)

        KXB = [("xb", k_) for k_ in range(8)]

        def kxt():
            return [(KX[0], k_) for k_ in range(8)]

        def fsz(ap):
            n = 1
            for d in ap.shape[1:]:
                n *= d
            return n

        ECOST = {"act": 0.75, "dve": 1.0, "pool": 2.0}

        def ecost(eng, out):
            return 220.0 + ECOST[eng] * fsz(out)

        def act(out, in_, func, r, w, **kw):
            A("act", lambda e: e.activation(out=out, in_=in_, func=func, **kw), r, w, cost=ecost("act", out) + (90.0 if kw else 0.0))

        def tt(eng, out, in0, in1, op, r, w):
            A(eng, lambda e: e.tensor_tensor(out=out, in0=in0, in1=in1, op=op), r, w, cost=ecost(eng, out))

        def ts(eng, out, in0, s1, s2, op0, op1, r, w):
            c_ = ecost(eng, out) if eng != "pool" else 3500.0
            if op1 is None:
                A(eng, lambda e: e.tensor_scalar(out=out, in0=in0, scalar1=s1, scalar2=None, op0=op0), r, w, cost=c_)
            else:
                A(eng, lambda e: e.tensor_scalar(out=out, in0=in0, scalar1=s1, scalar2=s2, op0=op0, op1=op1), r, w, cost=c_)

        def stt(out, in0, scalar, in1, op0, op1, r, w):
            A("dve", lambda e: e.scalar_tensor_tensor(out=out, in0=in0, scalar=scalar, in1=in1, op0=op0, op1=op1), r, w,
              cost=ecost("dve", out))

        def cp(eng, out, in_, r, w):
            if eng == "act":
                A("act", lambda e: e.copy(out=out, in_=in_), r, w, cost=ecost("act", out))
            else:
                A(eng, lambda e: e.tensor_copy(out=out, in_=in_), r, w, cost=ecost(eng, out))

        def mm(out, lhsT, rhs, start, stop, r, w):
            n_ = fsz(rhs)
            c_ = max(n_, 64) / 1.9 + 8.0
            if rhs.dtype == F32:
                c_ *= 4.0
            A("pe", lambda e: e.matmul(out, lhsT=lhsT, rhs=rhs, start=start, stop=stop), r, w, cost=c_)

        def tp(out, in_, ident, r, w):
            A("pe", lambda e: e.transpose(out, in_, ident), r, w, cost=90.0)

        def dbg(name, src_ap, key, row0=0):
            if name in dbg_d:
                d = dbg_d[name]
                n = src_ap.shape[0]
                A("pool", lambda e: e.dma_start(out=d[row0:row0 + n], in_=src_ap), r=key, w=[("dbg", name, row0)], dma="dbg")

        wstate = {"n": 0}

        def wload(li, gname):
            slot = wstate["n"] % NSLOT
            wstate["n"] += 1
            g = li * NG + GIDX[gname]
            A("sp", lambda e: e.dma_start(out=wr[slot][:], in_=wbf[g * 128:(g + 1) * 128, :]),
              r=[("wbf", g)], w=[("wr", slot)], dma="w%d" % slot, cost=7000.0)
            if li + 1 < NL:
                cast((li + 1) * NG + GIDX[gname], extra_reads=[("wr", slot)])
            return wr[slot], ("wr", slot)

        def w3(wt_, ncol):
            return wt_[:].rearrange("p (k c) -> p k c", c=ncol)

        ncast = {"n": 0}
        cast_done = set()

        def cast(g, extra_reads=()):
            if g in cast_done:
                return
            cast_done.add(g)
            i_ = ncast["n"]
            ncast["n"] += 1
            A("pool", lambda e: e.dma_start(out=wbf[g * 128:(g + 1) * 128, :], in_=wall[g * 128:(g + 1) * 128, :],
                                            max_dma_last_dim=8192), r=list(extra_reads), w=[("wbf", g)], dma="cv%d" % (i_ % 8), cost=12000.0)

        for i in range(NG):
            cast(i)
        A("sp", lambda e: e.dma_start(out=c32[:].rearrange("p a b -> p (a b)"), in_=cst), w=["c32"], dma="cst")
        A("sp", lambda e: e.dma_start(out=pc[:].rearrange("p a b -> p (a b)"), in_=pcol), w=["pc"], dma="cst")
        for li_ in range(NL):
            A("sp", lambda e, li_=li_: e.dma_start(out=pr[:, li_, :], in_=prow[:, li_ * NPR:li_ * NPR + 64].partition_broadcast(128)),
              w=(["pr"] if li_ == NL - 1 else [("prx", li_)]), dma="cst")
        A("sp", lambda e: e.dma_start(out=wdt[:].rearrange("p a b c -> p (a b c)"), in_=wdt_d), w=["wdt"], dma="cst")
        cp("dve", cbf[:], c32[:], ["c32"], ["cbf"])
        for li in range(NL):
            act(pr[:, li, 32:64], pr[:, li, 32:64], AF.Exp, ["pr"], ["pr"])
            ts("dve", pr[:, li, 32:64], pr[:, li, 32:64], -1.0, None, ALU.mult, None, ["pr"], ["pr"])
        sgbt = A2[:].rearrange("p a t -> p (a t)").bitcast(F32)[:, 0:NL * 1024].rearrange("p (l f) -> p l f", f=1024)
        for li_ in range(NL):
            A("sp", lambda e, li_=li_: e.dma_start(out=sgbt[:, li_, :],
                                                   in_=prow[:, li_ * NPR + 64:(li_ + 1) * NPR].partition_broadcast(128)),
              w=["sgbt"] + [("A2", i_) for i_ in range(32)], dma="sgw")
        for li in range(NL):
            w1v = W1[:, 0, 0:1024].rearrange("p (g t) -> p g t", t=128)
            A("sp", lambda e, li=li: e.dma_start(out=W1[:, 0, 0:1024], in_=sgw_d[:, li * 1024:(li + 1) * 1024]),
              r=[], w=[*kW1(0, (0, 2))], dma="sgw")
            tt("dve", w1v, w1v, bc(c32[:, TRI:TRI + 1, :], [128, 8, 128]), ALU.mult, [*kW1(0, (0, 2)), "c32"], [*kW1(0, (0, 2))])
            cp("pool", wTm[:, li], w1v, [*kW1(0, (0, 2))], [("wTm", li)])
            for hh in range(2):
                mm(ps[hh][:], c32[:, ONES, :], W1[:, 0, hh * 512:(hh + 1) * 512], True, True, [*kW1(0, (0, 2)), "c32"], [("ps", hh)])
                e_v = Esg[:, li, hh * 4:(hh + 1) * 4, :]
                tt("dve", e_v, ps[hh][:].rearrange("p (g t) -> p g t", t=128),
                   bc(pc[:, li, O_SGB + hh * 4:O_SGB + hh * 4 + 4].unsqueeze(2), [128, 4, 128]), ALU.mult,
                   [("ps", hh), "pc"], [("Esg", li)])
                tt("pool", e_v, e_v, sgbt[:, li, hh * 512:(hh + 1) * 512].rearrange("p (g t) -> p g t", t=128),
                   ALU.add, [("Esg", li), "sgbt"] + [("A2", i_) for i_ in range(32)], [("Esg", li)])
        dbg("Esg", Esg[:, 0].rearrange("p g t -> p (g t)"), [("Esg", 0)])

        eps_ln = LN_EPS
        psrot = {"n": 0}

        def nextps():
            b = psrot["n"] % 4
            psrot["n"] += 1
            return b

        def layer_norm(li, which):
            rbf = A1[:, 0:8, :]
            rsq = A1[:, 8:16, :]
            kA2 = [("A1", i) for i in range(16)]
            cp("dve", rbf, xt[:], kxt(), kA2[0:8])
            act(rsq, xt[:], AF.Square, kxt(), kA2[8:16])
            for kc in range(8):
                mm(ps[4][:, 0:T], cbf[:, ODIV, :], rbf[:, kc, :], kc == 0, kc == 7, kA2[0:8] + ["cbf"], [("ps", 4)])
            for kc in range(8):
                mm(ps[5][:, 0:T], cbf[:, ODIV, :], rsq[:, kc, :], kc == 0, kc == 7, kA2[8:16] + ["cbf"], [("ps", 5)])
            mean, var, rstd = lnst[:, 0, :], lnst[:, 1, :], lnst[:, 1, :]
            cp("act", mean, ps[4][:, 0:T], [("ps", 4)], ["ln0"])
            act(var, ps[4][:, 0:T], AF.Square, [("ps", 4)], ["ln1"])
            tt("dve", var, ps[5][:, 0:T], var, ALU.subtract, [("ps", 5), "ln1"], ["ln1"])
            ts("dve", var, var, eps_ln, None, ALU.add, None, ["ln1"], ["ln1"])
            act(rstd, var, AF.Ln, ["ln1"], ["ln1"])
            act(rstd, rstd, AF.Exp, ["ln1"], ["ln1"], scale=-0.5)
            for kc in range(8):
                o = O_LNG + which * 8 + kc
                ob = O_LNB + which * 8 + kc
                kk = (KX[0], kc)
                tt("dve", xt[:, kc, :], xt[:, kc, :], mean, ALU.subtract, [kk, "ln0"], [kk])
                stt(xt[:, kc, :], xt[:, kc, :], pc[:, li, o:o + 1], rstd, ALU.mult, ALU.mult, [kk, "ln1", "pc"], [kk])
                act(xb[:, kc, :], xt[:, kc, :], AF.Identity, [kk, "pc"], [("xb", kc)], bias=pc[:, li, ob:ob + 1])
                ts("dve", xt[:, kc, :], xt[:, kc, :], pc[:, li, ob:ob + 1], None, ALU.add, None, [kk, "pc"], [kk])

        def mem_phase(s):
            mraw = W1[:].rearrange("p a b -> p (a b)")[:, 0:2048].rearrange("p (c d) -> p c d", d=D)
            kmgb = [("A2", i_) for i_ in range(32)]
            A("pool", lambda e: e.dma_start(out=memgb.rearrange("p a b -> p (a b)"),
                                            in_=prow[:, NL * NPR:NL * NPR + 2 * D].partition_broadcast(128)), r=[], w=kmgb, dma="mem")
            kW1m = kW1(0, (0, 2, 3)) + kW1(1, (0,))
            A("pool", lambda e: e.dma_start(out=mraw, in_=memd[s * 256:(s + 1) * 256, :].rearrange("(c p) d -> p c d", p=128)),
              r=[], w=kW1m, dma="mem")
            st6 = sm[:, 0:24].rearrange("p (c h k) -> p c h k", c=2, h=2)
            mv = sm[:, 24:28].rearrange("p (c k) -> p c k", k=2)
            for c in range(2):
                for h in range(2):
                    A("dve", lambda e, c=c, h=h: e.bn_stats(out=st6[:, c, h, :], in_=mraw[:, c, h * 512:(h + 1) * 512]), kW1m, ["sm"])
                A("dve", lambda e, c=c: e.bn_aggr(out=mv[:, c, :], in_=st6[:, c].rearrange("p h k -> p (h k)")), ["sm"], ["sm"])
            rs = sm[:, 28:30]
            ts("dve", rs, mv[:, :, 1], LN_EPS, None, ALU.add, None, ["sm"], ["sm"])
            act(rs, rs, AF.Ln, ["sm"], ["sm"])
            act(rs, rs, AF.Exp, ["sm"], ["sm"], scale=-0.5)
            mnb = W2[:, 0, :].rearrange("p (c d) -> p c d", d=D)
            for c in range(2):
                ts("dve", mraw[:, c, :], mraw[:, c, :], mv[:, c, 0:1], rs[:, c:c + 1], ALU.subtract, ALU.mult, kW1m + ["sm"], kW1m)
                tt("pool", mraw[:, c, :], mraw[:, c, :], memgb[:, 0, :], ALU.mult, kW1m + kmgb, kW1m)
                tt("dve", mnb[:, c, :], mraw[:, c, :], memgb[:, 1, :], ALU.add, kW1m + kmgb, kW2a)
            memT = W2[:, 1, :].rearrange("p (k m) -> p k m", m=MEM)
            for c in range(2):
                pb16 = ps[4 + c][:].bitcast(BF16)
                for kc in range(8):
                    tp(pb16[:, kc * 128:(kc + 1) * 128], mnb[:, c, kc * 128:(kc + 1) * 128], cbf[:, IDENT, :],
                       kW2a + ["cbf"], [("ps", 4 + c)])
                cp("act", memT[:, :, c * 128:(c + 1) * 128], pb16.rearrange("p (k m) -> p k m", m=128),
                   [("ps", 4 + c)], kW2b)
            for li in range(NL):
                for gi in range(2):
                    wt_, wk = wload(li, "kv%d" % gi)
                    wv = w3(wt_, 512)
                    for j in range(4):
                        b = nextps()
                        for kc in range(8):
                            mm(ps[b][:, 0:MEM], wv[:, kc, j * 128:(j + 1) * 128], memT[:, kc, :], kc == 0, kc == 7,
                               [wk] + kW2b, [("ps", b)])
                        cp("act", KT[:, li, gi * 4 + j, :], ps[b][:, 0:MEM], [("ps", b)], [("KT", li)])
                for gi in range(2):
                    wt_, wk = wload(li, "kv%d" % (2 + gi))
                    wv = w3(wt_, 512)
                    for c in range(2):
                        b = nextps()
                        for kc in range(8):
                            mm(ps[b][:], memT[:, kc, c * 128:(c + 1) * 128], wv[:, kc, :], kc == 0, kc == 7,
                               [wk] + kW2b, [("ps", b)])
                        cp("dve", Vt[:, li, c, gi * 512:(gi + 1) * 512], ps[b][:], [("ps", b)], [("Vt", li)])

        def proj_fm(li, gname, rhs3, rkeys, nk, ncol, nblk, evac):
            wt_, wk = wload(li, gname)
            wv = w3(wt_, ncol)
            for j in range(nblk):
                b = nextps()
                for kc in range(nk):
                    mm(ps[b][:, 0:T], wv[:, kc, j * 128:(j + 1) * 128], rhs3[:, kc, :], kc == 0, kc == nk - 1,
                       [wk] + rkeys, [("ps", b)])
                evac(j, ps[b][:, 0:T], ("ps", b))

        kA1 = [("A1", i) for i in range(16)]
        kA2 = [("A2", i) for i in range(32)]
        kA3 = [("A3", i) for i in range(8)]
        kA4 = kA1 if BIG else [("A4", i) for i in range(16)]
        kmg = [("mg", i) for i in range(8)]

        def mixer(li, dbgrow):
            def sg_branch():
                uT = A4[:, 0:8, :]
                vn = A4[:, 8:16, :].rearrange("p a t -> p (a t)").rearrange("p (c f) -> p c f", f=1024)
                st6 = sm[:, 0:NCH * 12].rearrange("p (c h k) -> p c h k", c=NCH, h=2)
                mv = sm[:, 48:48 + NCH * 2].rearrange("p (c k) -> p c k", k=2)
                rs = sm[:, 64:64 + NCH]
                for gi in range(2):
                    wt_, wk = wload(li, "v%d" % gi)
                    wv = w3(wt_, 512)
                    for c in range(NCH):
                        b = nextps()
                        for kc in range(8):
                            mm(ps[b][:], xb[:, kc, c * 128:(c + 1) * 128], wv[:, kc, :], kc == 0, kc == 7, [wk, KXB], [("ps", b)])
                        act(vn[:, c, gi * 512:(gi + 1) * 512], ps[b][:], AF.Gelu, [("ps", b)], kA4[8:16])
                        A("dve", lambda e, c=c, gi=gi: e.bn_stats(out=st6[:, c, gi, :], in_=vn[:, c, gi * 512:(gi + 1) * 512]),
                          kA4[8:16], ["sm"])
                for c in range(NCH):
                    A("dve", lambda e, c=c: e.bn_aggr(out=mv[:, c, :], in_=st6[:, c].rearrange("p h k -> p (h k)")), ["sm"], ["sm"])
                ts("dve", rs, mv[:, :, 1], LN_EPS, None, ALU.add, None, ["sm"], ["sm"])
                act(rs, rs, AF.Ln, ["sm"], ["sm"])
                act(rs, rs, AF.Exp, ["sm"], ["sm"], scale=-0.5)
                nmr = sm[:, 72:72 + NCH]
                tt("dve", nmr, mv[:, :, 0], rs, ALU.mult, ["sm"], ["sm"])
                ts("dve", nmr, nmr, -1.0, None, ALU.mult, None, ["sm"], ["sm"])
                for c in range(NCH):
                    act(vn[:, c, :], vn[:, c, :], AF.Identity, kA4[8:16] + ["sm"], kA4[8:16], scale=rs[:, c:c + 1], bias=nmr[:, c:c + 1])
                for gi in range(2):
                    proj_fm(li, "u%d" % gi, xb, [KXB], 8, 512, 4,
                            lambda j, p_, pk, gi=gi: act(uT[:, gi * 4 + j, :], p_, AF.Gelu, [pk], [kA4[gi * 4 + j]]))
                for c in range(NCH):
                    for g in range(8):
                        b = 6 + g // 4
                        mm(ps[b][:, (g % 4) * 128:(g % 4 + 1) * 128], vn[:, c, g * 128:(g + 1) * 128], wTm[:, li, g, :],
                           True, True, kA4[8:16] + [("wTm", li)], [("ps", b)])
                    t1 = W1[:, 0, 0:1024].rearrange("p (g t) -> p g t", t=128)
                    for hh in range(2):
                        tt("dve", t1[:, hh * 4:(hh + 1) * 4, :], ps[6 + hh][:].rearrange("p (g t) -> p g t", t=128),
                           bc(pc[:, li, O_SGG + hh * 4:O_SGG + hh * 4 + 4].unsqueeze(2), [128, 4, 128]), ALU.mult,
                           [("ps", 6 + hh), "pc"], [*kW1(0, (0, 2))])
                    tt("pool", t1, t1, Esg[:, li], ALU.add, [*kW1(0, (0, 2)), ("Esg", li)], [*kW1(0, (0, 2))])
                    tt("dve", uT[:, :, c * 128:(c + 1) * 128], t1, uT[:, :, c * 128:(c + 1) * 128], ALU.mult,
                       [*kW1(0, (0, 2))] + kA4[0:8], kA4[0:8])
                dbg("saT", uT.rearrange("p a t -> p (a t)"), kA4[0:8], row0=dbgrow)
                for gi in range(2):
                    proj_fm(li, "ga%d" % gi, xb, [KXB], 8, 512, 4,
                            lambda j, p_, pk: act(sgt[:, j, :], p_, AF.Sigmoid, [pk], [ksgt(j)]))
                    proj_fm(li, "pa%d" % gi, uT, kA4[0:8], 8, 512, 4,
                            lambda j, p_, pk, gi=gi: tt("dve", mg[:, gi * 4 + j, :], p_, sgt[:, j, :], ALU.mult,
                                                        [pk, ksgt(j)], [kmg[gi * 4 + j]]))
                dbg("m1", mg[:].rearrange("p a t -> p (a t)"), kmg, row0=dbgrow)


            if BIG:
                sg_branch()
            for kc in range(16):
                act(Ddg[:, kc, :], cbf[:, IDENT, :], AF.Identity, ["cbf", "pc"], [("Ddg",)], scale=pc[:, li, O_DC + kc:O_DC + kc + 1])
            xsT = A2[:, 16:32, :]
            BCT = A3
            ztm = A2[:, 0:16, :].rearrange("p a t -> p (a t)").rearrange("p (c f) -> p c f", f=2048)
            kxs = kA2[16:32]
            kz = kA2[0:16]
            for gi in range(6):
                def ev(j, p_, pk, gi=gi):
                    blk = gi * 4 + j
                    i = blk % 2
                    raw = W1[:, i, 0:T + 3]
                    acc = W1[:, i, 516:516 + T]
                    kr, ka = kW1(i, (0, 1)), kW1(i, (2,))
                    cp("pool", raw[:, 0:3], halo[:, li, blk, :], [("halo", li, blk)], kr)
                    act(raw[:, 3:3 + T], p_, AF.Identity, [pk], kr)
                    cp("pool", halo[:, li, blk, :], raw[:, T:T + 3], kr, [("halo", li, blk)])
                    ow = O_CW + blk * 4
                    act(acc, raw[:, 0:T], AF.Identity, kr + ["pc"], ka, scale=pc[:, li, ow:ow + 1],
                        bias=pc[:, li, O_CB + blk:O_CB + blk + 1])
                    for k in range(1, 4):
                        stt(acc, raw[:, k:k + T], pc[:, li, ow + k:ow + k + 1], acc, ALU.mult, ALU.add, kr + ka + ["pc"], ka)
                    if blk < 16:
                        act(xsT[:, blk, :], acc, AF.Silu, ka, [kxs[blk]])
                    else:
                        act(BCT[:, blk - 16, :], acc, AF.Silu, ka, [kA3[blk - 16]])
                proj_fm(li, "xbc%d" % gi, xb, [KXB], 8, 512, 4, ev)
            for gi in range(4):
                wt_, wk = wload(li, "z%d" % gi)
                wv = w3(wt_, 512)
                for c in range(NCH):
                    b = nextps()
                    for kc in range(8):
                        mm(ps[b][:], xb[:, kc, c * 128:(c + 1) * 128], wv[:, kc, :], kc == 0, kc == 7, [wk, KXB], [("ps", b)])
                    act(ztm[:, c, gi * 512:(gi + 1) * 512], ps[b][:], AF.Silu, [("ps", b)], kz)
            for c in range(NCH):
                dtc = sm[:, 96 + c * 32:128 + c * 32]
                dac = sm[:, 224 + c * 32:256 + c * 32]
                for kc in range(8):
                    mm(ps[5][:, 0:32], xt[:, kc, c * 128:(c + 1) * 128], wdt[:, li, kc, :], kc == 0, kc == 7,
                       [kxt(), "wdt"], [("ps", 5)])
                tt("dve", dtc, ps[5][:, 0:32], pr[:, li, 0:32], ALU.add, [("ps", 5), "pr"], [("dt", c)])
                act(dtc, dtc, AF.Exp, [("dt", c)], [("dt", c)])
                act(dtc, dtc, AF.Ln, [("dt", c)], [("dt", c)], bias=1.0)
                tt("pool", dac, dtc, pr[:, li, 32:64], ALU.mult, [("dt", c), "pr"], [("da", c)])
                dhl = dahl[:, c]
                cp("dve", dhl[:, 0, :], dac, [("da", c)], [("dahl", c)])
                tt("dve", dhl[:, 1, :], dac, dhl[:, 0, :], ALU.subtract, [("da", c), ("dahl", c)], [("dahl", c)])
            cp("act", Sbf[:], Sst[:, li, :], [("Sst", li)], KSBF)
            yT = A1[:, 0:16, :]
            xdt = W2[:, 0, :]
            xdt2 = W2[:, 1, :]
            dec = [W3[:, 0, :], W3[:, 1, :], W3[:, 2, :], W3[:, 3, :]]
            CBm = W3[:, 4, :]
            Btm = W3[:, 5, :]
            yn = W3[:, 6:10, :].rearrange("p a b -> p (a b)")
            ex = sm[:, 352:448]
            ecs, cdb, dst = ex[:, 0:32], ex[:, 32:64], ex[:, 64:96]
            dtds = sm[:, 448:480]
            ssq = sm[:, 480:484]
            rsg = sm[:, 484:488]
            for c in range(NCH):
                cs_ = slice(c * 128, (c + 1) * 128)
                dtc = sm[:, 96 + c * 32:128 + c * 32]
                dac = sm[:, 224 + c * 32:256 + c * 32]
                mm(ps[5][:, 0:32], c32[:, TRI, :], dac, True, True, ["c32", ("da", c)], [("ps", 5)])
                mm(ps[5][:, 32:64], c32[:, ONES, :], dac, True, True, ["c32", ("da", c)], [("ps", 5)])
                mm(ps[5][:, 64:96], c32[:, UM, :], dac, True, True, ["c32", ("da", c)], [("ps", 5)])
                act(ex, ps[5][:, 0:96], AF.Exp, [("ps", 5)], ["ex"])
                tt("dve", dtds, dtc, dst, ALU.mult, [("dt", c), "ex"], ["dtds"])
                pb16 = ps[4][:].bitcast(BF16)
                for rnd in range(2):
                    for j in range(8):
                        blk = rnd * 8 + j
                        tp(pb16[:, j * 128:(j + 1) * 128], xsT[:, blk, cs_], cbf[:, IDENT, :], [kxs[blk], "cbf"], [("ps", 4)])
                    hs = slice(rnd * 16, (rnd + 1) * 16)
                    o1 = xdt[:, rnd * 1024:(rnd + 1) * 1024].rearrange("p (h d) -> p h d", d=64)
                    o2 = xdt2[:, rnd * 1024:(rnd + 1) * 1024].rearrange("p (h d) -> p h d", d=64)
                    tt("dve", o1, pb16.rearrange("p (h d) -> p h d", d=64), bc(dtc[:, hs].unsqueeze(2), [128, 16, 64]),
                       ALU.mult, [("ps", 4), ("dt", c)], [("xdt", rnd)])
                    tt("pool", o2, o1, bc(dst[:, hs].unsqueeze(2), [128, 16, 64]), ALU.mult, [("xdt", rnd), "ex"], [("xdt2", rnd)])
                pB = ps[5][:, 256:512].bitcast(BF16)
                for g in range(4):
                    tp(pB[:, g * 128:(g + 1) * 128], BCT[:, g, cs_], cbf[:, IDENT, :], [kA3[g], "cbf"], [("ps", 5)])
                cp("act", Btm, pB, [("ps", 5)], ["Btm"])
                for g in range(4):
                    mm(ps[2][:, g * 128:(g + 1) * 128], BCT[:, g, cs_], BCT[:, 4 + g, cs_], True, True,
                       [kA3[g], kA3[4 + g]], [("ps", 2)])
                tt("dve", CBm.rearrange("p (g l) -> p g l", l=128), ps[2][:].rearrange("p (g l) -> p g l", l=128),
                   bc(c32[:, TRI:TRI + 1, :], [128, 4, 128]), ALU.mult, [("ps", 2), "c32"], ["CBm"])
                def stageA(g):
                    for hh in range(2):
                        h0 = g * 8 + hh * 4
                        di = (g % 2) * 2 + hh
                        rr = W1[:, hh, 516:1028].bitcast(BF16).rearrange("p (a r l) -> p a r l", a=2, l=128)
                        for a_ in range(2):
                            tt(("pool" if a_ == 0 else "dve"), rr[:, a_], bc(cbf[:, TRI:TRI + 1, :], [128, 4, 128]),
                               bc(dahl[:, c, a_, h0:h0 + 4].unsqueeze(2), [128, 4, 128]), ALU.mult, ["cbf", ("dahl", c)], kW1(hh, (2,)))
                        for a_ in range(2):
                            mm(ps[hh][:], cbf[:, UM, :], rr[:, a_].rearrange("p r l -> p (r l)"), a_ == 0, a_ == 1,
                               kW1(hh, (2,)) + ["cbf"], [("ps", hh)])
                        d3 = dec[di].rearrange("p (r l) -> p r l", l=128)
                        act(dec[di], ps[hh][:], AF.Exp, [("ps", hh)], [("dec", di)])
                        tt("dve", d3, d3, bc(CBm[:, g * 128:(g + 1) * 128].unsqueeze(1), [128, 4, 128]), ALU.mult,
                           [("dec", di), "CBm"], [("dec", di)])

                stageA(0)
                for g in range(4):
                    if g + 1 < 4:
                        stageA(g + 1)
                    dg = (g % 2) * 2
                    bpy, bpo, bst = (3, 6, 7) if g % 2 == 0 else (2, 5, 4)
                    for i in range(4):
                        kc = g * 4 + i
                        mm(ps[bpy][:, i * 128:(i + 1) * 128], xsT[:, kc, cs_], Ddg[:, kc, :], i == 0, False,
                           [kxs[kc], ("Ddg",)], [("ps", bpy)])
                    for r in range(8):
                        h = g * 8 + r
                        mm(ps[bpy][:, r * 64:(r + 1) * 64], dec[dg + r // 4][:, (r % 4) * 128:(r % 4 + 1) * 128], xdt[:, h * 64:(h + 1) * 64],
                           False, r == 7, [("dec", dg + r // 4), ("xdt", h // 16)], [("ps", bpy)])
                    mm(ps[bpo][:], BCT[:, 4 + g, cs_], Sbf[:, g * 512:(g + 1) * 512], True, True, [kA3[4 + g]] + KSBF, [("ps", bpo)])
                    i2 = g % 2
                    yw = W1[:, i2, 1028:1540]
                    yg = W1[:, i2, 0:512]
                    junk = yw
                    tt("dve", yw.rearrange("p (h d) -> p h d", d=64), ps[bpo][:].rearrange("p (h d) -> p h d", d=64),
                       bc(ecs[:, g * 8:(g + 1) * 8].unsqueeze(2), [128, 8, 64]), ALU.mult, [("ps", bpo), "ex"], kW1(i2, (3,)))
                    tt("dve", yw, yw, ps[bpy][:], ALU.add, kW1(i2, (3,)) + [("ps", bpy)], kW1(i2, (3,)))
                    tt("pool", yg, yw, ztm[:, c, g * 512:(g + 1) * 512], ALU.mult, kW1(i2, (3,)) + kz, kW1(i2, (0,)))
                    A("act", lambda e, yg=yg, junk=junk, g=g: e.activation(out=junk, in_=yg, func=AF.Square, accum_out=ssq[:, g:g + 1]),
                      kW1(i2, (0,)), kW1(i2, (3,)) + [("ssq", g)])
                    ts("dve", rsg[:, g:g + 1], ssq[:, g:g + 1], 1.0 / 512.0, RMS_EPS, ALU.mult, ALU.add, [("ssq", g)], [("rsg", g)])
                    act(rsg[:, g:g + 1], rsg[:, g:g + 1], AF.Ln, [("rsg", g)], [("rsg", g)])
                    act(rsg[:, g:g + 1], rsg[:, g:g + 1], AF.Exp, [("rsg", g)], [("rsg", g)], scale=-0.5)
                    act(yn[:, g * 512:(g + 1) * 512], yg, AF.Identity, kW1(i2, (0,)) + [("rsg", g)], [("yn", g)], scale=rsg[:, g:g + 1])
                    mm(ps[bst][:], Btm[:, g * 128:(g + 1) * 128], xdt2[:, g * 512:(g + 1) * 512], True, True,
                       ["Btm", ("xdt2", g // 2)], [("ps", bst)])
                    Sg = Sst[:, li, g * 512:(g + 1) * 512]
                    tt("pool", Sg.rearrange("p (h d) -> p h d", d=64), Sg.rearrange("p (h d) -> p h d", d=64),
                       bc(cdb[:, g * 8:(g + 1) * 8].unsqueeze(2), [128, 8, 64]), ALU.mult, [("Sst", li), "ex"], [("Sst", li)])
                    tt("dve", Sg, Sg, ps[bst][:], ALU.add, [("Sst", li), ("ps", bst)], [("Sst", li)])
                    cp("act", Sbf[:, g * 512:(g + 1) * 512], Sg, [("Sst", li)], KSBF)
                for rnd in range(2):
                    for j in range(8):
                        blk = rnd * 8 + j
                        tp(pb16[:, j * 128:(j + 1) * 128], yn[:, blk * 128:(blk + 1) * 128], cbf[:, IDENT, :],
                           [("yn", blk // 4), "cbf"], [("ps", 4)])
                    og = O_NG + rnd * 8
                    tt("dve", yT[:, rnd * 8:(rnd + 1) * 8, cs_], pb16.rearrange("p (k l) -> p k l", l=128),
                       bc(pc[:, li, og:og + 8].unsqueeze(2), [128, 8, 128]), ALU.mult, [("ps", 4), "pc"], kA1[rnd * 8:(rnd + 1) * 8])
            dbg("yT", yT.rearrange("p a t -> p (a t)"), kA1, row0=dbgrow)
            if not BIG:
                sg_branch()
            for gi in range(2):
                proj_fm(li, "gb%d" % gi, xb, [KXB], 8, 512, 4,
                        lambda j, p_, pk: act(sgt[:, j, :], p_, AF.Sigmoid, [pk], [ksgt(j)]))
                for pi in range(2):
                    def ev2(jj, p_, pk, gi=gi, pi=pi):
                        blk = (gi * 2 + pi) * 2 + jj
                        j = pi * 2 + jj
                        tmp = W1[:, jj, 1028:1028 + T]
                        tt("dve", tmp, p_, sgt[:, j, :], ALU.mult, [pk, ksgt(j)], kW1(jj, (3,)))
                        tt("pool", mg[:, blk, :], tmp, mg[:, blk, :], ALU.add, kW1(jj, (3,)) + [kmg[blk]], [kmg[blk]])
                    proj_fm(li, "pb%d" % (gi * 2 + pi), yT, kA1, 16, 256, 2, ev2)
            for gi in range(2):
                proj_fm(li, "mo%d" % gi, mg, kmg, 8, 512, 4,
                        lambda j, p_, pk, gi=gi: stt(xt[:, gi * 4 + j, :], xt[:, gi * 4 + j, :], ALPHA, p_, ALU.mult, ALU.add,
                                                     [(KX[0], gi * 4 + j), pk], [(KX[0], gi * 4 + j)]))
            layer_norm(li, 0)
            dbg("x1", xt[:].rearrange("p a t -> p (a t)"), kxt(), row0=dbgrow)

        def xattn(li):
            qT = A1[:, 0:8, :]
            ET = A1[:, 8:16, :]
            oT = A3
            for gi in range(2):
                proj_fm(li, "xq%d" % gi, xb, [KXB], 8, 512, 4,
                        lambda j, p_, pk, gi=gi: act(qT[:, gi * 4 + j, :], p_, AF.Identity, [pk], [kA1[gi * 4 + j]], scale=0.0625))
            rden = lnst[:, 0, :]
            for hd in range(4):
                for mc in range(2):
                    b = nextps()
                    for dd in range(2):
                        mm(ps[b][:, 0:T], KT[:, li, 2 * hd + dd, mc * 128:(mc + 1) * 128], qT[:, 2 * hd + dd, :], dd == 0, dd == 1,
                           [("KT", li), kA1[2 * hd + dd]], [("ps", b)])
                    act(ET[:, hd * 2 + mc, :], ps[b][:, 0:T], AF.Exp, [("ps", b)], [kA1[8 + hd * 2 + mc]])
                for mc in range(2):
                    mm(ps[6][:, 0:T], cbf[:, ONES, :], ET[:, hd * 2 + mc, :], mc == 0, mc == 1, ["cbf", kA1[8 + hd * 2 + mc]], [("ps", 6)])
                act(rden, ps[6][:, 0:T], AF.Ln, [("ps", 6)], ["ln0"])
                act(rden, rden, AF.Exp, ["ln0"], ["ln0"], scale=-1.0)
                for dd in range(2):
                    b = nextps()
                    for mc in range(2):
                        mm(ps[b][:, 0:T], Vt[:, li, mc, (2 * hd + dd) * 128:(2 * hd + dd + 1) * 128], ET[:, hd * 2 + mc, :],
                           mc == 0, mc == 1, [("Vt", li), kA1[8 + hd * 2 + mc]], [("ps", b)])
                    tt("dve", oT[:, 2 * hd + dd, :], ps[b][:, 0:T], rden, ALU.mult, [("ps", b), "ln0"], [kA3[2 * hd + dd]])
            for gi in range(2):
                proj_fm(li, "xo%d" % gi, oT, kA3, 8, 512, 4,
                        lambda j, p_, pk, gi=gi: stt(xt[:, gi * 4 + j, :], xt[:, gi * 4 + j, :], ALPHA, p_, ALU.mult, ALU.add,
                                                     [(KX[0], gi * 4 + j), pk], [(KX[0], gi * 4 + j)]))
            layer_norm(li, 1)

        def ffn(li):
            hT = A2[:, 0:NKF, :]
            for gi in range(11):
                wt_, wk = wload(li, "fi%d" % gi)
                wv = w3(wt_, 512)
                for jj in range(2):
                    bg = nextps()
                    for kc in range(8):
                        mm(ps[bg][:, 0:T], wv[:, kc, jj * 128:(jj + 1) * 128], xb[:, kc, :], kc == 0, kc == 7, [wk, KXB], [("ps", bg)])
                    bu = nextps()
                    for kc in range(8):
                        mm(ps[bu][:, 0:T], wv[:, kc, (2 + jj) * 128:(3 + jj) * 128], xb[:, kc, :], kc == 0, kc == 7, [wk, KXB], [("ps", bu)])
                    tmp = lnst[:, jj, :]
                    act(tmp, ps[bg][:, 0:T], AF.Silu, [("ps", bg)], ["ln%d" % jj])
                    tt("dve", hT[:, gi * 2 + jj, :], ps[bu][:, 0:T], tmp, ALU.mult, [("ps", bu), "ln%d" % jj], [kA2[gi * 2 + jj]])
            for blk in range(8):
                wt_, wk = wload(li, "fo%d" % blk)
                wv = wt_[:, 0:NKF * 128].rearrange("p (k c) -> p k c", c=128)
                b = nextps()
                for kc in range(NKF):
                    mm(ps[b][:, 0:T], wv[:, kc, :], hT[:, kc, :], kc == 0, kc == NKF - 1, [wk, kA2[kc]], [("ps", b)])
                stt(xt[:, blk, :], xt[:, blk, :], ALPHA, ps[b][:, 0:T], ALU.mult, ALU.add, [(KX[0], blk), ("ps", b)], [(KX[0], blk)])
            layer_norm(li, 2)

        for s in range(NSEQ):
            if "xattn" in cfg.stages:
                mem_phase(s)
            for li in range(NL):
                A("pool", lambda e, li=li: e.memset(Sst[:, li, :], 0.0), [], [("Sst", li)])
                A("pool", lambda e, li=li: e.memset(halo[:, li], 0.0), [("halo", li, b_) for b_ in range(24)], [("halo", li, b_) for b_ in range(24)])
            def xload(ti_):
                row_ = (s * NT + ti_) * 128
                buf = xts[ti_ % nxb]
                A("pool", lambda e: e.dma_start(out=buf[:].rearrange("p a t -> p (a t)"), in_=xin[row_:row_ + 128, :]),
                  r=[], w=[("xt%d" % (ti_ % nxb), k_) for k_ in range(8)], dma="x%d" % (ti_ % nxb), cost=6000.0)
            if nxb == 2:
                xload(0)
            for ti in range(NT):
                row = (s * NT + ti) * 128
                xt = xts[ti % nxb]
                KX[0] = "xt%d" % (ti % nxb)
                if nxb == 1:
                    xload(ti)
                if nxb == 2 and ti + 1 < NT:
                    xload(ti + 1)
                cp("act", xb[:], xt[:], kxt(), [KXB])
                for li in range(NL):
                    if "mixer" in cfg.stages:
                        mixer(li, (s * NT + ti) * 128)
                    if "xattn" in cfg.stages:
                        xattn(li)
                    if "ffn" in cfg.stages:
                        ffn(li)
                A("pool", lambda e, row=row, xt=xt: e.dma_start(out=yout[row:row + 128, :], in_=xt[:].rearrange("p a t -> p (a t)")),
                  r=kxt(), w=[("y", row)], dma="y%d" % (ti % nxb), cost=6000.0)
        allout = [("y", (s * NT + ti) * 128) for s in range(NSEQ) for ti in range(NT)]
        allout += [k for k in P.last_writer if isinstance(k, tuple) and k[0] == "dbg"]
        A("sp", lambda e: e.nop(), r=allout, w=[])

        if cfg.sched:
            P.schedule()
            if cfg.dbg is not None and "verbose" in cfg.stages:
                print("sched est_ns", P.est_ns, {e: (len(P.ops[e]), round(sum(o.cost for o in P.ops[e] if not o.is_dma))) for e in ENGS})
        P.finalize()
        with nc.Block() as block:
            @block.sync
            def _(e):
                P.emit("sp", e, sems, chans)

            @block.scalar
            def _(e):
                P.emit("act", e, sems, chans)

            @block.vector
            def _(e):
                P.emit("dve", e, sems, chans)

            @block.gpsimd
            def _(e):
                P.emit("pool", e, sems, chans)

            @block.tensor
            def _(e):
                P.emit("pe", e, sems, chans)
    return nc


def _kgroup(Wm, c0, ncol):
    K = Wm.shape[0]
    blk = Wm[:, c0:c0 + ncol].reshape(K // 128, 128, ncol).transpose(1, 0, 2).reshape(128, -1)
    out = np.zeros((128, GSZ), np.float32)
    out[:, :blk.shape[1]] = blk
    return out


def layout_weights(inp, layers):
    allg = []
    for l in layers:
        w_in = inp["w_in"][l]
        g = {}
        for i in range(4):
            g["kv%d" % i] = _kgroup(inp["w_xkv"][l], i * 512, 512)
        for i in range(2):
            g["u%d" % i] = _kgroup(w_in, i * 512, 512)
            g["v%d" % i] = _kgroup(w_in, 1024 + i * 512, 512)
            g["ga%d" % i] = _kgroup(w_in, 7200 + i * 512, 512)
            g["gb%d" % i] = _kgroup(w_in, 8224 + i * 512, 512)
            g["pa%d" % i] = _kgroup(inp["p_a"][l], i * 512, 512)
            g["mo%d" % i] = _kgroup(inp["w_mix_o"][l], i * 512, 512)
            g["xq%d" % i] = _kgroup(inp["w_xq"][l], i * 512, 512)
            g["xo%d" % i] = _kgroup(inp["w_xo"][l], i * 512, 512)
        for i in range(4):
            g["z%d" % i] = _kgroup(w_in, 2048 + i * 512, 512)
            g["pb%d" % i] = _kgroup(inp["p_b"][l], i * 256, 256)
        for i in range(6):
            g["xbc%d" % i] = _kgroup(w_in, 4096 + i * 512, 512)
        wfi = inp["w_ffn_in"][l]
        for i in range(11):
            cols = np.concatenate([np.arange(i * 256, i * 256 + 256), FFH + np.arange(i * 256, i * 256 + 256)])
            g["fi%d" % i] = _kgroup(wfi[:, cols], 0, 512)
        for i in range(8):
            g["fo%d" % i] = _kgroup(inp["w_ffn_out"][l], i * 128, 128)
        allg += [g[n] for n in GROUPS]
    return np.ascontiguousarray(np.stack(allg).reshape(-1, GSZ))


def layout_small(inp, layers):
    NL = len(layers)
    ident = np.eye(128, dtype=np.float32)
    k = np.arange(128)
    tri = (k[:, None] <= k[None, :]).astype(np.float32)
    um = (k[:, None] > k[None, :]).astype(np.float32)
    ones = np.ones((128, 128), np.float32)
    cst = np.concatenate([ident, tri, um, ones, ones / 1024.0], axis=1)
    pcs, prs, wdts, sgws = [], [], [], []
    for l in layers:
        cols = []
        cols.append(inp["ln_g"][l].reshape(3, 8, 128).transpose(2, 0, 1).reshape(128, 24))
        cols.append(inp["ln_b"][l].reshape(3, 8, 128).transpose(2, 0, 1).reshape(128, 24))
        cols.append(inp["sg_ln_g"][l].reshape(8, 128).T)
        cols.append(inp["sg_ln_b"][l].reshape(8, 128).T)
        cols.append(inp["conv_w"][l].reshape(4, 24, 128).transpose(2, 1, 0).reshape(128, 96))
        cols.append(inp["conv_b"][l].reshape(24, 128).T)
        cols.append(np.repeat(inp["d_skip"][l], 64).reshape(16, 128).T)
        cols.append(inp["ssm_norm_g"][l].reshape(16, 128).T)
        pcs.append(np.concatenate(cols, axis=1))
        prs.append(np.concatenate([inp["dt_bias"][l], inp["a_log"][l], inp["sg_b"][l].reshape(-1)]))
        wdts.append(inp["w_in"][l][:, 7168:7200].reshape(8, 128, 32).transpose(1, 0, 2).reshape(128, 256))
        sgws.append(inp["sg_w"][l].transpose(2, 0, 1).reshape(128, 1024))
    pcol = np.concatenate(pcs, axis=1).astype(np.float32)
    prow = np.concatenate(prs + [inp["mem_ln_g"], inp["mem_ln_b"]])[None, :].astype(np.float32)
    return {"cst": np.ascontiguousarray(cst), "pcol": np.ascontiguousarray(pcol), "prow": np.ascontiguousarray(prow),
            "wdt": np.ascontiguousarray(np.concatenate(wdts, axis=1)), "sgw": np.ascontiguousarray(np.concatenate(sgws, axis=1))}


def layout_x(xs, nt, T):
    nseq = xs.shape[0]
    a = xs.reshape(nseq, nt, T, 8, 128).transpose(0, 1, 4, 3, 2)
    return np.ascontiguousarray(a.reshape(nseq * nt * 128, 8 * T))


def unlayout_x(y, nseq, nt, T):
    a = y.reshape(nseq, nt, 128, 8, T).transpose(0, 1, 4, 3, 2)
    return np.ascontiguousarray(a.reshape(nseq, nt * T, D))


def kernel(**inputs):
    inp = {k: np.asarray(v) for k, v in inputs.items()}
    cfg = Cfg(nseq=2, nt=SEQ // 512, nch=4, layers=(0, 1))
    nc = build(cfg)
    wall = layout_weights(inp, cfg.layers)
    small = layout_small(inp, cfg.layers)
    in_maps = []
    for c in range(NCORES):
        m = {"xT": layout_x(inp["x"][2 * c:2 * c + 2], cfg.nt, cfg.T),
             "mem": np.ascontiguousarray(inp["mem"][2 * c:2 * c + 2].reshape(-1, D)), "wall": wall}
        m.update(small)
        in_maps.append(m)
    res = run_bass_kernel_spmd(nc, in_maps, core_ids=list(range(NCORES)))
    out = np.concatenate([unlayout_x(r["yT"], 2, cfg.nt, cfg.T) for r in res.results], axis=0)
    return out.astype(np.float32)
```

```python
import contextlib
import numpy as np
import concourse.bass as bass
import concourse.mybir as mybir
from concourse.bass_utils import run_bass_kernel_spmd

F32 = mybir.dt.float32
BF16 = mybir.dt.bfloat16
AF = mybir.ActivationFunctionType
ALU = mybir.AluOpType

D = 1024
SEQ = 2048
BATCH = 16
DEPTH = 2
MEM = 256
NCORES = 8
ALPHA = float((2 * DEPTH) ** 0.25)
LN_EPS = 1e-5
RMS_EPS = 1e-5
FFH = 2816
NKF = FFH // 128
GSZ = 4096

GROUPS = (["kv%d" % i for i in range(4)] + ["v0", "v1", "u0", "u1", "ga0", "pa0", "ga1", "pa1"]
          + ["xbc%d" % i for i in range(6)] + ["z%d" % i for i in range(4)]
          + ["gb0", "pb0", "pb1", "gb1", "pb2", "pb3", "mo0", "mo1", "xq0", "xq1", "xo0", "xo1"]
          + ["fi%d" % i for i in range(11)] + ["fo%d" % i for i in range(8)])
GIDX = {n: i for i, n in enumerate(GROUPS)}
NG = len(GROUPS)

ENGS = ("pe", "act", "dve", "pool", "sp")


class Op:
    __slots__ = ("eng", "fn", "deps", "marked", "count", "is_dma", "chan", "chan_val", "cost", "idx", "fin", "ndep", "succ", "rdy")

    def __init__(self, eng, fn, is_dma):
        self.cost = 300.0
        self.idx = 0
        self.eng = eng
        self.fn = fn
        self.deps = {}
        self.marked = False
        self.count = 0
        self.is_dma = is_dma
        self.chan = None
        self.chan_val = 0


class Prog:
    def __init__(self, nc):
        self.nc = nc
        self.ops = {e: [] for e in ENGS}
        self.last_writer = {}
        self.readers = {}
        self.chan_count = {}
        self.chan_last = {}
        self.bulk = set()
        self.nops = 0

    def add(self, eng, fn, reads=(), writes=(), dma=None, cost=300.0):
        op = Op(eng, fn, dma is not None)
        op.cost = cost
        op.idx = self.nops
        self.nops += 1
        for k in reads:
            w = self.last_writer.get(k)
            if w is not None:
                op.deps[w] = True
        for k in writes:
            w = self.last_writer.get(k)
            if w is not None and w not in op.deps:
                op.deps[w] = False
            for r in self.readers.get(k, ()):
                if r is not op and r not in op.deps:
                    op.deps[r] = False
        for k in reads:
            self.readers.setdefault(k, []).append(op)
        for k in writes:
            self.last_writer[k] = op
            self.readers[k] = []
        if dma is not None:
            op.chan = dma
            self.chan_count[dma] = self.chan_count.get(dma, 0) + 16
            op.chan_val = self.chan_count[dma]
            if dma not in self.bulk:
                pl = self.chan_last.get(dma)
                if pl is not None:
                    op.deps[pl] = True
                self.chan_last[dma] = op
        for p, raw in op.deps.items():
            if (not p.is_dma) and self._needs_wait(op, p, raw):
                p.marked = True
        self.ops[eng].append(op)
        return op

    @staticmethod
    def _needs_wait(c, p, raw):
        if p.is_dma:
            return True
        if p.eng != c.eng:
            return True
        if c.is_dma:
            return True
        if c.eng == "pe":
            return False
        return True

    def emit(self, e, eng, sems, chan_sems):
        c = 0
        for op in self.ops[e]:
            if op.marked and not op.is_dma:
                c += 1
            op.count = c
        waited = {}
        for op in self.ops[e]:
            need = {}
            for p, raw in op.deps.items():
                if not self._needs_wait(op, p, raw):
                    continue
                if p.is_dma:
                    s, v = ("c", p.chan), (self.chan_count[p.chan] if p.chan in self.bulk else p.chan_val)
                else:
                    s, v = ("e", p.eng), p.count
                if v > need.get(s, 0):
                    need[s] = v
            for s, v in need.items():
                if waited.get(s, 0) >= v:
                    continue
                waited[s] = v
                sem = chan_sems[s[1]] if s[0] == "c" else sems[s[1]]
                eng.wait_ge(sem, v)
            ins = op.fn(eng)
            if op.is_dma:
                ins.then_inc(chan_sems[op.chan], 16)
            elif op.marked:
                ins.then_inc(sems[e], 1)

    def schedule(self, window=1500):
        import heapq
        allops = [op for e in ENGS for op in self.ops[e]]
        for op in allops:
            op.succ = []
            op.rdy = 0.0
        for op in allops:
            op.ndep = len(op.deps)
            for p in op.deps:
                p.succ.append(op)
        tfree = {e: 0.0 for e in ENGS}
        h_rdy = {e: [] for e in ENGS}
        h_idx = {e: [] for e in ENGS}
        order = {e: [] for e in ENGS}
        for op in allops:
            if op.ndep == 0:
                heapq.heappush(h_rdy[op.eng], (0.0, op.idx, op))
        done = 0
        nall = len(allops)
        base = 0
        sched = bytearray(nall)
        while done < nall:
            best = None
            for e in ENGS:
                hr, hi = h_rdy[e], h_idx[e]
                while hr and hr[0][0] <= tfree[e]:
                    _, i_, o_ = heapq.heappop(hr)
                    heapq.heappush(hi, (i_, o_))
                if hi:
                    cand = (tfree[e], 0, e)
                elif hr:
                    cand = (hr[0][0], 1, e)
                else:
                    continue
                if best is None or cand < best:
                    best = cand
            t0, kind, e = best
            if kind == 0:
                _, op = heapq.heappop(h_idx[e])
            else:
                _, _, op = heapq.heappop(h_rdy[e])
            start = max(tfree[e], op.rdy)
            if op.is_dma:
                tfree[e] = start + 60.0
                op.fin = start + op.cost
            elif e == "pe":
                tfree[e] = start + op.cost
                op.fin = start + op.cost + 150.0
            else:
                tfree[e] = start + op.cost
                op.fin = start + op.cost + 60.0
            order[e].append(op)
            done += 1
            for sx in op.succ:
                lat = 0.0 if (sx.eng == e and e == "pe") else 80.0
                if op.fin + lat > sx.rdy:
                    sx.rdy = op.fin + lat
                sx.ndep -= 1
                if sx.ndep == 0:
                    heapq.heappush(h_rdy[sx.eng], (sx.rdy, sx.idx, sx))
        self.ops = order
        self.est_ns = max(tfree.values())

    def finalize(self):
        for e in ENGS:
            c = 0
            for op in self.ops[e]:
                if op.marked and not op.is_dma:
                    c += 1
                op.count = c


def bc(ap, shape):
    return ap.to_broadcast(shape) if hasattr(ap, "to_broadcast") else ap.broadcast_to(shape)


class Cfg:
    def __init__(self, nseq=2, nt=8, nch=2, layers=(0, 1), dbg=None, stages=("mixer", "xattn", "ffn"), sched=True):
        self.nseq, self.nt, self.nch, self.layers = nseq, nt, nch, tuple(layers)
        self.T = nch * 128
        self.dbg = dbg or {}
        self.stages = stages
        self.sched = sched


def build(cfg):
    nc = bass.Bass("TRN2", target_bir_lowering=False)
    P = Prog(nc)
    P.bulk = {"cst"}
    NSEQ, NT, NCH, T = cfg.nseq, cfg.nt, cfg.nch, cfg.T
    LAY = cfg.layers
    NL = len(LAY)
    BIG = NCH >= 4

    xin = nc.dram_tensor("xT", [NSEQ * NT * 128, 8 * T], F32, kind="ExternalInput").ap()
    yout = nc.dram_tensor("yT", [NSEQ * NT * 128, 8 * T], F32, kind="ExternalOutput").ap()
    memd = nc.dram_tensor("mem", [NSEQ * 2 * 128, D], F32, kind="ExternalInput").ap()
    wall = nc.dram_tensor("wall", [NL * NG * 128, GSZ], F32, kind="ExternalInput").ap()
    wbf = nc.dram_tensor("wbf", [NL * NG * 128, GSZ], BF16, kind="Internal").ap()
    cst = nc.dram_tensor("cst", [128, 5 * 128], F32, kind="ExternalInput").ap()
    NPC = 3 * 8 * 2 + 8 * 2 + 24 * 5 + 16 * 2
    pcol = nc.dram_tensor("pcol", [128, NL * NPC], F32, kind="ExternalInput").ap()
    NPR = 32 * 2 + 1024
    prow = nc.dram_tensor("prow", [1, NL * NPR + 2 * D], F32, kind="ExternalInput").ap()
    wdt_d = nc.dram_tensor("wdt", [128, NL * 8 * 32], F32, kind="ExternalInput").ap()
    sgw_d = nc.dram_tensor("sgw", [128, NL * 8 * 128], F32, kind="ExternalInput").ap()
    dbg_d = {}
    for name, shp in cfg.dbg.items():
        dbg_d[name] = nc.dram_tensor("dbg_" + name, list(shp), F32, kind="ExternalOutput").ap()

    st = contextlib.ExitStack()
    with st:
        def sb(name, shape, dt):
            return st.enter_context(nc.sbuf_tensor(name, list(shape), dt))

        c32 = sb("c32", [128, 5, 128], F32)
        cbf = sb("cbf", [128, 5, 128], BF16)
        IDENT, TRI, UM, ONES, ODIV = 0, 1, 2, 3, 4
        pc = sb("pc", [128, NL, NPC], F32)
        O_LNG, O_LNB, O_SGG, O_SGB, O_CW, O_CB, O_DC, O_NG = 0, 24, 48, 56, 64, 160, 184, 200
        pr = sb("pr", [128, NL, 64], F32)
        wdt = sb("wdt_s", [128, NL, 8, 32], F32)
        wTm = sb("wTm", [128, NL, 8, 128], BF16)
        Esg = sb("Esg", [128, NL, 8, 128], F32)
        Ddg = sb("Ddg", [128, 16, 128], BF16)
        KT = sb("KT", [128, NL, 8, MEM], BF16)
        Vt = sb("Vt", [128, NL, 2, D], BF16)
        NSLOT = 3 if BIG else 4
        wr = [sb("wr%d" % i, [128, GSZ], BF16) for i in range(NSLOT)]
        xts = [sb("xt0", [128, 8, T], F32)] if BIG else [sb("xt0", [128, 8, T], F32), sb("xt1", [128, 8, T], F32)]
        nxb = len(xts)
        xt = xts[0]
        KX = ["xt0"]
        xb = sb("xb", [128, 8, T], BF16)
        Sst = sb("Sst", [128, NL, 2048], F32)
        halo = sb("halo", [128, NL, 24, 3], F32)
        A1 = sb("A1", [128, 16, T], BF16)
        A2 = sb("A2", [128, 32, T], BF16)
        A4 = A1 if BIG else sb("A4", [128, 16, T], BF16)
        A3 = sb("A3", [128, 8, T], BF16)
        mg = sb("mg", [128, 8, T], BF16)
        W1 = sb("W1", [128, 2, 1540], F32)
        W2 = sb("W2", [128, 2, 2048], BF16)
        W3 = sb("W3", [128, 10, 512], BF16)
        sgt = W2[:, 0, :].rearrange("p (j t) -> p j t", t=T) if BIG else sb("sgt", [128, 4, T], BF16)
        sm = sb("sm", [128, 512], F32)
        dahl = sb("dahl", [128, 4, 2, 32], BF16)
        lnst = sb("lnst", [128, 2, T], F32)
        if BIG:
            Sbf = lnst[:].rearrange("p a t -> p (a t)").bitcast(BF16)
            KSBF = ["ln0", "ln1"]
        else:
            Sbf = sb("Sbf", [128, 2048], BF16)
            KSBF = ["Sbf"]
        memgb = A2[:].rearrange("p a t -> p (a t)").bitcast(F32)[:, 0:2 * D].rearrange("p (a d) -> p a d", d=D)

        def ksgt(j):
            return ("xdt", j // 2) if BIG else ("sgt", j)
        kW2a = [("xdt", 0), ("xdt", 1)]
        kW2b = [("xdt2", 0), ("xdt2", 1)]

        ps = [st.enter_context(nc.psum_tensor("ps%d" % i, [128, 512], F32)) for i in range(8)]

        sems = {e: st.enter_context(nc.semaphore("s_" + e)) for e in ENGS}
        chan_names = (["cv%d" % i for i in range(8)] + ["cst", "sgw", "x0", "x1", "y0", "y1", "mem", "dbg"] + ["w%d" % i for i in range(NSLOT)])
        chans = {c: st.enter_context(nc.semaphore("c_" + c)) for c in chan_names}

        def kW1(i, qs):
            return sorted(set(("W1", i, (0 if q == 1 else q)) for q in qs))

        def flat(ks):
            out = []
            for k in ks:
                if isinstance(k, list):
                    out.extend(flat(k))
                else:
                    out.append(k)
            return out

        def A(eng, fn, r=(), w=(), dma=None, cost=300.0):
            return P.add(eng, fn, reads=flat(r), writes=flat(w), dma=dma, cost=cost)

        KXB = [("xb", k_) for k_ in range(8)]

        def kxt():
            return [(KX[0], k_) for k_ in range(8)]

        def fsz(ap):
            n = 1
            for d in ap.shape[1:]:
                n *= d
            return n

        ECOST = {"act": 0.75, "dve": 1.0, "pool": 2.0}

        def ecost(eng, out):
            return 220.0 + ECOST[eng] * fsz(out)

        def act(out, in_, func, r, w, **kw):
            A("act", lambda e: e.activation(out=out, in_=in_, func=func, **kw), r, w, cost=ecost("act", out) + (90.0 if kw else 0.0))

        def tt(eng, out, in0, in1, op, r, w):
            A(eng, lambda e: e.tensor_tensor(out=out, in0=in0, in1=in1, op=op), r, w, cost=ecost(eng, out))

        def ts(eng, out, in0, s1, s2, op0, op1, r, w):
            c_ = ecost(eng, out) if eng != "pool" else 3500.0
            if op1 is None:
                A(eng, lambda e: e.tensor_scalar(out=out, in0=in0, scalar1=s1, scalar2=None, op0=op0), r, w, cost=c_)
            else:
                A(eng, lambda e: e.tensor_scalar(out=out, in0=in0, scalar1=s1, scalar2=s2, op0=op0, op1=op1), r, w, cost=c_)

        def stt(out, in0, scalar, in1, op0, op1, r, w):
            A("dve", lambda e: e.scalar_tensor_tensor(out=out, in0=in0, scalar=scalar, in1=in1, op0=op0, op1=op1), r, w,
              cost=ecost("dve", out))

        def cp(eng, out, in_, r, w):
            if eng == "act":
                A("act", lambda e: e.copy(out=out, in_=in_), r, w, cost=ecost("act", out))
            else:
                A(eng, lambda e: e.tensor_copy(out=out, in_=in_), r, w, cost=ecost(eng, out))

        def mm(out, lhsT, rhs, start, stop, r, w):
            n_ = fsz(rhs)
            c_ = max(n_, 64) / 1.9 + 8.0
            if rhs.dtype == F32:
                c_ *= 4.0
            A("pe", lambda e: e.matmul(out, lhsT=lhsT, rhs=rhs, start=start, stop=stop), r, w, cost=c_)

        def tp(out, in_, ident, r, w):
            A("pe", lambda e: e.transpose(out, in_, ident), r, w, cost=90.0)

        def dbg(name, src_ap, key, row0=0):
            if name in dbg_d:
                d = dbg_d[name]
                n = src_ap.shape[0]
                A("pool", lambda e: e.dma_start(out=d[row0:row0 + n], in_=src_ap), r=key, w=[("dbg", name, row0)], dma="dbg")

        wstate = {"n": 0}

        def wload(li, gname):
            slot = wstate["n"] % NSLOT
            wstate["n"] += 1
            g = li * NG + GIDX[gname]
            A("sp", lambda e: e.dma_start(out=wr[slot][:], in_=wbf[g * 128:(g + 1) * 128, :]),
              r=[("wbf", g)], w=[("wr", slot)], dma="w%d" % slot, cost=7000.0)
            if li + 1 < NL:
                cast((li + 1) * NG + GIDX[gname], extra_reads=[("wr", slot)])
            return wr[slot], ("wr", slot)

        def w3(wt_, ncol):
            return wt_[:].rearrange("p (k c) -> p k c", c=ncol)

        ncast = {"n": 0}
        cast_done = set()

        def cast(g, extra_reads=()):
            if g in cast_done:
                return
            cast_done.add(g)
            i_ = ncast["n"]
            ncast["n"] += 1
            A("pool", lambda e: e.dma_start(out=wbf[g * 128:(g + 1) * 128, :], in_=wall[g * 128:(g + 1) * 128, :],
                                            max_dma_last_dim=8192), r=list(extra_reads), w=[("wbf", g)], dma="cv%d" % (i_ % 8), cost=12000.0)

        for i in range(NG):
            cast(i)
        A("sp", lambda e: e.dma_start(out=c32[:].rearrange("p a b -> p (a b)"), in_=cst), w=["c32"], dma="cst")
        A("sp", lambda e: e.dma_start(out=pc[:].rearrange("p a b -> p (a b)"), in_=pcol), w=["pc"], dma="cst")
        for li_ in range(NL):
            A("sp", lambda e, li_=li_: e.dma_start(out=pr[:, li_, :], in_=prow[:, li_ * NPR:li_ * NPR + 64].partition_broadcast(128)),
              w=(["pr"] if li_ == NL - 1 else [("prx", li_)]), dma="cst")
        A("sp", lambda e: e.dma_start(out=wdt[:].rearrange("p a b c -> p (a b c)"), in_=wdt_d), w=["wdt"], dma="cst")
        cp("dve", cbf[:], c32[:], ["c32"], ["cbf"])
        for li in range(NL):
            act(pr[:, li, 32:64], pr[:, li, 32:64], AF.Exp, ["pr"], ["pr"])
            ts("dve", pr[:, li, 32:64], pr[:, li, 32:64], -1.0, None, ALU.mult, None, ["pr"], ["pr"])
        sgbt = A2[:].rearrange("p a t -> p (a t)").bitcast(F32)[:, 0:NL * 1024].rearrange("p (l f) -> p l f", f=1024)
        for li_ in range(NL):
            A("sp", lambda e, li_=li_: e.dma_start(out=sgbt[:, li_, :],
                                                   in_=prow[:, li_ * NPR + 64:(li_ + 1) * NPR].partition_broadcast(128)),
              w=["sgbt"] + [("A2", i_) for i_ in range(32)], dma="sgw")
        for li in range(NL):
            w1v = W1[:, 0, 0:1024].rearrange("p (g t) -> p g t", t=128)
            A("sp", lambda e, li=li: e.dma_start(out=W1[:, 0, 0:1024], in_=sgw_d[:, li * 1024:(li + 1) * 1024]),
              r=[], w=[*kW1(0, (0, 2))], dma="sgw")
            tt("dve", w1v, w1v, bc(c32[:, TRI:TRI + 1, :], [128, 8, 128]), ALU.mult, [*kW1(0, (0, 2)), "c32"], [*kW1(0, (0, 2))])
            cp("pool", wTm[:, li], w1v, [*kW1(0, (0, 2))], [("wTm", li)])
            for hh in range(2):
                mm(ps[hh][:], c32[:, ONES, :], W1[:, 0, hh * 512:(hh + 1) * 512], True, True, [*kW1(0, (0, 2)), "c32"], [("ps", hh)])
                e_v = Esg[:, li, hh * 4:(hh + 1) * 4, :]
                tt("dve", e_v, ps[hh][:].rearrange("p (g t) -> p g t", t=128),
                   bc(pc[:, li, O_SGB + hh * 4:O_SGB + hh * 4 + 4].unsqueeze(2), [128, 4, 128]), ALU.mult,
                   [("ps", hh), "pc"], [("Esg", li)])
                tt("pool", e_v, e_v, sgbt[:, li, hh * 512:(hh + 1) * 512].rearrange("p (g t) -> p g t", t=128),
                   ALU.add, [("Esg", li), "sgbt"] + [("A2", i_) for i_ in range(32)], [("Esg", li)])
        dbg("Esg", Esg[:, 0].rearrange("p g t -> p (g t)"), [("Esg", 0)])

        eps_ln = LN_EPS
        psrot = {"n": 0}

        def nextps():
            b = psrot["n"] % 4
            psrot["n"] += 1
            return b

        def layer_norm(li, which):
            rbf = A1[:, 0:8, :]
            rsq = A1[:, 8:16, :]
            kA2 = [("A1", i) for i in range(16)]
            for kc in range(8):
                kk = (KX[0], kc)
                cp("dve", rbf[:, kc, :], xt[:, kc, :], [kk], [kA2[kc]])
                act(rsq[:, kc, :], xt[:, kc, :], AF.Square, [kk], [kA2[8 + kc]])
            for kc in range(8):
                mm(ps[4][:, 0:T], cbf[:, ODIV, :], rbf[:, kc, :], kc == 0, kc == 7, [kA2[kc], "cbf"], [("ps", 4)])
            for kc in range(8):
                mm(ps[5][:, 0:T], cbf[:, ODIV, :], rsq[:, kc, :], kc == 0, kc == 7, [kA2[8 + kc], "cbf"], [("ps", 5)])
            mean, var, rstd = lnst[:, 0, :], lnst[:, 1, :], lnst[:, 1, :]
            cp("act", mean, ps[4][:, 0:T], [("ps", 4)], ["ln0"])
            act(var, ps[4][:, 0:T], AF.Square, [("ps", 4)], ["ln1"])
            tt("dve", var, ps[5][:, 0:T], var, ALU.subtract, [("ps", 5), "ln1"], ["ln1"])
            ts("dve", var, var, eps_ln, None, ALU.add, None, ["ln1"], ["ln1"])
            act(rstd, var, AF.Ln, ["ln1"], ["ln1"])
            act(rstd, rstd, AF.Exp, ["ln1"], ["ln1"], scale=-0.5)
            for kc in range(8):
                o = O_LNG + which * 8 + kc
                ob = O_LNB + which * 8 + kc
                kk = (KX[0], kc)
                tt("pool", xt[:, kc, :], xt[:, kc, :], mean, ALU.subtract, [kk, "ln0"], [kk])
                stt(xt[:, kc, :], xt[:, kc, :], pc[:, li, o:o + 1], rstd, ALU.mult, ALU.mult, [kk, "ln1", "pc"], [kk])
                act(xb[:, kc, :], xt[:, kc, :], AF.Identity, [kk, "pc"], [("xb", kc)], bias=pc[:, li, ob:ob + 1])
                act(xt[:, kc, :], xt[:, kc, :], AF.Identity, [kk, "pc"], [kk], bias=pc[:, li, ob:ob + 1])

        def mem_phase(s):
            mraw = W1[:].rearrange("p a b -> p (a b)")[:, 0:2048].rearrange("p (c d) -> p c d", d=D)
            kmgb = [("A2", i_) for i_ in range(32)]
            A("pool", lambda e: e.dma_start(out=memgb.rearrange("p a b -> p (a b)"),
                                            in_=prow[:, NL * NPR:NL * NPR + 2 * D].partition_broadcast(128)), r=[], w=kmgb, dma="mem")
            kW1m = kW1(0, (0, 2, 3)) + kW1(1, (0,))
            A("pool", lambda e: e.dma_start(out=mraw, in_=memd[s * 256:(s + 1) * 256, :].rearrange("(c p) d -> p c d", p=128)),
              r=[], w=kW1m, dma="mem")
            st6 = sm[:, 0:24].rearrange("p (c h k) -> p c h k", c=2, h=2)
            mv = sm[:, 24:28].rearrange("p (c k) -> p c k", k=2)
            for c in range(2):
                for h in range(2):
                    A("dve", lambda e, c=c, h=h: e.bn_stats(out=st6[:, c, h, :], in_=mraw[:, c, h * 512:(h + 1) * 512]), kW1m, ["sm"])
                A("dve", lambda e, c=c: e.bn_aggr(out=mv[:, c, :], in_=st6[:, c].rearrange("p h k -> p (h k)")), ["sm"], ["sm"])
            rs = sm[:, 28:30]
            ts("dve", rs, mv[:, :, 1], LN_EPS, None, ALU.add, None, ["sm"], ["sm"])
            act(rs, rs, AF.Ln, ["sm"], ["sm"])
            act(rs, rs, AF.Exp, ["sm"], ["sm"], scale=-0.5)
            mnb = W2[:, 0, :].rearrange("p (c d) -> p c d", d=D)
            for c in range(2):
                ts("dve", mraw[:, c, :], mraw[:, c, :], mv[:, c, 0:1], rs[:, c:c + 1], ALU.subtract, ALU.mult, kW1m + ["sm"], kW1m)
                tt("pool", mraw[:, c, :], mraw[:, c, :], memgb[:, 0, :], ALU.mult, kW1m + kmgb, kW1m)
                tt("dve", mnb[:, c, :], mraw[:, c, :], memgb[:, 1, :], ALU.add, kW1m + kmgb, kW2a)
            memT = W2[:, 1, :].rearrange("p (k m) -> p k m", m=MEM)
            for c in range(2):
                pb16 = ps[4 + c][:].bitcast(BF16)
                for kc in range(8):
                    tp(pb16[:, kc * 128:(kc + 1) * 128], mnb[:, c, kc * 128:(kc + 1) * 128], cbf[:, IDENT, :],
                       kW2a + ["cbf"], [("ps", 4 + c)])
                cp("act", memT[:, :, c * 128:(c + 1) * 128], pb16.rearrange("p (k m) -> p k m", m=128),
                   [("ps", 4 + c)], kW2b)
            for li in range(NL):
                for gi in range(2):
                    wt_, wk = wload(li, "kv%d" % gi)
                    wv = w3(wt_, 512)
                    for j in range(4):
                        b = nextps()
                        for kc in range(8):
                            mm(ps[b][:, 0:MEM], wv[:, kc, j * 128:(j + 1) * 128], memT[:, kc, :], kc == 0, kc == 7,
                               [wk] + kW2b, [("ps", b)])
                        cp("act", KT[:, li, gi * 4 + j, :], ps[b][:, 0:MEM], [("ps", b)], [("KT", li)])
                for gi in range(2):
                    wt_, wk = wload(li, "kv%d" % (2 + gi))
                    wv = w3(wt_, 512)
                    for c in range(2):
                        b = nextps()
                        for kc in range(8):
                            mm(ps[b][:], memT[:, kc, c * 128:(c + 1) * 128], wv[:, kc, :], kc == 0, kc == 7,
                               [wk] + kW2b, [("ps", b)])
                        cp("dve", Vt[:, li, c, gi * 512:(gi + 1) * 512], ps[b][:], [("ps", b)], [("Vt", li)])

        def proj_fm(li, gname, rhs3, rkeys, nk, ncol, nblk, evac):
            wt_, wk = wload(li, gname)
            wv = w3(wt_, ncol)
            for j in range(nblk):
                b = nextps()
                for kc in range(nk):
                    mm(ps[b][:, 0:T], wv[:, kc, j * 128:(j + 1) * 128], rhs3[:, kc, :], kc == 0, kc == nk - 1,
                       [wk] + rkeys, [("ps", b)])
                evac(j, ps[b][:, 0:T], ("ps", b))

        kA1 = [("A1", i) for i in range(16)]
        kA2 = [("A2", i) for i in range(32)]
        kA3 = [("A3", i) for i in range(8)]
        kA4 = kA1 if BIG else [("A4", i) for i in range(16)]
        kmg = [("mg", i) for i in range(8)]

        def mixer(li, dbgrow):
            def sg_branch():
                uT = A4[:, 0:8, :]
                vn = A4[:, 8:16, :].rearrange("p a t -> p (a t)").rearrange("p (c f) -> p c f", f=1024)
                st6 = sm[:, 0:NCH * 12].rearrange("p (c h k) -> p c h k", c=NCH, h=2)
                mv = sm[:, 48:48 + NCH * 2].rearrange("p (c k) -> p c k", k=2)
                rs = sm[:, 64:64 + NCH]
                for gi in range(2):
                    wt_, wk = wload(li, "v%d" % gi)
                    wv = w3(wt_, 512)
                    for c in range(NCH):
                        b = nextps()
                        for kc in range(8):
                            mm(ps[b][:], xb[:, kc, c * 128:(c + 1) * 128], wv[:, kc, :], kc == 0, kc == 7, [wk, KXB], [("ps", b)])
                        act(vn[:, c, gi * 512:(gi + 1) * 512], ps[b][:], AF.Gelu, [("ps", b)], kA4[8:16])
                        A("dve", lambda e, c=c, gi=gi: e.bn_stats(out=st6[:, c, gi, :], in_=vn[:, c, gi * 512:(gi + 1) * 512]),
                          kA4[8:16], ["sm"])
                for c in range(NCH):
                    A("dve", lambda e, c=c: e.bn_aggr(out=mv[:, c, :], in_=st6[:, c].rearrange("p h k -> p (h k)")), ["sm"], ["sm"])
                ts("dve", rs, mv[:, :, 1], LN_EPS, None, ALU.add, None, ["sm"], ["sm"])
                act(rs, rs, AF.Ln, ["sm"], ["sm"])
                act(rs, rs, AF.Exp, ["sm"], ["sm"], scale=-0.5)
                nmr = sm[:, 72:72 + NCH]
                tt("dve", nmr, mv[:, :, 0], rs, ALU.mult, ["sm"], ["sm"])
                ts("dve", nmr, nmr, -1.0, None, ALU.mult, None, ["sm"], ["sm"])
                for c in range(NCH):
                    act(vn[:, c, :], vn[:, c, :], AF.Identity, kA4[8:16] + ["sm"], kA4[8:16], scale=rs[:, c:c + 1], bias=nmr[:, c:c + 1])
                for gi in range(2):
                    proj_fm(li, "u%d" % gi, xb, [KXB], 8, 512, 4,
                            lambda j, p_, pk, gi=gi: act(uT[:, gi * 4 + j, :], p_, AF.Gelu, [pk], [kA4[gi * 4 + j]]))
                for c in range(NCH):
                    for g in range(8):
                        b = 6 + g // 4
                        mm(ps[b][:, (g % 4) * 128:(g % 4 + 1) * 128], vn[:, c, g * 128:(g + 1) * 128], wTm[:, li, g, :],
                           True, True, kA4[8:16] + [("wTm", li)], [("ps", b)])
                    t1 = W1[:, 0, 0:1024].rearrange("p (g t) -> p g t", t=128)
                    for hh in range(2):
                        tt("dve", t1[:, hh * 4:(hh + 1) * 4, :], ps[6 + hh][:].rearrange("p (g t) -> p g t", t=128),
                           bc(pc[:, li, O_SGG + hh * 4:O_SGG + hh * 4 + 4].unsqueeze(2), [128, 4, 128]), ALU.mult,
                           [("ps", 6 + hh), "pc"], [*kW1(0, (0, 2))])
                    tt("pool", t1, t1, Esg[:, li], ALU.add, [*kW1(0, (0, 2)), ("Esg", li)], [*kW1(0, (0, 2))])
                    tt("dve", uT[:, :, c * 128:(c + 1) * 128], t1, uT[:, :, c * 128:(c + 1) * 128], ALU.mult,
                       [*kW1(0, (0, 2))] + kA4[0:8], kA4[0:8])
                dbg("saT", uT.rearrange("p a t -> p (a t)"), kA4[0:8], row0=dbgrow)
                for gi in range(2):
                    proj_fm(li, "ga%d" % gi, xb, [KXB], 8, 512, 4,
                            lambda j, p_, pk: act(sgt[:, j, :], p_, AF.Sigmoid, [pk], [ksgt(j)]))
                    proj_fm(li, "pa%d" % gi, uT, kA4[0:8], 8, 512, 4,
                            lambda j, p_, pk, gi=gi: tt("dve", mg[:, gi * 4 + j, :], p_, sgt[:, j, :], ALU.mult,
                                                        [pk, ksgt(j)], [kmg[gi * 4 + j]]))
                dbg("m1", mg[:].rearrange("p a t -> p (a t)"), kmg, row0=dbgrow)


            if BIG:
                sg_branch()
            for kc in range(16):
                act(Ddg[:, kc, :], cbf[:, IDENT, :], AF.Identity, ["cbf", "pc"], [("Ddg",)], scale=pc[:, li, O_DC + kc:O_DC + kc + 1])
            xsT = A2[:, 16:32, :]
            BCT = A3
            ztm = A2[:, 0:16, :].rearrange("p a t -> p (a t)").rearrange("p (c f) -> p c f", f=2048)
            kxs = kA2[16:32]
            kz = kA2[0:16]
            for gi in range(6):
                def ev(j, p_, pk, gi=gi):
                    blk = gi * 4 + j
                    i = blk % 2
                    raw = W1[:, i, 0:T + 3]
                    acc = W1[:, i, 516:516 + T]
                    kr, ka = kW1(i, (0, 1)), kW1(i, (2,))
                    cp("pool", raw[:, 0:3], halo[:, li, blk, :], [("halo", li, blk)], kr)
                    act(raw[:, 3:3 + T], p_, AF.Identity, [pk], kr)
                    cp("pool", halo[:, li, blk, :], raw[:, T:T + 3], kr, [("halo", li, blk)])
                    ow = O_CW + blk * 4
                    act(acc, raw[:, 0:T], AF.Identity, kr + ["pc"], ka, scale=pc[:, li, ow:ow + 1],
                        bias=pc[:, li, O_CB + blk:O_CB + blk + 1])
                    for k in range(1, 4):
                        stt(acc, raw[:, k:k + T], pc[:, li, ow + k:ow + k + 1], acc, ALU.mult, ALU.add, kr + ka + ["pc"], ka)
                    if blk < 16:
                        act(xsT[:, blk, :], acc, AF.Silu, ka, [kxs[blk]])
                    else:
                        act(BCT[:, blk - 16, :], acc, AF.Silu, ka, [kA3[blk - 16]])
                proj_fm(li, "xbc%d" % gi, xb, [KXB], 8, 512, 4, ev)
            for gi in range(4):
                wt_, wk = wload(li, "z%d" % gi)
                wv = w3(wt_, 512)
                for c in range(NCH):
                    b = nextps()
                    for kc in range(8):
                        mm(ps[b][:], xb[:, kc, c * 128:(c + 1) * 128], wv[:, kc, :], kc == 0, kc == 7, [wk, KXB], [("ps", b)])
                    act(ztm[:, c, gi * 512:(gi + 1) * 512], ps[b][:], AF.Silu, [("ps", b)], kz)
            for c in range(NCH):
                dtc = sm[:, 96 + c * 32:128 + c * 32]
                dac = sm[:, 224 + c * 32:256 + c * 32]
                for kc in range(8):
                    mm(ps[5][:, 0:32], xt[:, kc, c * 128:(c + 1) * 128], wdt[:, li, kc, :], kc == 0, kc == 7,
                       [kxt(), "wdt"], [("ps", 5)])
                tt("dve", dtc, ps[5][:, 0:32], pr[:, li, 0:32], ALU.add, [("ps", 5), "pr"], [("dt", c)])
                act(dtc, dtc, AF.Exp, [("dt", c)], [("dt", c)])
                act(dtc, dtc, AF.Ln, [("dt", c)], [("dt", c)], bias=1.0)
                tt("pool", dac, dtc, pr[:, li, 32:64], ALU.mult, [("dt", c), "pr"], [("da", c)])
                dhl = dahl[:, c]
                cp("dve", dhl[:, 0, :], dac, [("da", c)], [("dahl", c)])
                tt("dve", dhl[:, 1, :], dac, dhl[:, 0, :], ALU.subtract, [("da", c), ("dahl", c)], [("dahl", c)])
            cp("act", Sbf[:], Sst[:, li, :], [("Sst", li)], KSBF)
            yT = A1[:, 0:16, :]
            xdt = W2[:, 0, :]
            xdt2 = W2[:, 1, :]
            dec = [W3[:, 0, :], W3[:, 1, :], W3[:, 2, :], W3[:, 3, :]]
            CBm = W3[:, 4, :]
            Btm = W3[:, 5, :]
            yn = W3[:, 6:10, :].rearrange("p a b -> p (a b)")
            ex = sm[:, 352:448]
            ecs, cdb, dst = ex[:, 0:32], ex[:, 32:64], ex[:, 64:96]
            dtds = sm[:, 448:480]
            ssq = sm[:, 480:484]
            rsg = sm[:, 484:488]
            for c in range(NCH):
                cs_ = slice(c * 128, (c + 1) * 128)
                dtc = sm[:, 96 + c * 32:128 + c * 32]
                dac = sm[:, 224 + c * 32:256 + c * 32]
                mm(ps[5][:, 0:32], c32[:, TRI, :], dac, True, True, ["c32", ("da", c)], [("ps", 5)])
                mm(ps[5][:, 32:64], c32[:, ONES, :], dac, True, True, ["c32", ("da", c)], [("ps", 5)])
                mm(ps[5][:, 64:96], c32[:, UM, :], dac, True, True, ["c32", ("da", c)], [("ps", 5)])
                act(ex, ps[5][:, 0:96], AF.Exp, [("ps", 5)], ["ex"])
                tt("dve", dtds, dtc, dst, ALU.mult, [("dt", c), "ex"], ["dtds"])
                pb16 = ps[4][:].bitcast(BF16)
                for rnd in range(2):
                    for j in range(8):
                        blk = rnd * 8 + j
                        tp(pb16[:, j * 128:(j + 1) * 128], xsT[:, blk, cs_], cbf[:, IDENT, :], [kxs[blk], "cbf"], [("ps", 4)])
                    hs = slice(rnd * 16, (rnd + 1) * 16)
                    o1 = xdt[:, rnd * 1024:(rnd + 1) * 1024].rearrange("p (h d) -> p h d", d=64)
                    o2 = xdt2[:, rnd * 1024:(rnd + 1) * 1024].rearrange("p (h d) -> p h d", d=64)
                    tt("dve", o1, pb16.rearrange("p (h d) -> p h d", d=64), bc(dtc[:, hs].unsqueeze(2), [128, 16, 64]),
                       ALU.mult, [("ps", 4), ("dt", c)], [("xdt", rnd)])
                    tt("pool", o2, o1, bc(dst[:, hs].unsqueeze(2), [128, 16, 64]), ALU.mult, [("xdt", rnd), "ex"], [("xdt2", rnd)])
                pB = ps[5][:, 256:512].bitcast(BF16)
                for g in range(4):
                    tp(pB[:, g * 128:(g + 1) * 128], BCT[:, g, cs_], cbf[:, IDENT, :], [kA3[g], "cbf"], [("ps", 5)])
                cp("act", Btm, pB, [("ps", 5)], ["Btm"])
                for g in range(4):
                    mm(ps[2][:, g * 128:(g + 1) * 128], BCT[:, g, cs_], BCT[:, 4 + g, cs_], True, True,
                       [kA3[g], kA3[4 + g]], [("ps", 2)])
                tt("dve", CBm.rearrange("p (g l) -> p g l", l=128), ps[2][:].rearrange("p (g l) -> p g l", l=128),
                   bc(c32[:, TRI:TRI + 1, :], [128, 4, 128]), ALU.mult, [("ps", 2), "c32"], ["CBm"])
                def stageA(g):
                    for hh in range(2):
                        h0 = g * 8 + hh * 4
                        di = (g % 2) * 2 + hh
                        rr = W1[:, hh, 516:1028].bitcast(BF16).rearrange("p (a r l) -> p a r l", a=2, l=128)
                        for a_ in range(2):
                            tt(("pool" if a_ == 0 else "dve"), rr[:, a_], bc(cbf[:, TRI:TRI + 1, :], [128, 4, 128]),
                               bc(dahl[:, c, a_, h0:h0 + 4].unsqueeze(2), [128, 4, 128]), ALU.mult, ["cbf", ("dahl", c)], kW1(hh, (2,)))
                        for a_ in range(2):
                            mm(ps[hh][:], cbf[:, UM, :], rr[:, a_].rearrange("p r l -> p (r l)"), a_ == 0, a_ == 1,
                               kW1(hh, (2,)) + ["cbf"], [("ps", hh)])
                        d3 = dec[di].rearrange("p (r l) -> p r l", l=128)
                        act(dec[di], ps[hh][:], AF.Exp, [("ps", hh)], [("dec", di)])
                        tt("dve", d3, d3, bc(CBm[:, g * 128:(g + 1) * 128].unsqueeze(1), [128, 4, 128]), ALU.mult,
                           [("dec", di), "CBm"], [("dec", di)])

                stageA(0)
                for g in range(4):
                    if g + 1 < 4:
                        stageA(g + 1)
                    dg = (g % 2) * 2
                    bpy, bpo, bst = (3, 6, 7) if g % 2 == 0 else (2, 5, 4)
                    for i in range(4):
                        kc = g * 4 + i
                        mm(ps[bpy][:, i * 128:(i + 1) * 128], xsT[:, kc, cs_], Ddg[:, kc, :], i == 0, False,
                           [kxs[kc], ("Ddg",)], [("ps", bpy)])
                    for r in range(8):
                        h = g * 8 + r
                        mm(ps[bpy][:, r * 64:(r + 1) * 64], dec[dg + r // 4][:, (r % 4) * 128:(r % 4 + 1) * 128], xdt[:, h * 64:(h + 1) * 64],
                           False, r == 7, [("dec", dg + r // 4), ("xdt", h // 16)], [("ps", bpy)])
                    mm(ps[bpo][:], BCT[:, 4 + g, cs_], Sbf[:, g * 512:(g + 1) * 512], True, True, [kA3[4 + g]] + KSBF, [("ps", bpo)])
                    i2 = g % 2
                    yw = W1[:, i2, 1028:1540]
                    yg = W1[:, i2, 0:512]
                    junk = yw
                    tt("dve", yw.rearrange("p (h d) -> p h d", d=64), ps[bpo][:].rearrange("p (h d) -> p h d", d=64),
                       bc(ecs[:, g * 8:(g + 1) * 8].unsqueeze(2), [128, 8, 64]), ALU.mult, [("ps", bpo), "ex"], kW1(i2, (3,)))
                    tt("dve", yw, yw, ps[bpy][:], ALU.add, kW1(i2, (3,)) + [("ps", bpy)], kW1(i2, (3,)))
                    tt("pool", yg, yw, ztm[:, c, g * 512:(g + 1) * 512], ALU.mult, kW1(i2, (3,)) + kz, kW1(i2, (0,)))
                    A("act", lambda e, yg=yg, junk=junk, g=g: e.activation(out=junk, in_=yg, func=AF.Square, accum_out=ssq[:, g:g + 1]),
                      kW1(i2, (0,)), kW1(i2, (3,)) + [("ssq", g)])
                    ts("dve", rsg[:, g:g + 1], ssq[:, g:g + 1], 1.0 / 512.0, RMS_EPS, ALU.mult, ALU.add, [("ssq", g)], [("rsg", g)])
                    act(rsg[:, g:g + 1], rsg[:, g:g + 1], AF.Ln, [("rsg", g)], [("rsg", g)])
                    act(rsg[:, g:g + 1], rsg[:, g:g + 1], AF.Exp, [("rsg", g)], [("rsg", g)], scale=-0.5)
                    act(yn[:, g * 512:(g + 1) * 512], yg, AF.Identity, kW1(i2, (0,)) + [("rsg", g)], [("yn", g)], scale=rsg[:, g:g + 1])
                    mm(ps[bst][:], Btm[:, g * 128:(g + 1) * 128], xdt2[:, g * 512:(g + 1) * 512], True, True,
                       ["Btm", ("xdt2", g // 2)], [("ps", bst)])
                    Sg = Sst[:, li, g * 512:(g + 1) * 512]
                    tt("pool", Sg.rearrange("p (h d) -> p h d", d=64), Sg.rearrange("p (h d) -> p h d", d=64),
                       bc(cdb[:, g * 8:(g + 1) * 8].unsqueeze(2), [128, 8, 64]), ALU.mult, [("Sst", li), "ex"], [("Sst", li)])
                    tt("dve", Sg, Sg, ps[bst][:], ALU.add, [("Sst", li), ("ps", bst)], [("Sst", li)])
                    cp("act", Sbf[:, g * 512:(g + 1) * 512], Sg, [("Sst", li)], KSBF)
                for rnd in range(2):
                    for j in range(8):
                        blk = rnd * 8 + j
                        tp(pb16[:, j * 128:(j + 1) * 128], yn[:, blk * 128:(blk + 1) * 128], cbf[:, IDENT, :],
                           [("yn", blk // 4), "cbf"], [("ps", 4)])
                    og = O_NG + rnd * 8
                    tt("dve", yT[:, rnd * 8:(rnd + 1) * 8, cs_], pb16.rearrange("p (k l) -> p k l", l=128),
                       bc(pc[:, li, og:og + 8].unsqueeze(2), [128, 8, 128]), ALU.mult, [("ps", 4), "pc"], kA1[rnd * 8:(rnd + 1) * 8])
            dbg("yT", yT.rearrange("p a t -> p (a t)"), kA1, row0=dbgrow)
            if not BIG:
                sg_branch()
            for gi in range(2):
                proj_fm(li, "gb%d" % gi, xb, [KXB], 8, 512, 4,
                        lambda j, p_, pk: act(sgt[:, j, :], p_, AF.Sigmoid, [pk], [ksgt(j)]))
                for pi in range(2):
                    def ev2(jj, p_, pk, gi=gi, pi=pi):
                        blk = (gi * 2 + pi) * 2 + jj
                        j = pi * 2 + jj
                        tmp = W1[:, jj, 1028:1028 + T]
                        tt("dve", tmp, p_, sgt[:, j, :], ALU.mult, [pk, ksgt(j)], kW1(jj, (3,)))
                        tt("pool", mg[:, blk, :], tmp, mg[:, blk, :], ALU.add, kW1(jj, (3,)) + [kmg[blk]], [kmg[blk]])
                    proj_fm(li, "pb%d" % (gi * 2 + pi), yT, kA1, 16, 256, 2, ev2)
            for gi in range(2):
                proj_fm(li, "mo%d" % gi, mg, kmg, 8, 512, 4,
                        lambda j, p_, pk, gi=gi: stt(xt[:, gi * 4 + j, :], xt[:, gi * 4 + j, :], ALPHA, p_, ALU.mult, ALU.add,
                                                     [(KX[0], gi * 4 + j), pk], [(KX[0], gi * 4 + j)]))
            layer_norm(li, 0)
            dbg("x1", xt[:].rearrange("p a t -> p (a t)"), kxt(), row0=dbgrow)

        def xattn(li):
            qT = A1[:, 0:8, :]
            ET = A1[:, 8:16, :]
            oT = A3
            for gi in range(2):
                proj_fm(li, "xq%d" % gi, xb, [KXB], 8, 512, 4,
                        lambda j, p_, pk, gi=gi: act(qT[:, gi * 4 + j, :], p_, AF.Identity, [pk], [kA1[gi * 4 + j]], scale=0.0625))
            rden = lnst[:, 0, :]
            for hd in range(4):
                for mc in range(2):
                    b = nextps()
                    for dd in range(2):
                        mm(ps[b][:, 0:T], KT[:, li, 2 * hd + dd, mc * 128:(mc + 1) * 128], qT[:, 2 * hd + dd, :], dd == 0, dd == 1,
                           [("KT", li), kA1[2 * hd + dd]], [("ps", b)])
                    act(ET[:, hd * 2 + mc, :], ps[b][:, 0:T], AF.Exp, [("ps", b)], [kA1[8 + hd * 2 + mc]])
                for mc in range(2):
                    mm(ps[6][:, 0:T], cbf[:, ONES, :], ET[:, hd * 2 + mc, :], mc == 0, mc == 1, ["cbf", kA1[8 + hd * 2 + mc]], [("ps", 6)])
                act(rden, ps[6][:, 0:T], AF.Ln, [("ps", 6)], ["ln0"])
                act(rden, rden, AF.Exp, ["ln0"], ["ln0"], scale=-1.0)
                for dd in range(2):
                    b = nextps()
                    for mc in range(2):
                        mm(ps[b][:, 0:T], Vt[:, li, mc, (2 * hd + dd) * 128:(2 * hd + dd + 1) * 128], ET[:, hd * 2 + mc, :],
                           mc == 0, mc == 1, [("Vt", li), kA1[8 + hd * 2 + mc]], [("ps", b)])
                    tt("dve", oT[:, 2 * hd + dd, :], ps[b][:, 0:T], rden, ALU.mult, [("ps", b), "ln0"], [kA3[2 * hd + dd]])
            for gi in range(2):
                proj_fm(li, "xo%d" % gi, oT, kA3, 8, 512, 4,
                        lambda j, p_, pk, gi=gi: stt(xt[:, gi * 4 + j, :], xt[:, gi * 4 + j, :], ALPHA, p_, ALU.mult, ALU.add,
                                                     [(KX[0], gi * 4 + j), pk], [(KX[0], gi * 4 + j)]))
            layer_norm(li, 1)

        def ffn(li):
            hT = A2[:, 0:NKF, :]
            for gi in range(11):
                wt_, wk = wload(li, "fi%d" % gi)
                wv = w3(wt_, 512)
                for jj in range(2):
                    bg = nextps()
                    for kc in range(8):
                        mm(ps[bg][:, 0:T], wv[:, kc, jj * 128:(jj + 1) * 128], xb[:, kc, :], kc == 0, kc == 7, [wk, KXB], [("ps", bg)])
                    bu = nextps()
                    for kc in range(8):
                        mm(ps[bu][:, 0:T], wv[:, kc, (2 + jj) * 128:(3 + jj) * 128], xb[:, kc, :], kc == 0, kc == 7, [wk, KXB], [("ps", bu)])
                    tmp = lnst[:, jj, :]
                    act(tmp, ps[bg][:, 0:T], AF.Silu, [("ps", bg)], ["ln%d" % jj])
                    tt("dve", hT[:, gi * 2 + jj, :], ps[bu][:, 0:T], tmp, ALU.mult, [("ps", bu), "ln%d" % jj], [kA2[gi * 2 + jj]])
            for blk in range(8):
                wt_, wk = wload(li, "fo%d" % blk)
                wv = wt_[:, 0:NKF * 128].rearrange("p (k c) -> p k c", c=128)
                b = nextps()
                for kc in range(NKF):
                    mm(ps[b][:, 0:T], wv[:, kc, :], hT[:, kc, :], kc == 0, kc == NKF - 1, [wk, kA2[kc]], [("ps", b)])
                stt(xt[:, blk, :], xt[:, blk, :], ALPHA, ps[b][:, 0:T], ALU.mult, ALU.add, [(KX[0], blk), ("ps", b)], [(KX[0], blk)])
            layer_norm(li, 2)

        for s in range(NSEQ):
            if "xattn" in cfg.stages:
                mem_phase(s)
            for li in range(NL):
                A("pool", lambda e, li=li: e.memset(Sst[:, li, :], 0.0), [], [("Sst", li)])
                A("pool", lambda e, li=li: e.memset(halo[:, li], 0.0), [("halo", li, b_) for b_ in range(24)], [("halo", li, b_) for b_ in range(24)])
            def xload(ti_):
                row_ = (s * NT + ti_) * 128
                buf = xts[ti_ % nxb]
                A("pool", lambda e: e.dma_start(out=buf[:].rearrange("p a t -> p (a t)"), in_=xin[row_:row_ + 128, :]),
                  r=[], w=[("xt%d" % (ti_ % nxb), k_) for k_ in range(8)], dma="x%d" % (ti_ % nxb), cost=6000.0)
            if nxb == 2:
                xload(0)
            for ti in range(NT):
                row = (s * NT + ti) * 128
                xt = xts[ti % nxb]
                KX[0] = "xt%d" % (ti % nxb)
                if nxb == 1:
                    xload(ti)
                if nxb == 2 and ti + 1 < NT:
                    xload(ti + 1)
                cp("act", xb[:], xt[:], kxt(), [KXB])
                for li in range(NL):
                    if "mixer" in cfg.stages:
                        mixer(li, (s * NT + ti) * 128)
                    if "xattn" in cfg.stages:
                        xattn(li)
                    if "ffn" in cfg.stages:
                        ffn(li)
                A("pool", lambda e, row=row, xt=xt: e.dma_start(out=yout[row:row + 128, :], in_=xt[:].rearrange("p a t -> p (a t)")),
                  r=kxt(), w=[("y", row)], dma="y%d" % (ti % nxb), cost=6000.0)
        allout = [("y", (s * NT + ti) * 128) for s in range(NSEQ) for ti in range(NT)]
        allout += [k for k in P.last_writer if isinstance(k, tuple) and k[0] == "dbg"]
        A("sp", lambda e: e.nop(), r=allout, w=[])

        if cfg.sched:
            P.schedule()
            if cfg.dbg is not None and "verbose" in cfg.stages:
                print("sched est_ns", P.est_ns, {e: (len(P.ops[e]), round(sum(o.cost for o in P.ops[e] if not o.is_dma))) for e in ENGS})
        P.finalize()
        with nc.Block() as block:
            @block.sync
            def _(e):
                P.emit("sp", e, sems, chans)

            @block.scalar
            def _(e):
                P.emit("act", e, sems, chans)

            @block.vector
            def _(e):
                P.emit("dve", e, sems, chans)

            @block.gpsimd
            def _(e):
                P.emit("pool", e, sems, chans)

            @block.tensor
            def _(e):
                P.emit("pe", e, sems, chans)
    return nc


def _kgroup(Wm, c0, ncol):
    K = Wm.shape[0]
    blk = Wm[:, c0:c0 + ncol].reshape(K // 128, 128, ncol).transpose(1, 0, 2).reshape(128, -1)
    out = np.zeros((128, GSZ), np.float32)
    out[:, :blk.shape[1]] = blk
    return out


def layout_weights(inp, layers):
    allg = []
    for l in layers:
        w_in = inp["w_in"][l]
        g = {}
        for i in range(4):
            g["kv%d" % i] = _kgroup(inp["w_xkv"][l], i * 512, 512)
        for i in range(2):
            g["u%d" % i] = _kgroup(w_in, i * 512, 512)
            g["v%d" % i] = _kgroup(w_in, 1024 + i * 512, 512)
            g["ga%d" % i] = _kgroup(w_in, 7200 + i * 512, 512)
            g["gb%d" % i] = _kgroup(w_in, 8224 + i * 512, 512)
            g["pa%d" % i] = _kgroup(inp["p_a"][l], i * 512, 512)
            g["mo%d" % i] = _kgroup(inp["w_mix_o"][l], i * 512, 512)
            g["xq%d" % i] = _kgroup(inp["w_xq"][l], i * 512, 512)
            g["xo%d" % i] = _kgroup(inp["w_xo"][l], i * 512, 512)
        for i in range(4):
            g["z%d" % i] = _kgroup(w_in, 2048 + i * 512, 512)
            g["pb%d" % i] = _kgroup(inp["p_b"][l], i * 256, 256)
        for i in range(6):
            g["xbc%d" % i] = _kgroup(w_in, 4096 + i * 512, 512)
        wfi = inp["w_ffn_in"][l]
        for i in range(11):
            cols = np.concatenate([np.arange(i * 256, i * 256 + 256), FFH + np.arange(i * 256, i * 256 + 256)])
            g["fi%d" % i] = _kgroup(wfi[:, cols], 0, 512)
        for i in range(8):
            g["fo%d" % i] = _kgroup(inp["w_ffn_out"][l], i * 128, 128)
        allg += [g[n] for n in GROUPS]
    return np.ascontiguousarray(np.stack(allg).reshape(-1, GSZ))


def layout_small(inp, layers):
    NL = len(layers)
    ident = np.eye(128, dtype=np.float32)
    k = np.arange(128)
    tri = (k[:, None] <= k[None, :]).astype(np.float32)
    um = (k[:, None] > k[None, :]).astype(np.float32)
    ones = np.ones((128, 128), np.float32)
    cst = np.concatenate([ident, tri, um, ones, ones / 1024.0], axis=1)
    pcs, prs, wdts, sgws = [], [], [], []
    for l in layers:
        cols = []
        cols.append(inp["ln_g"][l].reshape(3, 8, 128).transpose(2, 0, 1).reshape(128, 24))
        cols.append(inp["ln_b"][l].reshape(3, 8, 128).transpose(2, 0, 1).reshape(128, 24))
        cols.append(inp["sg_ln_g"][l].reshape(8, 128).T)
        cols.append(inp["sg_ln_b"][l].reshape(8, 128).T)
        cols.append(inp["conv_w"][l].reshape(4, 24, 128).transpose(2, 1, 0).reshape(128, 96))
        cols.append(inp["conv_b"][l].reshape(24, 128).T)
        cols.append(np.repeat(inp["d_skip"][l], 64).reshape(16, 128).T)
        cols.append(inp["ssm_norm_g"][l].reshape(16, 128).T)
        pcs.append(np.concatenate(cols, axis=1))
        prs.append(np.concatenate([inp["dt_bias"][l], inp["a_log"][l], inp["sg_b"][l].reshape(-1)]))
        wdts.append(inp["w_in"][l][:, 7168:7200].reshape(8, 128, 32).transpose(1, 0, 2).reshape(128, 256))
        sgws.append(inp["sg_w"][l].transpose(2, 0, 1).reshape(128, 1024))
    pcol = np.concatenate(pcs, axis=1).astype(np.float32)
    prow = np.concatenate(prs + [inp["mem_ln_g"], inp["mem_ln_b"]])[None, :].astype(np.float32)
    return {"cst": np.ascontiguousarray(cst), "pcol": np.ascontiguousarray(pcol), "prow": np.ascontiguousarray(prow),
            "wdt": np.ascontiguousarray(np.concatenate(wdts, axis=1)), "sgw": np.ascontiguousarray(np.concatenate(sgws, axis=1))}


def layout_x(xs, nt, T):
    nseq = xs.shape[0]
    a = xs.reshape(nseq, nt, T, 8, 128).transpose(0, 1, 4, 3, 2)
    return np.ascontiguousarray(a.reshape(nseq * nt * 128, 8 * T))


def unlayout_x(y, nseq, nt, T):
    a = y.reshape(nseq, nt, 128, 8, T).transpose(0, 1, 4, 3, 2)
    return np.ascontiguousarray(a.reshape(nseq, nt * T, D))


def kernel(**inputs):
    inp = {k: np.asarray(v) for k, v in inputs.items()}
    cfg = Cfg(nseq=2, nt=SEQ // 512, nch=4, layers=(0, 1))
    nc = build(cfg)
    wall = layout_weights(inp, cfg.layers)
    small = layout_small(inp, cfg.layers)
    in_maps = []
    for c in range(NCORES):
        m = {"xT": layout_x(inp["x"][2 * c:2 * c + 2], cfg.nt, cfg.T),
             "mem": np.ascontiguousarray(inp["mem"][2 * c:2 * c + 2].reshape(-1, D)), "wall": wall}
        m.update(small)
        in_maps.append(m)
    res = run_bass_kernel_spmd(nc, in_maps, core_ids=list(range(NCORES)))
    out = np.concatenate([unlayout_x(r["yT"], 2, cfg.nt, cfg.T) for r in res.results], axis=0)
    return out.astype(np.float32)
```
